# Optimizing a Trainium2 kernel written in Bass

```python
import jax, jax.numpy as jnp
from jax import lax
import numpy as np

D_MODEL = 2048
BATCH = 8
SEQ = 4096
DEPTH = 1

D_MIX = D_MODEL
RET_WIDTH = D_MIX // 2
GDN_WIDTH = D_MIX - RET_WIDTH
RET_HEADS = 4
RET_V_DIM = RET_WIDTH // RET_HEADS
RET_QK_DIM = RET_V_DIM // 2
RET_QK_WIDTH = RET_HEADS * RET_QK_DIM
RET_CHUNK = 128
ROPE_BASE = 10000.0
GDN_HEAD_DIM = 128
GDN_HEADS = GDN_WIDTH // GDN_HEAD_DIM
GDN_CHUNK = 64
CONV_WIDTH = 4
RMS_EPS = 1e-6
GN_EPS = 1e-5
LN_EPS = 1e-5
DEEPNORM_ALPHA = (2.0 * DEPTH) ** 0.25
DEEPNORM_BETA = (8.0 * DEPTH) ** -0.25
IN_COLS = 2 * RET_QK_WIDTH + 2 * RET_WIDTH + 4 * GDN_WIDTH + 2 * GDN_HEADS

kernel_name = "hybrid_retention_gated_deltanet_deepnorm_adaln"

f32 = jnp.float32


def _split_sizes():
    return [RET_QK_WIDTH, RET_QK_WIDTH, RET_WIDTH, RET_WIDTH,
            3 * GDN_WIDTH, GDN_WIDTH, GDN_HEADS, GDN_HEADS]


def _split_points():
    pts, acc = [], 0
    for s in _split_sizes()[:-1]:
        acc += s
        pts.append(acc)
    return pts


def _to_chunks(t, chunk):
    b, t_len, h, d = t.shape
    return t.reshape(b, t_len // chunk, chunk, h, d).transpose(1, 0, 3, 2, 4)


def _from_chunks(t):
    n, b, h, c, d = t.shape
    return t.transpose(1, 0, 3, 2, 4).reshape(b, n * c, h, d)


def _rotary(t):
    t_len, d = t.shape[1], t.shape[-1]
    inv_freq = 1.0 / (ROPE_BASE ** (jnp.arange(0, d, 2, dtype=f32) / d))
    ang = jnp.arange(t_len, dtype=f32)[:, None] * inv_freq[None, :]
    cos = jnp.cos(ang)[None, :, None, :]
    sin = jnp.sin(ang)[None, :, None, :]
    t1, t2 = t[..., : d // 2], t[..., d // 2:]
    return jnp.concatenate([t1 * cos - t2 * sin, t1 * sin + t2 * cos], axis=-1)


def _retention(q, k, v):
    b, t_len, h, dk = q.shape
    dv = v.shape[-1]
    c = RET_CHUNK
    log_gamma = jnp.log(1.0 - 2.0 ** (-5.0 - jnp.arange(h, dtype=f32)))
    qc = _to_chunks(q, c)
    kc = _to_chunks(k * (dk ** -0.5), c)
    vc = _to_chunks(v, c)
    idx = jnp.arange(c, dtype=f32)
    diff = idx[:, None] - idx[None, :]
    decay = jnp.where(diff[None] >= 0,
                      jnp.exp(log_gamma[:, None, None] * jnp.maximum(diff, 0.0)[None]), 0.0)
    scores = jnp.einsum('nbhid,nbhjd->nbhij', qc, kc) * decay[None, None]
    intra = jnp.einsum('nbhij,nbhje->nbhie', scores, vc)
    q_dec = jnp.exp(log_gamma[:, None] * (idx[None, :] + 1.0))
    k_dec = jnp.exp(log_gamma[:, None] * (c - 1.0 - idx[None, :]))
    chunk_dec = jnp.exp(log_gamma * c)

    def step(state, inp):
        q_i, k_i, v_i = inp
        o = jnp.einsum('bhid,bhde->bhie', q_i, state) * q_dec[None, :, :, None]
        state = state * chunk_dec[None, :, None, None] + jnp.einsum(
            'bhjd,bhje->bhde', k_i * k_dec[None, :, :, None], v_i)
        return state, o

    s0 = jnp.zeros((b, h, dk, dv), f32)
    _, inter = lax.scan(step, s0, (qc, kc, vc))
    return _from_chunks(intra + inter)


def _gated_delta(q, k, v, g, beta):
    b, t_len, h, dk = q.shape
    dv = v.shape[-1]
    c = GDN_CHUNK
    n = t_len // c
    qc = _to_chunks(q * (dk ** -0.5), c)
    kc = _to_chunks(k, c)
    vc = _to_chunks(v, c)
    gc = g.reshape(b, n, c, h).transpose(1, 0, 3, 2)
    bc = beta.reshape(b, n, c, h).transpose(1, 0, 3, 2)
    gcum = jnp.cumsum(gc, axis=-1)
    idx = jnp.arange(c)
    tril = idx[:, None] >= idx[None, :]
    strict = idx[:, None] > idx[None, :]
    decay = jnp.exp(jnp.where(tril, gcum[..., :, None] - gcum[..., None, :], -jnp.inf))
    k_beta = kc * bc[..., None]
    v_beta = vc * bc[..., None]
    a_mat = jnp.where(strict, jnp.einsum('nbhid,nbhjd->nbhij', k_beta, kc) * decay, 0.0)
    eye = jnp.eye(c, dtype=f32)
    t_mat = lax.linalg.triangular_solve(eye + a_mat, jnp.broadcast_to(eye, a_mat.shape),
                                        left_side=True, lower=True, unit_diagonal=True)
    v_t = jnp.einsum('nbhij,nbhje->nbhie', t_mat, v_beta)
    k_cumdecay = jnp.einsum('nbhij,nbhjd->nbhid', t_mat, k_beta * jnp.exp(gcum)[..., None])
    attn_intra = jnp.where(tril, jnp.einsum('nbhid,nbhjd->nbhij', qc, kc) * decay, 0.0)
    q_exp = qc * jnp.exp(gcum)[..., None]
    g_last = gcum[..., -1]
    k_tail = kc * jnp.exp(g_last[..., None] - gcum)[..., None]

    def step(state, inp):
        q_e, kcd, vt, attn, kt, gl = inp
        v_new = vt - jnp.einsum('bhcd,bhde->bhce', kcd, state)
        o = jnp.einsum('bhcd,bhde->bhce', q_e, state) + jnp.einsum('bhij,bhje->bhie', attn, v_new)
        state = state * jnp.exp(gl)[..., None, None] + jnp.einsum('bhcd,bhce->bhde', kt, v_new)
        return state, o

    s0 = jnp.zeros((b, h, dk, dv), f32)
    _, out = lax.scan(step, s0, (q_exp, k_cumdecay, v_t, attn_intra, k_tail, g_last))
    return _from_chunks(out)


def _causal_conv(t, w):
    t_len = t.shape[1]
    tp = jnp.pad(t, ((0, 0), (CONV_WIDTH - 1, 0), (0, 0)))
    out = tp[:, 0:t_len] * w[0]
    for j in range(1, CONV_WIDTH):
        out = out + tp[:, j:j + t_len] * w[j]
    return out


def _l2norm(t):
    return t * lax.rsqrt(jnp.sum(t * t, axis=-1, keepdims=True) + RMS_EPS)


def _layernorm(t, w, b):
    t = t.astype(f32)
    mu = jnp.mean(t, axis=-1, keepdims=True)
    var = jnp.mean(jnp.square(t - mu), axis=-1, keepdims=True)
    return (t - mu) * lax.rsqrt(var + LN_EPS) * w + b


def setup_inputs(seed: int = 0) -> dict:
    key = jax.random.key(seed)
    ks = jax.random.split(key, 14)
    x = jax.random.normal(ks[0], (BATCH, SEQ, D_MODEL), f32)
    c = jax.random.normal(ks[1], (BATCH, D_MODEL), f32)
    w_ada = jax.random.normal(ks[2], (DEPTH, D_MODEL, 3 * D_MODEL), f32) * D_MODEL ** -0.5
    b_ada = 0.02 * jax.random.normal(ks[3], (DEPTH, 3 * D_MODEL), f32)
    col_scale = np.concatenate([
        np.ones(2 * RET_QK_WIDTH, np.float32),
        np.full(RET_WIDTH, DEEPNORM_BETA, np.float32),
        np.ones(RET_WIDTH, np.float32),
        np.ones(2 * GDN_WIDTH, np.float32),
        np.full(GDN_WIDTH, DEEPNORM_BETA, np.float32),
        np.ones(GDN_WIDTH + 2 * GDN_HEADS, np.float32)])
    w_in = jax.random.normal(ks[4], (DEPTH, D_MODEL, IN_COLS), f32) * (D_MODEL ** -0.5) * jnp.asarray(col_scale)
    gdn_conv_w = jax.random.normal(ks[5], (DEPTH, CONV_WIDTH, 3 * GDN_WIDTH), f32) * CONV_WIDTH ** -0.5
    gdn_a_log = jnp.log(jax.random.uniform(ks[6], (DEPTH, GDN_HEADS), f32, 1.0, 16.0))
    dt = jnp.exp(jax.random.uniform(ks[7], (DEPTH, GDN_HEADS), f32, float(np.log(1e-3)), float(np.log(1e-1))))
    gdn_dt_bias = dt + jnp.log(-jnp.expm1(-dt))
    ret_gn_w = 1.0 + 0.02 * jax.random.normal(ks[8], (DEPTH, RET_WIDTH), f32)
    ret_gn_b = 0.02 * jax.random.normal(ks[9], (DEPTH, RET_WIDTH), f32)
    gdn_norm_w = 1.0 + 0.02 * jax.random.normal(ks[10], (DEPTH, GDN_HEAD_DIM), f32)
    w_out = jax.random.normal(ks[11], (DEPTH, D_MIX, D_MODEL), f32) * (D_MIX ** -0.5) * DEEPNORM_BETA
    ln_w = 1.0 + 0.02 * jax.random.normal(ks[12], (DEPTH, D_MODEL), f32)
    ln_b = 0.02 * jax.random.normal(ks[13], (DEPTH, D_MODEL), f32)
    return {"x": x, "c": c, "w_ada": w_ada, "b_ada": b_ada, "w_in": w_in,
            "gdn_conv_w": gdn_conv_w, "gdn_a_log": gdn_a_log, "gdn_dt_bias": gdn_dt_bias,
            "ret_gn_w": ret_gn_w, "ret_gn_b": ret_gn_b, "gdn_norm_w": gdn_norm_w,
            "w_out": w_out, "ln_w": ln_w, "ln_b": ln_b}


def reference(x, c, w_ada, b_ada, w_in, gdn_conv_w, gdn_a_log, gdn_dt_bias,
              ret_gn_w, ret_gn_b, gdn_norm_w, w_out, ln_w, ln_b):
    out_dtype = x.dtype
    b, t_len, _ = x.shape
    h_stream = x.astype(f32)
    for l in range(DEPTH):
        mod = (jax.nn.silu(c) @ w_ada[l] + b_ada[l]).astype(f32)
        shift, scale, gate = jnp.split(mod, 3, axis=-1)
        hin = h_stream * (1.0 + scale[:, None, :]) + shift[:, None, :]
        proj = (hin.astype(w_in.dtype) @ w_in[l]).astype(f32)
        r_q, r_k, r_v, r_g, g_qkv, g_g, g_a, g_b = jnp.split(proj, _split_points(), axis=-1)

        rq = _rotary(r_q.reshape(b, t_len, RET_HEADS, RET_QK_DIM))
        rk = _rotary(r_k.reshape(b, t_len, RET_HEADS, RET_QK_DIM))
        rv = r_v.reshape(b, t_len, RET_HEADS, RET_V_DIM)
        ry = _retention(rq, rk, rv)
        mu = jnp.mean(ry, axis=-1, keepdims=True)
        var = jnp.mean(jnp.square(ry - mu), axis=-1, keepdims=True)
        ry = ((ry - mu) * lax.rsqrt(var + GN_EPS)).reshape(b, t_len, RET_WIDTH)
        ret_out = (ry * ret_gn_w[l] + ret_gn_b[l]) * jax.nn.silu(r_g)

        qkv = jax.nn.silu(_causal_conv(g_qkv, gdn_conv_w[l].astype(f32)))
        gq, gk, gv = jnp.split(qkv, 3, axis=-1)
        gq = _l2norm(gq.reshape(b, t_len, GDN_HEADS, GDN_HEAD_DIM))
        gk = _l2norm(gk.reshape(b, t_len, GDN_HEADS, GDN_HEAD_DIM))
        gv = gv.reshape(b, t_len, GDN_HEADS, GDN_HEAD_DIM)
        g_decay = -jnp.exp(gdn_a_log[l].astype(f32)) * jax.nn.softplus(g_a + gdn_dt_bias[l].astype(f32))
        beta = jax.nn.sigmoid(g_b)
        gy = _gated_delta(gq, gk, gv, g_decay, beta)
        gy = gy * lax.rsqrt(jnp.mean(gy * gy, axis=-1, keepdims=True) + RMS_EPS) * gdn_norm_w[l]
        gdn_out = gy.reshape(b, t_len, GDN_WIDTH) * jax.nn.silu(g_g)

        mixed = jnp.concatenate([ret_out, gdn_out], axis=-1)
        y = (mixed.astype(w_out.dtype) @ w_out[l]).astype(f32)
        h_stream = _layernorm(DEEPNORM_ALPHA * h_stream + gate[:, None, :] * y, ln_w[l], ln_b[l])
    return h_stream.astype(out_dtype)
```

```python
import os
import numpy as np
import ml_dtypes
import concourse.bass as bass
import concourse.mybir as mybir
from concourse.bass_utils import run_bass_kernel_spmd

F32 = mybir.dt.float32
BF16 = mybir.dt.bfloat16
AF = mybir.ActivationFunctionType
ALU = mybir.AluOpType
AX = mybir.AxisListType


class Buf:
    def __init__(self, name, t, psum=False):
        self.name = name
        self.t = t
        self.psum = psum
        self.reads = {}
        self.writes = {}

    def v(self, lo, hi, p0=0, p1=None):
        if p1 is None:
            p1 = self.t.shape[0]
        if self.psum:
            return V(self.t[p0:p1, lo:hi], self, 0, 512)
        return V(self.t[p0:p1, lo:hi], self, lo, hi)

    def whole(self):
        return self.v(0, self.t.shape[1])


class V:
    def __init__(self, ap, buf, lo, hi):
        self.ap = ap
        self.buf = buf
        self.lo = lo
        self.hi = hi

    def r(self, pattern, **kw):
        return V(self.ap.rearrange(pattern, **kw), self.buf, self.lo, self.hi)

    def __getitem__(self, idx):
        return V(self.ap[idx], self.buf, self.lo, self.hi)

    def with_ap(self, ap):
        return V(ap, self.buf, self.lo, self.hi)


class Op:
    __slots__ = ("q", "te", "seq", "deps", "fn", "waits", "signal", "ndma")


class Prog:
    QUEUES = ("pe", "act", "dve", "pool", "sp")

    def __init__(self, nc, n_lanes=12):
        self.nc = nc
        self.qops = {q: [] for q in self.QUEUES}
        self.teops = {q: [] for q in self.QUEUES}
        self.lanes = ["L%d" % i for i in range(n_lanes)]
        for l in self.lanes:
            self.teops[l] = []
        self.next_lane = 0

    def _track(self, op, outs, ins):
        deps = {}

        def need(te, seq):
            if te == op.te:
                return
            if deps.get(te, -1) < seq:
                deps[te] = seq

        raw_same = -1
        for v in ins:
            for (te, lo, hi), seq in v.buf.writes.items():
                if lo < v.hi and v.lo < hi:
                    if te == op.te:
                        raw_same = max(raw_same, seq)
                    else:
                        need(te, seq)
            if v.buf.psum:
                for (te, lo, hi), seq in v.buf.reads.items():
                    need(te, seq)
        for v in outs:
            for d in (v.buf.writes, v.buf.reads):
                for (te, lo, hi), seq in d.items():
                    if lo < v.hi and v.lo < hi:
                        if te == op.te:
                            raw_same = max(raw_same, seq)
                        else:
                            need(te, seq)
        if raw_same >= 0 and op.te in ("act", "dve", "pool"):
            deps[op.te] = raw_same
        op.deps = deps
        for v in outs:
            for d in (v.buf.writes, v.buf.reads):
                for k in [k for k in d if v.lo <= k[1] and k[2] <= v.hi]:
                    del d[k]
            v.buf.writes[(op.te, v.lo, v.hi)] = op.seq
        for v in ins:
            v.buf.reads[(op.te, v.lo, v.hi)] = op.seq

    def op(self, q, fn, outs=(), ins=()):
        o = Op()
        o.q = q
        o.te = q
        o.seq = len(self.teops[q])
        o.fn = fn
        o.ndma = 0
        self._track(o, outs, ins)
        self.teops[q].append(o)
        self.qops[q].append(o)
        return o

    def dma(self, fns, outs=(), ins=(), q="sp"):
        lane = self.lanes[self.next_lane]
        self.next_lane = (self.next_lane + 1) % len(self.lanes)
        o = Op()
        o.q = q
        o.te = lane
        o.seq = len(self.teops[lane])
        o.fn = fns
        o.ndma = len(fns)
        self._track(o, outs, ins)
        if o.seq > 0:
            o.deps[lane] = o.seq - 1
        self.teops[lane].append(o)
        self.qops[q].append(o)
        return o

    def final_wait(self):
        o = Op()
        o.q = "sp"
        o.te = "sp"
        o.seq = len(self.teops["sp"])
        o.fn = None
        o.ndma = 0
        o.deps = {te: len(ops) - 1 for te, ops in self.teops.items() if te.startswith("L") and ops}
        self.teops["sp"].append(o)
        self.qops["sp"].append(o)

    def finalize(self):
        for q in self.QUEUES:
            seen = {}
            for o in self.qops[q]:
                w = []
                for te, seq in o.deps.items():
                    if seen.get(te, -1) < seq:
                        seen[te] = seq
                        w.append((te, seq))
                o.waits = w
        for te, ops in self.teops.items():
            for o in ops:
                o.signal = te.startswith("L")
        for q in self.QUEUES:
            for o in self.qops[q]:
                for te, seq in o.waits:
                    self.teops[te][seq].signal = True
        self.val = {}
        for te, ops in self.teops.items():
            c = 0
            vals = []
            for o in ops:
                if te.startswith("L"):
                    c += 16 * o.ndma
                elif o.signal:
                    c += 1
                vals.append(c)
            self.val[te] = vals

    def emit(self, sems):
        nc = self.nc
        with nc.Block() as block:

            def mk(q):
                def body(eng):
                    for o in self.qops[q]:
                        for te, seq in o.waits:
                            eng.wait_ge(sems[te], self.val[te][seq])
                        if o.ndma:
                            for f in o.fn:
                                f(eng).then_inc(sems[o.te], 16)
                        elif o.fn is not None:
                            ins = o.fn(eng)
                            if o.signal:
                                ins.then_inc(sems[o.te], 1)

                return body

            block.tensor(mk("pe"))
            block.scalar(mk("act"))
            block.vector(mk("dve"))
            block.gpsimd(mk("pool"))
            block.sync(mk("sp"))

    def barrier(self):
        last = {}
        for te, ops in self.teops.items():
            i = len(ops) - 1
            while i >= 0 and ops[i].fn is None:
                i -= 1
            if i >= 0:
                last[te] = i
        for q in self.QUEUES:
            o = Op()
            o.q = q
            o.te = q
            o.seq = len(self.teops[q])
            o.fn = None
            o.ndma = 0
            o.deps = {te: s for te, s in last.items() if te != q}
            self.teops[q].append(o)
            self.qops[q].append(o)


def _ap(x):
    return x.ap if isinstance(x, V) else x


def _vs(*xs):
    return [x for x in xs if isinstance(x, V)]


def bcast_mid(v, n, inner):
    a = v.ap
    return v.with_ap(bass.AP(tensor=a.tensor, offset=a.offset, ap=[list(a.ap[0]), [0, n], [1, inner]]))


def bcast_last(v, n, inner):
    a = v.ap
    return v.with_ap(bass.AP(tensor=a.tensor, offset=a.offset, ap=[list(a.ap[0]), [a.ap[1][0], n], [0, inner]]))


def mm(P, out, lhsT, rhs, start=True, stop=True):
    return P.op("pe", lambda e: e.matmul(out=out.ap, lhsT=lhsT.ap, rhs=rhs.ap, start=start, stop=stop),
                outs=[out], ins=[lhsT, rhs])


def tr(P, out, in_, ident):
    return P.op("pe", lambda e: e.transpose(out=out.ap, in_=in_.ap, identity=ident.ap), outs=[out], ins=[in_, ident])


def act(P, out, in_, func, scale=1.0, bias=0.0):
    return P.op("act", lambda e: e.activation(out=out.ap, in_=in_.ap, func=func, scale=_ap(scale), bias=_ap(bias)),
                outs=[out], ins=[in_] + _vs(scale, bias))


def tt(P, q, out, a, b, op):
    return P.op(q, lambda e: e.tensor_tensor(out=out.ap, in0=a.ap, in1=b.ap, op=op), outs=[out], ins=[a, b])


def ts(P, q, out, a, s1, op0, s2=None, op1=None):
    if op1 is None:
        return P.op(q, lambda e: e.tensor_scalar(out=out.ap, in0=a.ap, scalar1=_ap(s1), scalar2=None, op0=op0),
                    outs=[out], ins=[a] + _vs(s1))
    return P.op(q, lambda e: e.tensor_scalar(out=out.ap, in0=a.ap, scalar1=_ap(s1), scalar2=_ap(s2), op0=op0, op1=op1),
                outs=[out], ins=[a] + _vs(s1, s2))


def stt(P, out, a, s, b, op0, op1):
    return P.op("dve", lambda e: e.scalar_tensor_tensor(out=out.ap, in0=a.ap, scalar=_ap(s), in1=b.ap, op0=op0, op1=op1),
                outs=[out], ins=[a, b] + _vs(s))


def cp(P, q, out, in_):
    if q == "act":
        return P.op("act", lambda e: e.copy(out=out.ap, in_=in_.ap), outs=[out], ins=[in_])
    return P.op(q, lambda e: e.tensor_copy(out=out.ap, in_=in_.ap), outs=[out], ins=[in_])


def dma(P, out, in_, q="sp", outs=None, ins=None, **kw):
    o = out.ap if isinstance(out, V) else out
    i = in_.ap if isinstance(in_, V) else in_
    return P.dma([lambda e: e.dma_start(out=o, in_=i, **kw)], outs=_vs(out) if outs is None else outs,
                 ins=_vs(in_) if ins is None else ins, q=q)


T = 4096
D = 2048
NTILE = T // 128
RQW = 512
RW = 1024
GW = 1024
IN_COLS = 7184
RET_COLS = 3072
GDN_OFF = 3072
GDN_COLS = 4112
GN_EPS = 1e-5
RMS_EPS = 1e-6
LN_EPS = 1e-5
ALPHA = 2.0 ** 0.25
GAMMAS = [1.0 - 2.0 ** (-5.0 - h) for h in range(4)]

C_ID, C_TRIU, C_ONES, C_GQ, C_GK, C_RMASK, C_NSTRICT, C_D8, C_M8, C_M16, C_M32, C_M64, C_END = 0, 128, 256, 384, 896, 1408, 1920, 2048, 2176, 2304, 2432, 2560, 2688
B_ID, B_ONES, B_TRIU, B_NEG, B_NTRIU, B_END = 0, 128, 256, 384, 512, 640


def host_consts():
    p = np.arange(128)
    cf = np.zeros((128, C_END), np.float32)
    cf[:, C_ID:C_ID + 128] = np.eye(128)
    cf[:, C_TRIU:C_TRIU + 128] = (p[:, None] <= p[None, :])
    cf[:, C_ONES:C_ONES + 128] = 1.0
    for h, g in enumerate(GAMMAS):
        lg = np.log(np.float64(g))
        cf[:, C_GQ + h * 128:C_GQ + (h + 1) * 128] = np.exp(lg * (p + 1.0))[:, None]
        cf[:, C_GK + h * 128:C_GK + (h + 1) * 128] = (np.exp(lg * (127.0 - p)) * 128.0 ** -0.5)[:, None]
        cf[:, C_RMASK + h * 128:C_RMASK + (h + 1) * 128] = np.where(p[None, :] >= p[:, None], np.exp(-lg * 128.0), 0.0)
    cf[:, C_NSTRICT:C_NSTRICT + 128] = np.where(p[None, :] > p[:, None], -1.0, 0.0)
    cf[:, C_D8:C_D8 + 128] = (p[:, None] // 8 == p[None, :] // 8)
    for col, b in ((C_M8, 8), (C_M16, 16), (C_M32, 32), (C_M64, 64)):
        cf[:, col:col + 128] = (p[:, None] // (2 * b) == p[None, :] // (2 * b)) & (p[:, None] // b != p[None, :] // b)
    cb = np.zeros((128, B_END), np.float32)
    cb[:, B_ID:B_ID + 128] = np.eye(128)
    cb[:, B_ONES:B_ONES + 128] = 1.0
    cb[:, B_TRIU:B_TRIU + 128] = (p[:, None] <= p[None, :])
    cb[:, B_NEG:B_NEG + 128] = np.where(p[None, :] < p[:, None], -30000.0, 0.0)
    cb[:, B_NTRIU:B_NTRIU + 128] = -1.0 * (p[:, None] <= p[None, :])
    cb = cb.astype(ml_dtypes.bfloat16)
    inv_freq = (1.0 / (np.float32(10000.0) ** (np.arange(0, 128, 2, dtype=np.float32) / np.float32(128)))).astype(np.float32)
    ang = (np.arange(T, dtype=np.float32)[:, None] * inv_freq[None, :]).astype(np.float32)
    cos = np.cos(ang).astype(np.float32)
    sin = np.sin(ang).astype(np.float32)
    rope = np.concatenate([cos, cos, -sin, sin], axis=1).astype(np.float32)
    return cf, cb, np.ascontiguousarray(rope)


class Arena:
    def __init__(self, nc, words, ap=None):
        self.t = nc.alloc_sbuf_tensor("arena", [128, words], F32) if ap is None else ap
        self.words = words
        self.off = 0

    def alloc(self, name, n, dt):
        w = n if dt == F32 else (n + 1) // 2
        w = (w + 7) // 8 * 8
        assert self.off + w <= self.words, ("SBUF arena overflow", name, self.off, w)
        ap = self.t[:, self.off:self.off + w]
        self.off += w
        if dt != F32:
            ap = ap.bitcast(dt)
        return Buf(name, ap[:, 0:n])


def vb(buf, lo, hi):
    ap = buf.t[:, :].bitcast(BF16)[:, lo:hi]
    return V(ap, buf, 0, 512)


def r3(v, h):
    return v.r("p (h e) -> p h e", h=h)


def build(nt=NTILE, stop_after=99, debug=False):
    nc = bass.Bass("TRN2", target_bir_lowering=False)

    def din(name, shape, dtype=F32):
        return nc.dram_tensor(name, shape, dtype, kind="ExternalInput").ap()

    x_d = din("x", [T, D])
    ccol_d = din("ccol", [128, 16])
    wada_d = din("wada", [D, 3 * D])
    badac_d = din("badac", [128, 32])
    badag_d = din("badag", [1, D])
    win_d = din("win", [D, IN_COLS])
    convw_d = din("convw", [128, 96])
    alog_d = din("alog", [1, 8])
    dtb_d = din("dtb", [1, 8])
    gnw_d = din("gnw", [1, RW])
    gnb_d = din("gnb", [1, RW])
    gdnw_d = din("gdnw", [1, GW])
    wout_d = din("wout", [D, D])
    lnw_d = din("lnw", [1, D])
    lnb_d = din("lnb", [1, D])
    cf_d = din("cf", [128, C_END])
    cb_d = din("cb", [128, B_END], BF16)
    rope_d = din("rope", [T, 256])
    out_d = nc.dram_tensor("out", [T, D], F32, kind="ExternalOutput").ap()
    skind = "ExternalOutput" if debug else "Internal"
    mT_d = nc.dram_tensor("mT", [NTILE, 128, D], BF16, kind=skind).ap()
    hT_d = nc.dram_tensor("hTs", [NTILE, 128, D], BF16, kind=skind).ap()

    P = Prog(nc)
    A = Arena(nc, 51968)
    ps = [Buf("ps%d" % i, nc.alloc_psum_tensor("ps%d" % i, [128, 512], F32), psum=True) for i in range(8)]

    cf = A.alloc("cf", C_END, F32)
    cb = A.alloc("cb", B_END, BF16)
    dma(P, cf.whole(), cf_d)
    dma(P, cb.whole(), cb_d)
    idf = cf.v(C_ID, C_ID + 128)
    idb = cb.v(B_ID, B_ID + 128)
    mhalf = A.alloc("mhalf", 8, F32)
    P.op("pool", lambda e: e.memset(mhalf.t[:, :], -0.5), outs=[mhalf.whole()])
    epsb = A.alloc("epsb", 1, F32)
    lqb = A.alloc("lqb", 1, F32)
    zb0 = A.alloc("zb0", 1, F32)
    oneb = A.alloc("oneb", 1, F32)
    P.op("pool", lambda e: e.memset(epsb.t[:, :], RMS_EPS), outs=[epsb.whole()])
    P.op("pool", lambda e: e.memset(lqb.t[:, :], float(np.log(128.0 ** -0.5))), outs=[lqb.whole()])
    P.op("pool", lambda e: e.memset(zb0.t[:, :], 0.0), outs=[zb0.whole()])
    P.op("pool", lambda e: e.memset(oneb.t[:, :], 1.0), outs=[oneb.whole()])
    modc = A.alloc("modc", 32, F32)
    sc1 = A.alloc("sc1", 16, F32)
    grow = A.alloc("grow", D, F32)
    R1 = A.alloc("R1", 16 * 3072, BF16)
    R2 = A.alloc("R2", 16 * 2056, BF16)
    base_mark = A.off
    R1f = R1.t.bitcast(F32)
    R2f = R2.t.bitcast(F32)

    wstg = [A.alloc("wstg%d" % i, 1024, F32) for i in range(3)]
    base_mark = A.off
    wl_cnt = [0]

    def load_w(dst, dst_stride, src_cols, dst_col0=0):
        pieces = []
        for (c0, n) in src_cols:
            o = 0
            while o < n:
                m = min(1024, n - o)
                pieces.append((c0 + o, m))
                o += m
        dc = dst_col0
        for (c0, m) in pieces:
            for k in range(16):
                i = wl_cnt[0]
                wl_cnt[0] += 1
                stg = wstg[i % 3]
                dma(P, stg.v(0, m), win_d[k * 128:(k + 1) * 128, c0:c0 + m])
                cp(P, ("pool", "dve", "act")[i % 3], dst.v(k * dst_stride + dc, k * dst_stride + dc + m), stg.v(0, m))
            dc += m

    load_w(R1, 3072, [(0, RET_COLS)])

    ccol = A.alloc("ccol", 16, F32)
    scol = A.alloc("scol", 16, F32)
    badac = A.alloc("badac", 32, F32)
    badag = A.alloc("badag", D, F32)
    dma(P, ccol.whole(), ccol_d)
    dma(P, badac.whole(), badac_d)
    dma(P, badag.v(0, D, 0, 1), badag_d)
    act(P, scol.whole(), ccol.whole(), AF.Silu)
    AR2 = Arena(None, R2f.shape[1], ap=R2f)
    wab = [AR2.alloc("wab%d" % i, 16 * 512, F32) for i in range(2)]
    for blk in range(12):
        wb = wab[blk % 2]
        src = wada_d[:, blk * 512:(blk + 1) * 512].rearrange("(k p) n -> p k n", p=128)
        fns = []
        for k0 in range(0, 16, 4):
            o_ap = wb.t[:, k0 * 512:(k0 + 4) * 512].rearrange("p (k n) -> p k n", k=4)
            i_ap = src[:, k0:k0 + 4, :]
            fns.append(lambda e, o_ap=o_ap, i_ap=i_ap: e.dma_start(out=o_ap, in_=i_ap))
        P.dma(fns, outs=[wb.whole()])
        if blk < 8:
            for jj in range(4):
                j = blk * 4 + jj
                for k in range(16):
                    mm(P, ps[0].v(j, j + 1), wb.v(k * 512 + jj * 128, k * 512 + (jj + 1) * 128), scol.v(k, k + 1),
                       start=(k == 0), stop=(k == 15))
        else:
            g = blk - 8
            bank = ps[1 + g % 2]
            for k in range(16):
                mm(P, bank.v(0, 512, 0, 1), scol.v(k, k + 1), wb.v(k * 512, (k + 1) * 512), start=(k == 0), stop=(k == 15))
            tt(P, "dve", grow.v(g * 512, (g + 1) * 512, 0, 1), bank.v(0, 512, 0, 1), badag.v(g * 512, (g + 1) * 512, 0, 1), ALU.add)
    tt(P, "dve", modc.whole(), ps[0].v(0, 32), badac.whole(), ALU.add)
    ts(P, "dve", sc1.whole(), modc.v(16, 32), 1.0, ALU.add)
    P.barrier()
    A.off = base_mark

    def gdn_pass(hg):
        import math
        A.off = base_mark
        AR1 = Arena(None, R1f.shape[1], ap=R1f)
        al = AR1.alloc
        W = R2
        WS = 2056
        o = GDN_OFF
        load_w(R2, WS, [(o + hg * 512, 512), (o + 1024 + hg * 512, 512), (o + 2048 + hg * 512, 512),
                        (o + 3072 + hg * 512, 512), (o + 4096 + hg * 4, 4), (o + 4104 + hg * 4, 4)])
        hT2 = [al("hT2_%d" % i, 16 * 256, BF16) for i in range(2)]
        qkvT = al("qkvT", 12 * 256, BF16)
        raw = [al("raw%d" % i, 264, F32) for i in range(2)]
        a1 = [al("a1_%d" % i, 256, F32) for i in range(2)]
        a2 = [al("a2_%d" % i, 256, F32) for i in range(2)]
        halo = al("halo", 48, F32)
        convw = al("convw", 96, F32)
        sgg = al("sgg", 1024, F32)
        gab = al("gab", 16, F32)
        sq = al("sq", 512, BF16)
        lnr = al("lnr", 512, F32)
        z8 = al("z8", 4, F32)
        e8 = al("e8", 8, F32)
        sp8 = al("sp8", 4, F32)
        g8 = al("g8", 4, F32)
        bt = al("bt", 4, F32)
        gc = al("gc", 8, F32)
        egc = al("egc", 4, F32)
        egl = al("egl", 4, F32)
        dgl = al("dgl", 4, F32)
        etl = al("etl", 4, F32)
        bege = al("bege", 4, F32)
        ss8 = al("ss8", 4, F32)
        nega = al("nega", 4, F32)
        dtb = al("dtb", 4, F32)
        gdnw = al("gdnw", 512, F32)
        Ghi = al("Ghi", 512, BF16)
        Glo = al("Glo", 512, BF16)
        dm = al("dm", 512, BF16)
        ndms = al("ndms", 512, BF16)
        kb = al("kb", 512, BF16)
        kbe = al("kbe", 512, BF16)
        ktl = al("ktl", 512, BF16)
        vbt = al("vbt", 512, BF16)
        kbT = al("kbT", 512, BF16)
        Qb = [al("Q%d" % i, 512, BF16) for i in range(2)]
        QTb = [al("QT%d" % i, 512, BF16) for i in range(2)]
        Xb = [al("X%d" % i, 512, BF16) for i in range(2)]
        attnT = al("attnT", 512, BF16)
        Bd = al("Bd", 512, BF16)
        Ad = al("Ad", 512, BF16)
        TUb = al("TUb", 512, BF16)
        W1s = al("W1s", 512, BF16)
        Aoff = [al("Aoff%d" % i, 512, BF16) for i in range(4)]
        nkcdT = al("nkcdT", 512, BF16)
        vnew = al("vnew", 512, BF16)
        Sg32 = al("Sg32", 512, F32)
        Sgbf = al("Sgbf", 512, BF16)
        o1s = al("o1s", 512, F32)
        osq = al("osq", 512, F32)
        gyb = al("gyb", 512, BF16)
        mst2 = al("mst2", 512, BF16)
        ones_bf = cb.v(B_ONES, B_ONES + 128)
        triu_bf = cb.v(B_TRIU, B_TRIU + 128)
        ntriu_bf = cb.v(B_NTRIU, B_NTRIU + 128)
        neg_bf = cb.v(B_NEG, B_NEG + 128)
        dma(P, convw.whole(), convw_d)
        dma(P, nega.whole(), alog_d[:, hg * 4:(hg + 1) * 4].partition_broadcast(128))
        dma(P, dtb.whole(), dtb_d[:, hg * 4:(hg + 1) * 4].partition_broadcast(128))
        dma(P, gdnw.whole(), gdnw_d[:, hg * 512:(hg + 1) * 512].partition_broadcast(128))
        act(P, nega.whole(), nega.whole(), AF.Exp)
        ts(P, "dve", nega.whole(), nega.whole(), -1.0, ALU.mult)
        P.op("pool", lambda e: e.memset(halo.t[:, :], 0.0), outs=[halo.whole()])
        P.op("pool", lambda e: e.memset(Sg32.t[:, :], 0.0), outs=[Sg32.whole()])
        P.op("pool", lambda e: e.memset(Sgbf.t[:, :], 0.0), outs=[Sgbf.whole()])

        def hd(buf, h):
            return buf.v(h * 128, (h + 1) * 128)

        for s in range(nt // 2):
            H = hT2[s % 2]
            for c in range(2):
                t = 2 * s + c
                dst = V(H.t[:, :].rearrange("p (k w) -> p k w", k=16)[:, :, c * 128:(c + 1) * 128], H, 0, 16 * 256)
                dma(P, dst, hT_d[t].rearrange("p (k w) -> p k w", k=16))
            for c in range(2):
                bank = ps[2 + c]
                for k in range(16):
                    mm(P, bank.whole(), H.v(k * 256 + c * 128, k * 256 + (c + 1) * 128), W.v(k * WS + 1536, k * WS + 2048),
                       start=(k == 0), stop=(k == 15))
                act(P, sgg.v(c * 512, (c + 1) * 512), bank.whole(), AF.Silu)
            for c in range(2):
                for k in range(16):
                    mm(P, ps[1].v(c * 8, (c + 1) * 8), H.v(k * 256 + c * 128, k * 256 + (c + 1) * 128),
                       W.v(k * WS + 2048, k * WS + 2056), start=(k == 0), stop=(k == 15))
            cp(P, "dve", gab.whole(), ps[1].v(0, 16))
            for ch in range(12):
                bank = ps[4 + ch % 2]
                for k in range(16):
                    mm(P, bank.v(0, 256), W.v(k * WS + ch * 128, k * WS + (ch + 1) * 128), H.v(k * 256, (k + 1) * 256),
                       start=(k == 0), stop=(k == 15))
                gch = (ch // 4) * 8 + hg * 4 + ch % 4
                R = raw[ch % 2]
                b1, b2 = a1[ch % 2], a2[ch % 2]
                cw = lambda tap, gch=gch: convw.v(gch * 4 + tap, gch * 4 + tap + 1)
                cp(P, "pool", R.v(0, 3), halo.v(ch * 4, ch * 4 + 3))
                cp(P, "act", R.v(3, 259), bank.v(0, 256))
                cp(P, "pool", halo.v(ch * 4, ch * 4 + 3), R.v(256, 259))
                ts(P, "pool", b1.whole(), R.v(0, 256), cw(0), ALU.mult)
                stt(P, b1.whole(), R.v(1, 257), cw(1), b1.whole(), ALU.mult, ALU.add)
                ts(P, "pool", b2.whole(), R.v(2, 258), cw(2), ALU.mult)
                stt(P, b2.whole(), R.v(3, 259), cw(3), b2.whole(), ALU.mult, ALU.add)
                tt(P, "pool", b1.whole(), b1.whole(), b2.whole(), ALU.add)
                act(P, qkvT.v(ch * 256, (ch + 1) * 256), b1.whole(), AF.Silu)
            for pr in range(4):
                reg = qkvT.v(pr * 512, (pr + 1) * 512)
                tt(P, "pool", sq.whole(), reg, reg, ALU.mult)
                bank = ps[6 + pr % 2]
                for u in range(2):
                    mm(P, bank.v(u * 256, (u + 1) * 256), ones_bf, sq.v(u * 256, (u + 1) * 256))
                act(P, lnr.whole(), bank.whole(), AF.Ln, bias=epsb.whole())
                act(P, lnr.whole(), lnr.whole(), AF.Exp, scale=-0.5, bias=(lqb.whole() if pr < 2 else zb0.whole()))
                tt(P, "dve", reg, reg, lnr.whole(), ALU.mult)
            for c in range(2):
                t = 2 * s + c
                qT = lambda h: qkvT.v(h * 256 + c * 128, h * 256 + (c + 1) * 128)
                kT = lambda h: qkvT.v((4 + h) * 256 + c * 128, (4 + h) * 256 + (c + 1) * 128)
                vT = lambda h: qkvT.v((8 + h) * 256 + c * 128, (8 + h) * 256 + (c + 1) * 128)
                ga = gab.v(c * 8, c * 8 + 4)
                gb = gab.v(c * 8 + 4, c * 8 + 8)
                tt(P, "dve", z8.whole(), ga, dtb.whole(), ALU.add)
                act(P, e8.v(0, 4), z8.whole(), AF.Exp)
                act(P, e8.v(4, 8), gb, AF.Exp, scale=-1.0)
                act(P, sp8.whole(), e8.v(0, 4), AF.Ln, bias=oneb.whole())
                tt(P, "dve", g8.whole(), sp8.whole(), nega.whole(), ALU.mult)
                ts(P, "dve", bt.whole(), e8.v(4, 8), 1.0, ALU.add)
                P.op("dve", lambda e: e.reciprocal(out=bt.t[:, :], in_=bt.t[:, :]), outs=[bt.whole()], ins=[bt.whole()])
                mm(P, ps[0].v(0, 4), cf.v(C_TRIU, C_TRIU + 128), g8.whole())
                mm(P, ps[0].v(4, 8), cf.v(C_ONES, C_ONES + 128), g8.whole())
                cp(P, "dve", gc.whole(), ps[0].v(0, 8))
                act(P, egc.whole(), gc.v(0, 4), AF.Exp)
                act(P, egl.whole(), gc.v(4, 8), AF.Exp)
                tt(P, "dve", dgl.whole(), gc.v(4, 8), gc.v(0, 4), ALU.subtract)
                act(P, etl.whole(), dgl.whole(), AF.Exp)
                tt(P, "dve", bege.whole(), bt.whole(), egc.whole(), ALU.mult)
                gbc_ = bcast_last(g8.whole(), 4, 128)
                cp(P, "pool", r3(Ghi.whole(), 4), gbc_)
                tt(P, "pool", r3(Glo.whole(), 4), gbc_, r3(Ghi.whole(), 4), ALU.subtract)
                for h in range(4):
                    Db = ps[4].v(h * 128, (h + 1) * 128)
                    mm(P, Db, hd(Ghi, h), triu_bf, start=True, stop=False)
                    mm(P, Db, hd(Glo, h), triu_bf, start=False, stop=False)
                    mm(P, Db, ntriu_bf, hd(Ghi, h), start=False, stop=False)
                    mm(P, Db, ntriu_bf, hd(Glo, h), start=False, stop=False)
                    mm(P, Db, idb, neg_bf, start=False, stop=True)
                act(P, dm.whole(), ps[4].whole(), AF.Exp)
                tt(P, "pool", r3(ndms.whole(), 4), r3(dm.whole(), 4), bcast_mid(cf.v(C_NSTRICT, C_NSTRICT + 128), 4, 128), ALU.mult)
                for h in range(4):
                    tr(P, vb(ps[0], h * 128, (h + 1) * 128), kT(h), idb)
                for h in range(4):
                    tr(P, vb(ps[1], h * 128, (h + 1) * 128), vT(h), idb)
                k3 = r3(vb(ps[0], 0, 512), 4)
                tt(P, "dve", r3(kb.whole(), 4), k3, bcast_last(bt.whole(), 4, 128), ALU.mult)
                tt(P, "dve", r3(kbe.whole(), 4), k3, bcast_last(bege.whole(), 4, 128), ALU.mult)
                tt(P, "dve", r3(ktl.whole(), 4), k3, bcast_last(etl.whole(), 4, 128), ALU.mult)
                tt(P, "dve", r3(vbt.whole(), 4), r3(vb(ps[1], 0, 512), 4), bcast_last(bt.whole(), 4, 128), ALU.mult)
                for h in range(4):
                    tr(P, vb(ps[2], h * 128, (h + 1) * 128), hd(kb, h), idb)
                cp(P, "act", kbT.whole(), vb(ps[2], 0, 512))
                for h in range(4):
                    mm(P, ps[6].v(h * 128, (h + 1) * 128), kT(h), hd(kbT, h))
                tt(P, "dve", Qb[0].whole(), ps[6].whole(), ndms.whole(), ALU.mult)
                for h in range(4):
                    mm(P, ps[3].v(h * 128, (h + 1) * 128), kT(h), qT(h))
                tt(P, "dve", attnT.whole(), ps[3].whole(), dm.whole(), ALU.mult)
                for h in range(4):
                    tr(P, vb(ps[0], h * 128, (h + 1) * 128), hd(Qb[0], h), idb)
                cp(P, "act", QTb[0].whole(), vb(ps[0], 0, 512))
                mk = lambda col: bcast_mid(cf.v(col, col + 128), 4, 128)
                tt(P, "pool", r3(Bd.whole(), 4), r3(Qb[0].whole(), 4), mk(C_D8), ALU.mult)
                tt(P, "pool", r3(Ad.whole(), 4), r3(QTb[0].whole(), 4), mk(C_D8), ALU.mult)
                for li, col in enumerate((C_M8, C_M16, C_M32, C_M64)):
                    tt(P, "pool", r3(Aoff[li].whole(), 4), r3(QTb[0].whole(), 4), mk(col), ALU.mult)
                tt(P, "pool", r3(Xb[0].whole(), 4), r3(Bd.whole(), 4), mk(C_ID), ALU.add)
                for h in range(4):
                    mm(P, ps[4].v(h * 128, (h + 1) * 128), hd(Ad, h), hd(Bd, h))
                for h in range(4):
                    mm(P, ps[6].v(h * 128, (h + 1) * 128), hd(Bd, h), hd(Ad, h))
                cp(P, "act", Qb[1].whole(), ps[4].whole())
                cp(P, "dve", QTb[1].whole(), ps[6].whole())
                for h in range(4):
                    mm(P, ps[3].v(h * 128, (h + 1) * 128), hd(QTb[1], h), hd(Xb[0], h))
                tt(P, "dve", Xb[1].whole(), ps[3].whole(), Xb[0].whole(), ALU.add)
                for h in range(4):
                    mm(P, ps[6].v(h * 128, (h + 1) * 128), hd(Qb[1], h), hd(QTb[1], h))
                cp(P, "dve", Ad.whole(), ps[6].whole())
                for h in range(4):
                    mm(P, ps[3].v(h * 128, (h + 1) * 128), hd(Ad, h), hd(Xb[1], h))
                tt(P, "dve", Xb[0].whole(), ps[3].whole(), Xb[1].whole(), ALU.add)
                cur = 0
                for li in range(4):
                    U = Xb[cur]
                    for h in range(4):
                        tr(P, vb(ps[0], h * 128, (h + 1) * 128), hd(U, h), idb)
                    cp(P, "act", TUb.whole(), vb(ps[0], 0, 512))
                    for h in range(4):
                        mm(P, ps[4].v(h * 128, (h + 1) * 128), hd(Aoff[li], h), hd(U, h))
                    cp(P, "act", W1s.whole(), ps[4].whole())
                    for h in range(4):
                        mm(P, ps[3].v(h * 128, (h + 1) * 128), hd(TUb, h), hd(W1s, h))
                    tt(P, "dve", Xb[1 - cur].whole(), ps[3].whole(), U.whole(), ALU.add)
                    cur = 1 - cur
                TT = Xb[cur]
                for h in range(4):
                    mm(P, ps[4].v(h * 128, (h + 1) * 128), hd(kbe, h), hd(TT, h))
                act(P, nkcdT.whole(), ps[4].whole(), AF.Identity, scale=-1.0)
                for h in range(4):
                    pv = ps[6].v(h * 128, (h + 1) * 128)
                    mm(P, pv, hd(TT, h), hd(vbt, h), start=True, stop=False)
                    mm(P, pv, hd(nkcdT, h), hd(Sgbf, h), start=False, stop=True)
                cp(P, "act", vnew.whole(), ps[6].whole())
                for h in range(4):
                    mm(P, ps[0].v(h * 128, (h + 1) * 128), qT(h), hd(Sgbf, h))
                for h in range(4):
                    mm(P, ps[2].v(h * 128, (h + 1) * 128), hd(attnT, h), hd(vnew, h))
                for h in range(4):
                    mm(P, ps[5].v(h * 128, (h + 1) * 128), hd(ktl, h), hd(vnew, h))
                for h in range(4):
                    stt(P, hd(Sg32, h), hd(Sg32, h), egl.v(h, h + 1), ps[5].v(h * 128, (h + 1) * 128), ALU.mult, ALU.add)
                cp(P, "pool", Sgbf.whole(), Sg32.whole())
                tt(P, "dve", r3(o1s.whole(), 4), r3(ps[0].whole(), 4), bcast_last(egc.whole(), 4, 128), ALU.mult)
                tt(P, "dve", o1s.whole(), ps[2].whole(), o1s.whole(), ALU.add)
                tt(P, "pool", osq.whole(), o1s.whole(), o1s.whole(), ALU.mult)
                P.op("dve", lambda e: e.tensor_reduce(out=ss8.t[:, :], in_=osq.t[:, :].rearrange("p (h e) -> p h e", h=4),
                                                      axis=AX.X, op=ALU.add), outs=[ss8.whole()], ins=[osq.whole()])
                ts(P, "pool", ss8.whole(), ss8.whole(), 1.0 / 128.0, ALU.mult, RMS_EPS, ALU.add)
                tt(P, "pool", ss8.whole(), ss8.whole(), mhalf.v(0, 4), ALU.pow)
                tt(P, "dve", r3(o1s.whole(), 4), r3(o1s.whole(), 4), bcast_last(ss8.whole(), 4, 128), ALU.mult)
                tt(P, "pool", osq.whole(), sgg.v(c * 512, (c + 1) * 512), gdnw.whole(), ALU.mult)
                tt(P, "pool", gyb.whole(), o1s.whole(), osq.whole(), ALU.mult)
                for h in range(4):
                    tr(P, vb(ps[1], h * 128, (h + 1) * 128), hd(gyb, h), idb)
                cp(P, "act", mst2.whole(), vb(ps[1], 0, 512))
                dma(P, mT_d[t, :, 1024 + hg * 512:1024 + (hg + 1) * 512], mst2.whole())
        P.barrier()

    def pass3():
        A.off = base_mark
        AR1 = Arena(None, R1f.shape[1], ap=R1f)
        AR2 = Arena(None, R2f.shape[1], ap=R2f)
        Wo = AR1.alloc("Wo", 16 * D, BF16)
        gbc = AR2.alloc("gbc", D, F32)
        wst = [AR2.alloc("wst%d" % i, D, F32) for i in range(2)]
        xs3 = [AR2.alloc("x3_%d" % i, D, F32) for i in range(2)]
        mts = [AR2.alloc("mts%d" % i, D, BF16) for i in range(2)]
        zb = [AR2.alloc("zb%d" % i, D, F32) for i in range(2)]
        lnw = AR1.alloc("lnw", D, F32)
        lnb = AR1.alloc("lnb", D, F32)
        st3 = A.alloc("st3", 24, F32)
        mv3 = A.alloc("mv3", 2, F32)
        rs3 = A.alloc("rs3", 1, F32)
        nm3 = A.alloc("nm3", 1, F32)
        dma(P, lnw.whole(), lnw_d.partition_broadcast(128))
        dma(P, lnb.whole(), lnb_d.partition_broadcast(128))
        for g in range(4):
            bank = ps[g % 2]
            mm(P, bank.whole(), cf.v(C_ONES, C_ONES + 128, 0, 1), grow.v(g * 512, (g + 1) * 512, 0, 1))
            cp(P, "act", gbc.v(g * 512, (g + 1) * 512), bank.whole())
        for k in range(16):
            dma(P, wst[k % 2].whole(), wout_d[k * 128:(k + 1) * 128, :])
            tt(P, "dve" if k % 2 else "pool", Wo.v(k * D, (k + 1) * D), wst[k % 2].whole(), gbc.whole(), ALU.mult)
        for t in range(nt):
            sl = t % 2
            dma(P, xs3[sl].whole(), x_d[t * 128:(t + 1) * 128, :])
            dma(P, mts[sl].whole(), mT_d[t])
            z = zb[sl]
            for n in range(4):
                bank = ps[(t % 2) * 4 + n]
                for k in range(16):
                    mm(P, bank.whole(), mts[sl].v(k * 128, (k + 1) * 128), Wo.v(k * D + n * 512, k * D + (n + 1) * 512),
                       start=(k == 0), stop=(k == 15))
                stt(P, z.v(n * 512, (n + 1) * 512), xs3[sl].v(n * 512, (n + 1) * 512), float(ALPHA), bank.whole(), ALU.mult, ALU.add)
                P.op("dve", lambda e, n=n, z=z: e.bn_stats(out=st3.t[:, n * 6:(n + 1) * 6], in_=z.t[:, n * 512:(n + 1) * 512]),
                     outs=[st3.v(n * 6, (n + 1) * 6)], ins=[z.v(n * 512, (n + 1) * 512)])
            P.op("dve", lambda e: e.bn_aggr(out=mv3.t[:, :], in_=st3.t[:, :]), outs=[mv3.whole()], ins=[st3.whole()])
            ts(P, "pool", rs3.whole(), mv3.v(1, 2), LN_EPS, ALU.add)
            tt(P, "pool", rs3.whole(), rs3.whole(), mhalf.v(0, 1), ALU.pow)
            tt(P, "pool", nm3.whole(), mv3.v(0, 1), rs3.whole(), ALU.mult)
            ts(P, "pool", nm3.whole(), nm3.whole(), -1.0, ALU.mult)
            act(P, z.whole(), z.whole(), AF.Identity, scale=rs3.whole(), bias=nm3.whole())
            tt(P, "pool", z.whole(), z.whole(), lnw.whole(), ALU.mult)
            tt(P, "dve", z.whole(), z.whole(), lnb.whole(), ALU.add)
            dma(P, out_d[t * 128:(t + 1) * 128, :], z.whole())

    AR2 = Arena(None, R2f.shape[1], ap=R2f)
    xs = [AR2.alloc("xs%d" % i, D, F32) for i in range(2)]
    hT1 = [AR2.alloc("hT1_%d" % i, D, BF16) for i in range(2)]
    rp = [AR2.alloc("rp%d" % i, 256, F32) for i in range(2)]
    qs = AR2.alloc("qs", 512, F32)
    ks = AR2.alloc("ks", 512, F32)
    tA = AR2.alloc("tA", 512, F32)
    tB = AR2.alloc("tB", 512, F32)
    qhat = AR2.alloc("qhat", 1024, BF16)
    qkT = AR2.alloc("qkT", 1024, BF16)
    vbf = AR2.alloc("vbf", 1024, BF16)
    sg = AR2.alloc("sg", 1024, F32)
    sT = AR2.alloc("sT", 512, BF16)
    S32 = AR2.alloc("S32", 1024, F32)
    Sbf = AR2.alloc("Sbf", 1024, BF16)
    yn = AR2.alloc("yn", 1024, F32)
    retb = AR2.alloc("retb", 1024, BF16)
    mst = AR2.alloc("mst", 1024, BF16)
    gnw = A.alloc("gnw", RW, F32)
    gnb = A.alloc("gnb", RW, F32)
    st1 = A.alloc("st1", 24, F32)
    mv1 = A.alloc("mv1", 8, F32)
    rs1 = A.alloc("rs1", 4, F32)
    nm1 = A.alloc("nm1", 4, F32)
    KS = int(os.environ.get("KSUB", "0"))
    if not KS & 1:
        dma(P, gnw.whole(), gnw_d.partition_broadcast(128))
        dma(P, gnb.whole(), gnb_d.partition_broadcast(128))
    if not KS & 2:
        P.op("pool", lambda e: e.memset(S32.t[:, :], 0.0), outs=[S32.whole()])
        P.op("pool", lambda e: e.memset(Sbf.t[:, :], 0.0), outs=[Sbf.whole()])

    def make_hT(xbuf, hT):
        for g4 in range(4):
            bank = ps[g4 % 2]
            for kk in range(4):
                k = g4 * 4 + kk
                tr(P, bank.v(kk * 128, (kk + 1) * 128), xbuf.v(k * 128, (k + 1) * 128), idf)
            for kk in range(4):
                k = g4 * 4 + kk
                dst = hT.v(k * 128, (k + 1) * 128)
                src = bank.v(kk * 128, (kk + 1) * 128)
                KHT = int(os.environ.get("KHT", "0"))
                if (g4 % 2 == 0 and KHT == 0) or KHT == 1:
                    act(P, dst, src, AF.Identity, scale=sc1.v(k, k + 1), bias=modc.v(k, k + 1))
                else:
                    ts(P, "dve", dst, src, sc1.v(k, k + 1), ALU.mult, modc.v(k, k + 1), ALU.add)

    def rotary(src, dst, rpb):
        a = src.whole().ap
        rot = src.whole().with_ap(bass.AP(tensor=a.tensor, offset=a.offset + 64,
                                          ap=[list(a.ap[0]), [128, 4], [-64, 2], [1, 64]]))
        c2 = bcast_mid(rpb.v(0, 128), 4, 128)
        s = rpb.v(128, 256).ap
        s2 = rpb.v(128, 256).with_ap(bass.AP(tensor=s.tensor, offset=s.offset, ap=[list(s.ap[0]), [0, 4], [64, 2], [1, 64]]))
        tt(P, "dve", r3(tA.whole(), 4), r3(src.whole(), 4), c2, ALU.mult)
        tt(P, "pool", tB.whole().r("p (h t d) -> p h t d", h=4, t=2), rot, s2, ALU.mult)
        tt(P, "dve", dst, tA.whole(), tB.whole(), ALU.add)

    for t in range(nt if stop_after >= 1 else 0):
        sl = t % 2
        dma(P, xs[sl].whole(), x_d[t * 128:(t + 1) * 128, :])
        if not KS & 16:
            dma(P, rp[sl].whole(), rope_d[t * 128:(t + 1) * 128, :])
        H = hT1[sl]
        if not KS & 8:
            make_hT(xs[sl], H)
        if not KS & 4:
            dma(P, hT_d[t], H.whole())
        KL = int(os.environ.get("KLVL", "9"))
        if KL < 1:
            continue
        for n in range(6):
            bank = ps[2 + n % 2]
            for k in range(16):
                mm(P, bank.whole(), H.v(k * 128, (k + 1) * 128), R1.v(k * 3072 + n * 512, k * 3072 + (n + 1) * 512),
                   start=(k == 0), stop=(k == 15))
            if n == 0:
                tt(P, "dve", qs.whole(), bank.whole(), cf.v(C_GQ, C_GQ + 512), ALU.mult)
                rotary(qs, qhat.v(0, 512), rp[sl])
            elif n == 1:
                tt(P, "dve", ks.whole(), bank.whole(), cf.v(C_GK, C_GK + 512), ALU.mult)
                rotary(ks, qhat.v(512, 1024), rp[sl])
            elif n < 4:
                cp(P, "act", vbf.v((n - 2) * 512, (n - 1) * 512), bank.whole())
            else:
                act(P, sg.v((n - 4) * 512, (n - 3) * 512), bank.whole(), AF.Silu)
        if KL < 2:
            continue
        for m in range(8):
            tr(P, vb(ps[4], m * 128, (m + 1) * 128), qhat.v(m * 128, (m + 1) * 128), idb)
        cp(P, "dve", qkT.whole(), vb(ps[4], 0, 1024))
        if KL < 3:
            continue
        for h in range(4):
            mm(P, ps[5].v(h * 128, (h + 1) * 128), qkT.v(512 + h * 128, 512 + (h + 1) * 128), qkT.v(h * 128, (h + 1) * 128))
        tt(P, "dve", sT.whole(), ps[5].whole(), cf.v(C_RMASK, C_RMASK + 512), ALU.mult)
        obs = []
        for h in range(4):
            ob = ps[6 + h // 2].v((h % 2) * 256, (h % 2 + 1) * 256)
            obs.append(ob)
            mm(P, ob, sT.v(h * 128, (h + 1) * 128), vbf.v(h * 256, (h + 1) * 256), start=True, stop=False)
            mm(P, ob, qkT.v(h * 128, (h + 1) * 128), Sbf.v(h * 256, (h + 1) * 256), start=False, stop=True)
        ubs = []
        for h in range(4):
            ub = ps[h // 2].v((h % 2) * 256, (h % 2 + 1) * 256)
            ubs.append(ub)
            mm(P, ub, qhat.v(512 + h * 128, 512 + (h + 1) * 128), vbf.v(h * 256, (h + 1) * 256))
        if KL < 4:
            continue
        for h in range(4):
            P.op("dve", lambda e, h=h: e.bn_stats(out=st1.t[:, h * 6:(h + 1) * 6], in_=obs[h].ap),
                 outs=[st1.v(h * 6, (h + 1) * 6)], ins=[obs[h]])
        for h in range(4):
            P.op("dve", lambda e, h=h: e.bn_aggr(out=mv1.t[:, h * 2:(h + 1) * 2], in_=st1.t[:, h * 6:(h + 1) * 6]),
                 outs=[mv1.v(h * 2, (h + 1) * 2)], ins=[st1.v(h * 6, (h + 1) * 6)])
        if KL < 5:
            continue
        mvv = mv1.whole().r("p (h two) -> p h two", two=2)
        ts(P, "pool", rs1.whole(), mvv[:, :, 1], GN_EPS, ALU.add)
        tt(P, "pool", rs1.whole(), rs1.whole(), mhalf.v(0, 4), ALU.pow)
        tt(P, "pool", nm1.whole(), mvv[:, :, 0], rs1.whole(), ALU.mult)
        ts(P, "pool", nm1.whole(), nm1.whole(), -1.0, ALU.mult)
        if KL < 6:
            continue
        for h in range(4):
            act(P, yn.v(h * 256, (h + 1) * 256), obs[h], AF.Identity, scale=rs1.v(h, h + 1), bias=nm1.v(h, h + 1))
        for h in range(4):
            stt(P, S32.v(h * 256, (h + 1) * 256), S32.v(h * 256, (h + 1) * 256), float(GAMMAS[h] ** 128), ubs[h], ALU.mult, ALU.add)
        cp(P, "pool", Sbf.whole(), S32.whole())
        tt(P, "pool", yn.whole(), yn.whole(), gnw.whole(), ALU.mult)
        tt(P, "dve", yn.whole(), yn.whole(), gnb.whole(), ALU.add)
        tt(P, "pool", retb.whole(), yn.whole(), sg.whole(), ALU.mult)
        for m in range(8):
            tr(P, vb(ps[4], m * 128, (m + 1) * 128), retb.v(m * 128, (m + 1) * 128), idb)
        cp(P, "act", mst.whole(), vb(ps[4], 0, 1024))
        dma(P, mT_d[t, :, 0:1024], mst.whole())
    P.barrier()
    A.off = base_mark
    if stop_after >= 2:
        for hg in range(2):
            gdn_pass(hg)
    if stop_after >= 3:
        pass3()
    return nc, P, A, ps, locals()


def finish(nc, P):
    P.final_wait()
    P.finalize()
    sems = {te: nc.alloc_semaphore("s_" + te) for te in P.teops}
    P.emit(sems)
    return nc


_CONSTS = None


def prep_core_inputs(inp, b):
    global _CONSTS
    if _CONSTS is None:
        _CONSTS = host_consts()
    cf, cbt, rope = _CONSTS
    f = lambda a: np.ascontiguousarray(np.asarray(a, dtype=np.float32))
    b_ada = np.asarray(inp["b_ada"])[0]
    conv = np.asarray(inp["gdn_conv_w"])[0]
    return {
        "x": f(np.asarray(inp["x"])[b]),
        "ccol": f(np.asarray(inp["c"])[b].reshape(16, 128).T),
        "wada": f(np.asarray(inp["w_ada"])[0]),
        "badac": f(b_ada[:4096].reshape(32, 128).T),
        "badag": f(b_ada[4096:].reshape(1, D)),
        "win": f(np.asarray(inp["w_in"])[0]),
        "convw": f(conv.reshape(4, 24, 128).transpose(2, 1, 0).reshape(128, 96)),
        "alog": f(np.asarray(inp["gdn_a_log"])[0].reshape(1, 8)),
        "dtb": f(np.asarray(inp["gdn_dt_bias"])[0].reshape(1, 8)),
        "gnw": f(np.asarray(inp["ret_gn_w"])[0].reshape(1, RW)),
        "gnb": f(np.asarray(inp["ret_gn_b"])[0].reshape(1, RW)),
        "gdnw": f(np.tile(np.asarray(inp["gdn_norm_w"])[0], 8).reshape(1, GW)),
        "wout": f(np.asarray(inp["w_out"])[0]),
        "lnw": f(np.asarray(inp["ln_w"])[0].reshape(1, D)),
        "lnb": f(np.asarray(inp["ln_b"])[0].reshape(1, D)),
        "cf": cf, "cb": cbt, "rope": rope,
    }


def kernel(**inputs):
    nc, P, A, ps, _ = build(nt=NTILE, stop_after=3, debug=False)
    finish(nc, P)
    in_maps = [prep_core_inputs(inputs, b) for b in range(8)]
    res = run_bass_kernel_spmd(nc, in_maps, core_ids=list(range(8)))
    out = np.stack([np.asarray(r["out"], dtype=np.float32) for r in res.results], axis=0)
    return out.astype(np.asarray(inputs["x"]).dtype)
```

```python
import os
import numpy as np
import ml_dtypes
import concourse.bass as bass
import concourse.mybir as mybir
from concourse.bass_utils import run_bass_kernel_spmd

F32 = mybir.dt.float32
BF16 = mybir.dt.bfloat16
AF = mybir.ActivationFunctionType
ALU = mybir.AluOpType
AX = mybir.AxisListType


class Buf:
    def __init__(self, name, t, psum=False):
        self.name = name
        self.t = t
        self.psum = psum
        self.reads = {}
        self.writes = {}

    def v(self, lo, hi, p0=0, p1=None):
        if p1 is None:
            p1 = self.t.shape[0]
        if self.psum:
            return V(self.t[p0:p1, lo:hi], self, 0, 512)
        return V(self.t[p0:p1, lo:hi], self, lo, hi)

    def whole(self):
        return self.v(0, self.t.shape[1])


class V:
    def __init__(self, ap, buf, lo, hi):
        self.ap = ap
        self.buf = buf
        self.lo = lo
        self.hi = hi

    def r(self, pattern, **kw):
        return V(self.ap.rearrange(pattern, **kw), self.buf, self.lo, self.hi)

    def __getitem__(self, idx):
        return V(self.ap[idx], self.buf, self.lo, self.hi)

    def with_ap(self, ap):
        return V(ap, self.buf, self.lo, self.hi)


class Op:
    __slots__ = ("q", "te", "seq", "deps", "fn", "waits", "signal", "ndma")


class Prog:
    QUEUES = ("pe", "act", "dve", "pool", "sp")

    def __init__(self, nc, n_lanes=12):
        self.nc = nc
        self.qops = {q: [] for q in self.QUEUES}
        self.teops = {q: [] for q in self.QUEUES}
        self.lanes = ["L%d" % i for i in range(n_lanes)]
        for l in self.lanes:
            self.teops[l] = []
        self.next_lane = 0

    def _track(self, op, outs, ins):
        deps = {}

        def need(te, seq):
            if te == op.te:
                return
            if deps.get(te, -1) < seq:
                deps[te] = seq

        raw_same = -1
        for v in ins:
            for (te, lo, hi), seq in v.buf.writes.items():
                if lo < v.hi and v.lo < hi:
                    if te == op.te:
                        raw_same = max(raw_same, seq)
                    else:
                        need(te, seq)
            if v.buf.psum:
                for (te, lo, hi), seq in v.buf.reads.items():
                    need(te, seq)
        for v in outs:
            for d in (v.buf.writes, v.buf.reads):
                for (te, lo, hi), seq in d.items():
                    if lo < v.hi and v.lo < hi:
                        if te == op.te:
                            raw_same = max(raw_same, seq)
                        else:
                            need(te, seq)
        if raw_same >= 0 and op.te in ("act", "dve", "pool"):
            deps[op.te] = raw_same
        op.deps = deps
        for v in outs:
            for d in (v.buf.writes, v.buf.reads):
                for k in [k for k in d if v.lo <= k[1] and k[2] <= v.hi]:
                    del d[k]
            v.buf.writes[(op.te, v.lo, v.hi)] = op.seq
        for v in ins:
            v.buf.reads[(op.te, v.lo, v.hi)] = op.seq

    def op(self, q, fn, outs=(), ins=()):
        o = Op()
        o.q = q
        o.te = q
        o.seq = len(self.teops[q])
        o.fn = fn
        o.ndma = 0
        self._track(o, outs, ins)
        self.teops[q].append(o)
        self.qops[q].append(o)
        return o

    def dma(self, fns, outs=(), ins=(), q="sp"):
        lane = self.lanes[self.next_lane]
        self.next_lane = (self.next_lane + 1) % len(self.lanes)
        o = Op()
        o.q = q
        o.te = lane
        o.seq = len(self.teops[lane])
        o.fn = fns
        o.ndma = len(fns)
        self._track(o, outs, ins)
        if o.seq > 0:
            o.deps[lane] = o.seq - 1
        self.teops[lane].append(o)
        self.qops[q].append(o)
        return o

    def final_wait(self):
        o = Op()
        o.q = "sp"
        o.te = "sp"
        o.seq = len(self.teops["sp"])
        o.fn = None
        o.ndma = 0
        o.deps = {te: len(ops) - 1 for te, ops in self.teops.items() if te.startswith("L") and ops}
        self.teops["sp"].append(o)
        self.qops["sp"].append(o)

    def finalize(self):
        for q in self.QUEUES:
            seen = {}
            for o in self.qops[q]:
                w = []
                for te, seq in o.deps.items():
                    if seen.get(te, -1) < seq:
                        seen[te] = seq
                        w.append((te, seq))
                o.waits = w
        for te, ops in self.teops.items():
            for o in ops:
                o.signal = te.startswith("L")
        for q in self.QUEUES:
            for o in self.qops[q]:
                for te, seq in o.waits:
                    self.teops[te][seq].signal = True
        self.val = {}
        for te, ops in self.teops.items():
            c = 0
            vals = []
            for o in ops:
                if te.startswith("L"):
                    c += 16 * o.ndma
                elif o.signal:
                    c += 1
                vals.append(c)
            self.val[te] = vals

    def emit(self, sems):
        nc = self.nc
        with nc.Block() as block:

            def mk(q):
                def body(eng):
                    for o in self.qops[q]:
                        for te, seq in o.waits:
                            eng.wait_ge(sems[te], self.val[te][seq])
                        if o.ndma:
                            for f in o.fn:
                                f(eng).then_inc(sems[o.te], 16)
                        elif o.fn is not None:
                            ins = o.fn(eng)
                            if o.signal:
                                ins.then_inc(sems[o.te], 1)

                return body

            block.tensor(mk("pe"))
            block.scalar(mk("act"))
            block.vector(mk("dve"))
            block.gpsimd(mk("pool"))
            block.sync(mk("sp"))

    def barrier(self):
        last = {}
        for te, ops in self.teops.items():
            i = len(ops) - 1
            while i >= 0 and ops[i].fn is None:
                i -= 1
            if i >= 0:
                last[te] = i
        for q in self.QUEUES:
            o = Op()
            o.q = q
            o.te = q
            o.seq = len(self.teops[q])
            o.fn = None
            o.ndma = 0
            o.deps = {te: s for te, s in last.items() if te != q}
            self.teops[q].append(o)
            self.qops[q].append(o)


def _ap(x):
    return x.ap if isinstance(x, V) else x


def _vs(*xs):
    return [x for x in xs if isinstance(x, V)]


def bcast_mid(v, n, inner):
    a = v.ap
    return v.with_ap(bass.AP(tensor=a.tensor, offset=a.offset, ap=[list(a.ap[0]), [0, n], [1, inner]]))


def bcast_last(v, n, inner):
    a = v.ap
    return v.with_ap(bass.AP(tensor=a.tensor, offset=a.offset, ap=[list(a.ap[0]), [a.ap[1][0], n], [0, inner]]))


def mm(P, out, lhsT, rhs, start=True, stop=True):
    return P.op("pe", lambda e: e.matmul(out=out.ap, lhsT=lhsT.ap, rhs=rhs.ap, start=start, stop=stop),
                outs=[out], ins=[lhsT, rhs])


def tr(P, out, in_, ident):
    return P.op("pe", lambda e: e.transpose(out=out.ap, in_=in_.ap, identity=ident.ap), outs=[out], ins=[in_, ident])


def act(P, out, in_, func, scale=1.0, bias=0.0):
    return P.op("act", lambda e: e.activation(out=out.ap, in_=in_.ap, func=func, scale=_ap(scale), bias=_ap(bias)),
                outs=[out], ins=[in_] + _vs(scale, bias))


def tt(P, q, out, a, b, op):
    return P.op(q, lambda e: e.tensor_tensor(out=out.ap, in0=a.ap, in1=b.ap, op=op), outs=[out], ins=[a, b])


def ts(P, q, out, a, s1, op0, s2=None, op1=None):
    if op1 is None:
        return P.op(q, lambda e: e.tensor_scalar(out=out.ap, in0=a.ap, scalar1=_ap(s1), scalar2=None, op0=op0),
                    outs=[out], ins=[a] + _vs(s1))
    return P.op(q, lambda e: e.tensor_scalar(out=out.ap, in0=a.ap, scalar1=_ap(s1), scalar2=_ap(s2), op0=op0, op1=op1),
                outs=[out], ins=[a] + _vs(s1, s2))


def stt(P, out, a, s, b, op0, op1):
    return P.op("dve", lambda e: e.scalar_tensor_tensor(out=out.ap, in0=a.ap, scalar=_ap(s), in1=b.ap, op0=op0, op1=op1),
                outs=[out], ins=[a, b] + _vs(s))


def cp(P, q, out, in_):
    if q == "act":
        return P.op("act", lambda e: e.copy(out=out.ap, in_=in_.ap), outs=[out], ins=[in_])
    return P.op(q, lambda e: e.tensor_copy(out=out.ap, in_=in_.ap), outs=[out], ins=[in_])


def dma(P, out, in_, q="sp", outs=None, ins=None, **kw):
    o = out.ap if isinstance(out, V) else out
    i = in_.ap if isinstance(in_, V) else in_
    return P.dma([lambda e: e.dma_start(out=o, in_=i, **kw)], outs=_vs(out) if outs is None else outs,
                 ins=_vs(in_) if ins is None else ins, q=q)


T = 4096
D = 2048
NTILE = T // 128
RQW = 512
RW = 1024
GW = 1024
IN_COLS = 7184
RET_COLS = 3072
GDN_OFF = 3072
GDN_COLS = 4112
GN_EPS = 1e-5
RMS_EPS = 1e-6
LN_EPS = 1e-5
ALPHA = 2.0 ** 0.25
GAMMAS = [1.0 - 2.0 ** (-5.0 - h) for h in range(4)]

C_ID, C_TRIU, C_ONES, C_GQ, C_GK, C_RMASK, C_NSTRICT, C_D8, C_M8, C_M16, C_M32, C_M64, C_END = 0, 128, 256, 384, 896, 1408, 1920, 2048, 2176, 2304, 2432, 2560, 2688
B_ID, B_ONES, B_TRIU, B_NEG, B_NTRIU, B_END = 0, 128, 256, 384, 512, 640


def host_consts():
    p = np.arange(128)
    cf = np.zeros((128, C_END), np.float32)
    cf[:, C_ID:C_ID + 128] = np.eye(128)
    cf[:, C_TRIU:C_TRIU + 128] = (p[:, None] <= p[None, :])
    cf[:, C_ONES:C_ONES + 128] = 1.0
    for h, g in enumerate(GAMMAS):
        lg = np.log(np.float64(g))
        cf[:, C_GQ + h * 128:C_GQ + (h + 1) * 128] = np.exp(lg * (p + 1.0))[:, None]
        cf[:, C_GK + h * 128:C_GK + (h + 1) * 128] = (np.exp(lg * (127.0 - p)) * 128.0 ** -0.5)[:, None]
        cf[:, C_RMASK + h * 128:C_RMASK + (h + 1) * 128] = np.where(p[None, :] >= p[:, None], np.exp(-lg * 128.0), 0.0)
    cf[:, C_NSTRICT:C_NSTRICT + 128] = np.where(p[None, :] > p[:, None], -1.0, 0.0)
    cf[:, C_D8:C_D8 + 128] = (p[:, None] // 8 == p[None, :] // 8)
    for col, b in ((C_M8, 8), (C_M16, 16), (C_M32, 32), (C_M64, 64)):
        cf[:, col:col + 128] = (p[:, None] // (2 * b) == p[None, :] // (2 * b)) & (p[:, None] // b != p[None, :] // b)
    cb = np.zeros((128, B_END), np.float32)
    cb[:, B_ID:B_ID + 128] = np.eye(128)
    cb[:, B_ONES:B_ONES + 128] = 1.0
    cb[:, B_TRIU:B_TRIU + 128] = (p[:, None] <= p[None, :])
    cb[:, B_NEG:B_NEG + 128] = np.where(p[None, :] < p[:, None], -30000.0, 0.0)
    cb[:, B_NTRIU:B_NTRIU + 128] = -1.0 * (p[:, None] <= p[None, :])
    cb = cb.astype(ml_dtypes.bfloat16)
    inv_freq = (1.0 / (np.float32(10000.0) ** (np.arange(0, 128, 2, dtype=np.float32) / np.float32(128)))).astype(np.float32)
    ang = (np.arange(T, dtype=np.float32)[:, None] * inv_freq[None, :]).astype(np.float32)
    cos = np.cos(ang).astype(np.float32)
    sin = np.sin(ang).astype(np.float32)
    rope = np.concatenate([cos, cos, -sin, sin], axis=1).astype(np.float32)
    return cf, cb, np.ascontiguousarray(rope)


class Arena:
    def __init__(self, nc, words, ap=None):
        self.t = nc.alloc_sbuf_tensor("arena", [128, words], F32) if ap is None else ap
        self.words = words
        self.off = 0

    def alloc(self, name, n, dt):
        w = n if dt == F32 else (n + 1) // 2
        w = (w + 7) // 8 * 8
        assert self.off + w <= self.words, ("SBUF arena overflow", name, self.off, w)
        ap = self.t[:, self.off:self.off + w]
        self.off += w
        if dt != F32:
            ap = ap.bitcast(dt)
        return Buf(name, ap[:, 0:n])


def vb(buf, lo, hi):
    ap = buf.t[:, :].bitcast(BF16)[:, lo:hi]
    return V(ap, buf, 0, 512)


def r3(v, h):
    return v.r("p (h e) -> p h e", h=h)


def build(nt=NTILE, stop_after=99, debug=False):
    nc = bass.Bass("TRN2", target_bir_lowering=False)

    def din(name, shape, dtype=F32):
        return nc.dram_tensor(name, shape, dtype, kind="ExternalInput").ap()

    x_d = din("x", [T, D])
    ccol_d = din("ccol", [128, 16])
    wada_d = din("wada", [D, 3 * D])
    badac_d = din("badac", [128, 32])
    badag_d = din("badag", [1, D])
    win_d = din("win", [D, IN_COLS])
    convw_d = din("convw", [128, 96])
    alog_d = din("alog", [1, 8])
    dtb_d = din("dtb", [1, 8])
    gnw_d = din("gnw", [1, RW])
    gnb_d = din("gnb", [1, RW])
    gdnw_d = din("gdnw", [1, GW])
    wout_d = din("wout", [D, D])
    lnw_d = din("lnw", [1, D])
    lnb_d = din("lnb", [1, D])
    cf_d = din("cf", [128, C_END])
    cb_d = din("cb", [128, B_END], BF16)
    rope_d = din("rope", [T, 256])
    out_d = nc.dram_tensor("out", [T, D], F32, kind="ExternalOutput").ap()
    skind = "ExternalOutput" if debug else "Internal"
    mT_d = nc.dram_tensor("mT", [NTILE, 128, D], BF16, kind=skind).ap()
    hT_d = nc.dram_tensor("hTs", [NTILE, 128, D], BF16, kind=skind).ap()

    P = Prog(nc)
    A = Arena(nc, 51968)
    ps = [Buf("ps%d" % i, nc.alloc_psum_tensor("ps%d" % i, [128, 512], F32), psum=True) for i in range(8)]

    cf = A.alloc("cf", C_END, F32)
    cb = A.alloc("cb", B_END, BF16)
    dma(P, cf.whole(), cf_d)
    dma(P, cb.whole(), cb_d)
    idf = cf.v(C_ID, C_ID + 128)
    idb = cb.v(B_ID, B_ID + 128)
    mhalf = A.alloc("mhalf", 8, F32)
    P.op("pool", lambda e: e.memset(mhalf.t[:, :], -0.5), outs=[mhalf.whole()])
    epsb = A.alloc("epsb", 1, F32)
    lqb = A.alloc("lqb", 1, F32)
    zb0 = A.alloc("zb0", 1, F32)
    oneb = A.alloc("oneb", 1, F32)
    P.op("pool", lambda e: e.memset(epsb.t[:, :], RMS_EPS), outs=[epsb.whole()])
    P.op("pool", lambda e: e.memset(lqb.t[:, :], float(np.log(128.0 ** -0.5))), outs=[lqb.whole()])
    P.op("pool", lambda e: e.memset(zb0.t[:, :], 0.0), outs=[zb0.whole()])
    P.op("pool", lambda e: e.memset(oneb.t[:, :], 1.0), outs=[oneb.whole()])
    modc = A.alloc("modc", 32, F32)
    sc1 = A.alloc("sc1", 16, F32)
    grow = A.alloc("grow", D, F32)
    R1 = A.alloc("R1", 16 * 3072, BF16)
    R2 = A.alloc("R2", 16 * 2056, BF16)
    base_mark = A.off
    R1f = R1.t.bitcast(F32)
    R2f = R2.t.bitcast(F32)

    wstg = [A.alloc("wstg%d" % i, 1024, F32) for i in range(3)]
    base_mark = A.off
    wl_cnt = [0]

    def load_w(dst, dst_stride, src_cols, dst_col0=0):
        pieces = []
        for (c0, n) in src_cols:
            o = 0
            while o < n:
                m = min(1024, n - o)
                pieces.append((c0 + o, m))
                o += m
        dc = dst_col0
        for (c0, m) in pieces:
            for k in range(16):
                i = wl_cnt[0]
                wl_cnt[0] += 1
                stg = wstg[i % 3]
                dma(P, stg.v(0, m), win_d[k * 128:(k + 1) * 128, c0:c0 + m])
                cp(P, ("pool", "dve", "act")[i % 3], dst.v(k * dst_stride + dc, k * dst_stride + dc + m), stg.v(0, m))
            dc += m

    load_w(R1, 3072, [(0, RET_COLS)])

    ccol = A.alloc("ccol", 16, F32)
    scol = A.alloc("scol", 16, F32)
    badac = A.alloc("badac", 32, F32)
    badag = A.alloc("badag", D, F32)
    dma(P, ccol.whole(), ccol_d)
    dma(P, badac.whole(), badac_d)
    dma(P, badag.v(0, D, 0, 1), badag_d)
    act(P, scol.whole(), ccol.whole(), AF.Silu)
    AR2 = Arena(None, R2f.shape[1], ap=R2f)
    wab = [AR2.alloc("wab%d" % i, 16 * 512, F32) for i in range(2)]
    for blk in range(12):
        wb = wab[blk % 2]
        src = wada_d[:, blk * 512:(blk + 1) * 512].rearrange("(k p) n -> p k n", p=128)
        fns = []
        for k0 in range(0, 16, 4):
            o_ap = wb.t[:, k0 * 512:(k0 + 4) * 512].rearrange("p (k n) -> p k n", k=4)
            i_ap = src[:, k0:k0 + 4, :]
            fns.append(lambda e, o_ap=o_ap, i_ap=i_ap: e.dma_start(out=o_ap, in_=i_ap))
        P.dma(fns, outs=[wb.whole()])
        if blk < 8:
            for jj in range(4):
                j = blk * 4 + jj
                for k in range(16):
                    mm(P, ps[0].v(j, j + 1), wb.v(k * 512 + jj * 128, k * 512 + (jj + 1) * 128), scol.v(k, k + 1),
                       start=(k == 0), stop=(k == 15))
        else:
            g = blk - 8
            bank = ps[1 + g % 2]
            for k in range(16):
                mm(P, bank.v(0, 512, 0, 1), scol.v(k, k + 1), wb.v(k * 512, (k + 1) * 512), start=(k == 0), stop=(k == 15))
            tt(P, "dve", grow.v(g * 512, (g + 1) * 512, 0, 1), bank.v(0, 512, 0, 1), badag.v(g * 512, (g + 1) * 512, 0, 1), ALU.add)
    tt(P, "dve", modc.whole(), ps[0].v(0, 32), badac.whole(), ALU.add)
    ts(P, "dve", sc1.whole(), modc.v(16, 32), 1.0, ALU.add)
    P.barrier()
    A.off = base_mark

    def gdn_pass(hg):
        A.off = base_mark
        AR1 = Arena(None, R1f.shape[1], ap=R1f)
        al = AR1.alloc
        W = R2
        WS = 2056
        o = GDN_OFF
        load_w(R2, WS, [(o + hg * 512, 512), (o + 1024 + hg * 512, 512), (o + 2048 + hg * 512, 512),
                        (o + 3072 + hg * 512, 512), (o + 4096 + hg * 4, 4), (o + 4104 + hg * 4, 4)])
        hT2 = [al("hT2_%d" % i, 16 * 256, BF16) for i in range(2)]
        qkvT = al("qkvT", 12 * 256, BF16)
        raw = [al("raw%d" % i, 264, F32) for i in range(2)]
        a1 = [al("a1_%d" % i, 256, F32) for i in range(2)]
        halo = al("halo", 48, F32)
        convw = al("convw", 96, F32)
        sgg = al("sgg", 1024, F32)
        gab = al("gab", 16, F32)
        sq = al("sq", 512, BF16)
        lnr = al("lnr", 512, F32)
        nega = al("nega", 4, F32)
        dtb = al("dtb", 4, F32)
        gdnw = al("gdnw", 512, F32)
        Sg32 = al("Sg32", 512, F32)
        Sgbf = al("Sgbf", 512, BF16)
        nkcdT = al("nkcdT", 512, BF16)
        vnew = al("vnew", 512, BF16)
        o1s = al("o1s", 512, F32)
        osq = al("osq", 512, F32)
        gyb = al("gyb", 512, BF16)
        mst2 = al("mst2", 512, BF16)
        ss8 = al("ss8", 4, F32)

        class TB:
            pass

        tbs = []
        for i in range(2):
            b = TB()
            for nm in ("z8", "sp8", "g8", "bt", "egc", "egl", "dgl", "etl", "bege"):
                setattr(b, nm, al("%s_%d" % (nm, i), 4, F32))
            b.e8 = al("e8_%d" % i, 8, F32)
            b.gc = al("gc_%d" % i, 8, F32)
            for nm in ("Ghi", "Glo", "dm", "ndms", "kb", "kbe", "ktl", "vbt", "kbT", "attnT", "Bd", "Ad", "TUb", "W1s",
                       "Q0", "Q1", "QT0", "QT1", "X0", "X1", "Ao0", "Ao1", "Ao2", "Ao3"):
                setattr(b, nm, al("%s_%d" % (nm, i), 512, BF16))
            b.pA, b.pB, b.pC, b.pD = ps[4 * i], ps[4 * i + 1], ps[4 * i + 2], ps[4 * i + 3]
            tbs.append(b)

        ones_bf = cb.v(B_ONES, B_ONES + 128)
        triu_bf = cb.v(B_TRIU, B_TRIU + 128)
        ntriu_bf = cb.v(B_NTRIU, B_NTRIU + 128)
        neg_bf = cb.v(B_NEG, B_NEG + 128)
        dma(P, convw.whole(), convw_d)
        dma(P, nega.whole(), alog_d[:, hg * 4:(hg + 1) * 4].partition_broadcast(128))
        dma(P, dtb.whole(), dtb_d[:, hg * 4:(hg + 1) * 4].partition_broadcast(128))
        dma(P, gdnw.whole(), gdnw_d[:, hg * 512:(hg + 1) * 512].partition_broadcast(128))
        act(P, nega.whole(), nega.whole(), AF.Exp)
        ts(P, "dve", nega.whole(), nega.whole(), -1.0, ALU.mult)
        P.op("pool", lambda e: e.memset(halo.t[:, :], 0.0), outs=[halo.whole()])
        P.op("pool", lambda e: e.memset(Sg32.t[:, :], 0.0), outs=[Sg32.whole()])
        P.op("pool", lambda e: e.memset(Sgbf.t[:, :], 0.0), outs=[Sgbf.whole()])

        def hd(buf, h):
            return buf.v(h * 128, (h + 1) * 128)

        def pq(bank, h):
            return bank.v(h * 128, (h + 1) * 128)

        mk = lambda col: bcast_mid(cf.v(col, col + 128), 4, 128)

        def twork(c, b):
            qT = lambda h: qkvT.v(h * 256 + c * 128, h * 256 + (c + 1) * 128)
            kT = lambda h: qkvT.v((4 + h) * 256 + c * 128, (4 + h) * 256 + (c + 1) * 128)
            vT = lambda h: qkvT.v((8 + h) * 256 + c * 128, (8 + h) * 256 + (c + 1) * 128)
            ga = gab.v(c * 8, c * 8 + 4)
            gb = gab.v(c * 8 + 4, c * 8 + 8)
            tt(P, "dve", b.z8.whole(), ga, dtb.whole(), ALU.add)
            act(P, b.e8.v(0, 4), b.z8.whole(), AF.Exp)
            act(P, b.e8.v(4, 8), gb, AF.Exp, scale=-1.0)
            yield
            act(P, b.sp8.whole(), b.e8.v(0, 4), AF.Ln, bias=oneb.whole())
            ts(P, "dve", b.bt.whole(), b.e8.v(4, 8), 1.0, ALU.add)
            yield
            tt(P, "dve", b.g8.whole(), b.sp8.whole(), nega.whole(), ALU.mult)
            P.op("dve", lambda e: e.reciprocal(out=b.bt.t[:, :], in_=b.bt.t[:, :]), outs=[b.bt.whole()], ins=[b.bt.whole()])
            yield
            mm(P, b.pB.v(0, 4), cf.v(C_TRIU, C_TRIU + 128), b.g8.whole())
            mm(P, b.pB.v(4, 8), cf.v(C_ONES, C_ONES + 128), b.g8.whole())
            gbc_ = bcast_last(b.g8.whole(), 4, 128)
            cp(P, "dve", r3(b.Ghi.whole(), 4), gbc_)
            for h in range(4):
                tr(P, vb(b.pA, h * 128, (h + 1) * 128), kT(h), idb)
            yield
            cp(P, "dve", b.gc.whole(), b.pB.v(0, 8))
            tt(P, "dve", r3(b.Glo.whole(), 4), gbc_, r3(b.Ghi.whole(), 4), ALU.subtract)
            yield
            act(P, b.egc.whole(), b.gc.v(0, 4), AF.Exp)
            act(P, b.egl.whole(), b.gc.v(4, 8), AF.Exp)
            tt(P, "dve", b.dgl.whole(), b.gc.v(4, 8), b.gc.v(0, 4), ALU.subtract)
            for h in range(4):
                Db = pq(b.pB, h)
                mm(P, Db, hd(b.Ghi, h), triu_bf, start=True, stop=False)
                mm(P, Db, hd(b.Glo, h), triu_bf, start=False, stop=False)
                mm(P, Db, ntriu_bf, hd(b.Ghi, h), start=False, stop=False)
                mm(P, Db, ntriu_bf, hd(b.Glo, h), start=False, stop=False)
                mm(P, Db, idb, neg_bf, start=False, stop=True)
            yield
            act(P, b.etl.whole(), b.dgl.whole(), AF.Exp)
            tt(P, "dve", b.bege.whole(), b.bt.whole(), b.egc.whole(), ALU.mult)
            act(P, b.dm.whole(), b.pB.whole(), AF.Exp)
            k3 = r3(vb(b.pA, 0, 512), 4)
            tt(P, "dve", r3(b.kb.whole(), 4), k3, bcast_last(b.bt.whole(), 4, 128), ALU.mult)
            yield
            tt(P, "dve", r3(b.kbe.whole(), 4), k3, bcast_last(b.bege.whole(), 4, 128), ALU.mult)
            tt(P, "dve", r3(b.ktl.whole(), 4), k3, bcast_last(b.etl.whole(), 4, 128), ALU.mult)
            for h in range(4):
                tr(P, vb(b.pC, h * 128, (h + 1) * 128), hd(b.kb, h), idb)
            yield
            cp(P, "act", b.kbT.whole(), vb(b.pC, 0, 512))
            tt(P, "dve", r3(b.ndms.whole(), 4), r3(b.dm.whole(), 4), mk(C_NSTRICT), ALU.mult)
            for h in range(4):
                tr(P, vb(b.pA, h * 128, (h + 1) * 128), vT(h), idb)
            yield
            tt(P, "dve", r3(b.vbt.whole(), 4), r3(vb(b.pA, 0, 512), 4), bcast_last(b.bt.whole(), 4, 128), ALU.mult)
            for h in range(4):
                mm(P, pq(b.pC, h), kT(h), hd(b.kbT, h))
            for h in range(4):
                mm(P, pq(b.pD, h), kT(h), qT(h))
            yield
            tt(P, "dve", b.Q0.whole(), b.pC.whole(), b.ndms.whole(), ALU.mult)
            tt(P, "dve", b.attnT.whole(), b.pD.whole(), b.dm.whole(), ALU.mult)
            yield
            for h in range(4):
                tr(P, vb(b.pA, h * 128, (h + 1) * 128), hd(b.Q0, h), idb)
            tt(P, "dve", r3(b.Bd.whole(), 4), r3(b.Q0.whole(), 4), mk(C_D8), ALU.mult)
            yield
            cp(P, "act", b.QT0.whole(), vb(b.pA, 0, 512))
            tt(P, "dve", r3(b.X0.whole(), 4), r3(b.Bd.whole(), 4), mk(C_ID), ALU.add)
            yield
            tt(P, "dve", r3(b.Ad.whole(), 4), r3(b.QT0.whole(), 4), mk(C_D8), ALU.mult)
            yield
            for h in range(4):
                mm(P, pq(b.pB, h), hd(b.Ad, h), hd(b.Bd, h))
            for h in range(4):
                mm(P, pq(b.pC, h), hd(b.Bd, h), hd(b.Ad, h))
            aos = (b.Ao0, b.Ao1, b.Ao2, b.Ao3)
            for li, col in enumerate((C_M8, C_M16)):
                tt(P, "dve", r3(aos[li].whole(), 4), r3(b.QT0.whole(), 4), mk(col), ALU.mult)
            yield
            cp(P, "act", b.Q1.whole(), b.pB.whole())
            cp(P, "dve", b.QT1.whole(), b.pC.whole())
            yield
            for h in range(4):
                mm(P, pq(b.pD, h), hd(b.QT1, h), hd(b.X0, h))
            for h in range(4):
                mm(P, pq(b.pC, h), hd(b.Q1, h), hd(b.QT1, h))
            for li, col in ((2, C_M32), (3, C_M64)):
                tt(P, "dve", r3(aos[li].whole(), 4), r3(b.QT0.whole(), 4), mk(col), ALU.mult)
            yield
            tt(P, "dve", b.X1.whole(), b.pD.whole(), b.X0.whole(), ALU.add)
            cp(P, "act", b.Ad.whole(), b.pC.whole())
            yield
            for h in range(4):
                mm(P, pq(b.pD, h), hd(b.Ad, h), hd(b.X1, h))
            yield
            tt(P, "dve", b.X0.whole(), b.pD.whole(), b.X1.whole(), ALU.add)
            yield
            xs_ = [b.X0, b.X1]
            cur = 0
            for li in range(4):
                U = xs_[cur]
                for h in range(4):
                    tr(P, vb(b.pA, h * 128, (h + 1) * 128), hd(U, h), idb)
                for h in range(4):
                    mm(P, pq(b.pB, h), hd(aos[li], h), hd(U, h))
                yield
                cp(P, "act", b.TUb.whole(), vb(b.pA, 0, 512))
                cp(P, "act", b.W1s.whole(), b.pB.whole())
                yield
                for h in range(4):
                    mm(P, pq(b.pD, h), hd(b.TUb, h), hd(b.W1s, h))
                yield
                tt(P, "dve", xs_[1 - cur].whole(), b.pD.whole(), U.whole(), ALU.add)
                yield
                cur = 1 - cur
            b.TT = xs_[cur]
            for h in range(4):
                mm(P, pq(b.pB, h), hd(b.kbe, h), hd(b.TT, h))
            yield

        def chain(c, b, t):
            qT = lambda h: qkvT.v(h * 256 + c * 128, h * 256 + (c + 1) * 128)
            TT = b.TT
            act(P, nkcdT.whole(), b.pB.whole(), AF.Identity, scale=-1.0)
            for h in range(4):
                pv = pq(b.pC, h)
                mm(P, pv, hd(TT, h), hd(b.vbt, h), start=True, stop=False)
                mm(P, pv, hd(nkcdT, h), hd(Sgbf, h), start=False, stop=True)
            cp(P, "act", vnew.whole(), b.pC.whole())
            for h in range(4):
                mm(P, pq(b.pA, h), qT(h), hd(Sgbf, h))
            for h in range(4):
                mm(P, pq(b.pD, h), hd(b.ktl, h), hd(vnew, h))
            for h in range(4):
                mm(P, pq(b.pB, h), hd(b.attnT, h), hd(vnew, h))
            for h in range(4):
                stt(P, hd(Sg32, h), hd(Sg32, h), b.egl.v(h, h + 1), pq(b.pD, h), ALU.mult, ALU.add)
            cp(P, "act", Sgbf.whole(), Sg32.whole())
            tt(P, "dve", r3(o1s.whole(), 4), r3(b.pA.whole(), 4), bcast_last(b.egc.whole(), 4, 128), ALU.mult)
            tt(P, "dve", o1s.whole(), b.pB.whole(), o1s.whole(), ALU.add)
            act(P, osq.whole(), o1s.whole(), AF.Square)
            P.op("dve", lambda e: e.tensor_reduce(out=ss8.t[:, :], in_=osq.t[:, :].rearrange("p (h e) -> p h e", h=4),
                                                  axis=AX.X, op=ALU.add), outs=[ss8.whole()], ins=[osq.whole()])
            ts(P, "pool", ss8.whole(), ss8.whole(), 1.0 / 128.0, ALU.mult, RMS_EPS, ALU.add)
            tt(P, "pool", ss8.whole(), ss8.whole(), mhalf.v(0, 4), ALU.pow)
            tt(P, "dve", osq.whole(), sgg.v(c * 512, (c + 1) * 512), gdnw.whole(), ALU.mult)
            tt(P, "dve", r3(o1s.whole(), 4), r3(o1s.whole(), 4), bcast_last(ss8.whole(), 4, 128), ALU.mult)
            tt(P, "dve", gyb.whole(), o1s.whole(), osq.whole(), ALU.mult)
            for h in range(4):
                tr(P, vb(b.pA, h * 128, (h + 1) * 128), hd(gyb, h), idb)
            cp(P, "act", mst2.whole(), vb(b.pA, 0, 512))
            dma(P, mT_d[t, :, 1024 + hg * 512:1024 + (hg + 1) * 512], mst2.whole())

        for s in range(nt // 2):
            H = hT2[s % 2]
            for c in range(2):
                t = 2 * s + c
                dst = V(H.t[:, :].rearrange("p (k w) -> p k w", k=16)[:, :, c * 128:(c + 1) * 128], H, 0, 16 * 256)
                dma(P, dst, hT_d[t].rearrange("p (k w) -> p k w", k=16))
            for c in range(2):
                bank = ps[2 + c]
                for k in range(16):
                    mm(P, bank.whole(), H.v(k * 256 + c * 128, k * 256 + (c + 1) * 128), W.v(k * WS + 1536, k * WS + 2048),
                       start=(k == 0), stop=(k == 15))
                act(P, sgg.v(c * 512, (c + 1) * 512), bank.whole(), AF.Silu)
            for c in range(2):
                for k in range(16):
                    mm(P, ps[1].v(c * 8, (c + 1) * 8), H.v(k * 256 + c * 128, k * 256 + (c + 1) * 128),
                       W.v(k * WS + 2048, k * WS + 2056), start=(k == 0), stop=(k == 15))
            cp(P, "dve", gab.whole(), ps[1].v(0, 16))
            for ch in range(12):
                bank = ps[4 + ch % 2]
                for k in range(16):
                    mm(P, bank.v(0, 256), W.v(k * WS + ch * 128, k * WS + (ch + 1) * 128), H.v(k * 256, (k + 1) * 256),
                       start=(k == 0), stop=(k == 15))
                gch = (ch // 4) * 8 + hg * 4 + ch % 4
                R = raw[ch % 2]
                b1 = a1[ch % 2]
                cw = lambda tap, gch=gch: convw.v(gch * 4 + tap, gch * 4 + tap + 1)
                cp(P, "pool", R.v(0, 3), halo.v(ch * 4, ch * 4 + 3))
                cp(P, "act", R.v(3, 259), bank.v(0, 256))
                cp(P, "pool", halo.v(ch * 4, ch * 4 + 3), R.v(256, 259))
                act(P, b1.whole(), R.v(0, 256), AF.Identity, scale=cw(0), bias=zb0.whole())
                stt(P, b1.whole(), R.v(1, 257), cw(1), b1.whole(), ALU.mult, ALU.add)
                stt(P, b1.whole(), R.v(2, 258), cw(2), b1.whole(), ALU.mult, ALU.add)
                stt(P, b1.whole(), R.v(3, 259), cw(3), b1.whole(), ALU.mult, ALU.add)
                act(P, qkvT.v(ch * 256, (ch + 1) * 256), b1.whole(), AF.Silu)
            for pr in range(4):
                reg = qkvT.v(pr * 512, (pr + 1) * 512)
                act(P, sq.whole(), reg, AF.Square)
                bank = ps[6 + pr % 2]
                for u in range(2):
                    mm(P, bank.v(u * 256, (u + 1) * 256), ones_bf, sq.v(u * 256, (u + 1) * 256))
                act(P, lnr.whole(), bank.whole(), AF.Ln, bias=epsb.whole())
                act(P, lnr.whole(), lnr.whole(), AF.Exp, scale=-0.5, bias=(lqb.whole() if pr < 2 else zb0.whole()))
                tt(P, "dve", reg, reg, lnr.whole(), ALU.mult)
            gens = [twork(0, tbs[0]), twork(1, tbs[1])]
            while gens:
                for g in list(gens):
                    try:
                        next(g)
                    except StopIteration:
                        gens.remove(g)
            for c in range(2):
                chain(c, tbs[c], 2 * s + c)
        P.barrier()

    def pass3():
        A.off = base_mark
        AR1 = Arena(None, R1f.shape[1], ap=R1f)
        AR2 = Arena(None, R2f.shape[1], ap=R2f)
        Wo = AR1.alloc("Wo", 16 * D, BF16)
        gbc = AR2.alloc("gbc", D, F32)
        wst = [AR2.alloc("wst%d" % i, D, F32) for i in range(2)]
        xs3 = [AR2.alloc("x3_%d" % i, D, F32) for i in range(2)]
        mts = [AR2.alloc("mts%d" % i, D, BF16) for i in range(2)]
        zb = [AR2.alloc("zb%d" % i, D, F32) for i in range(2)]
        lnw = AR1.alloc("lnw", D, F32)
        lnb = AR1.alloc("lnb", D, F32)
        st3 = A.alloc("st3", 24, F32)
        mv3 = A.alloc("mv3", 2, F32)
        rs3 = A.alloc("rs3", 1, F32)
        nm3 = A.alloc("nm3", 1, F32)
        dma(P, lnw.whole(), lnw_d.partition_broadcast(128))
        dma(P, lnb.whole(), lnb_d.partition_broadcast(128))
        for g in range(4):
            bank = ps[g % 2]
            mm(P, bank.whole(), cf.v(C_ONES, C_ONES + 128, 0, 1), grow.v(g * 512, (g + 1) * 512, 0, 1))
            cp(P, "act", gbc.v(g * 512, (g + 1) * 512), bank.whole())
        for k in range(16):
            dma(P, wst[k % 2].whole(), wout_d[k * 128:(k + 1) * 128, :])
            tt(P, "dve" if k % 2 else "pool", Wo.v(k * D, (k + 1) * D), wst[k % 2].whole(), gbc.whole(), ALU.mult)
        for t in range(nt):
            sl = t % 2
            dma(P, xs3[sl].whole(), x_d[t * 128:(t + 1) * 128, :])
            dma(P, mts[sl].whole(), mT_d[t])
            z = zb[sl]
            for n in range(4):
                bank = ps[(t % 2) * 4 + n]
                for k in range(16):
                    mm(P, bank.whole(), mts[sl].v(k * 128, (k + 1) * 128), Wo.v(k * D + n * 512, k * D + (n + 1) * 512),
                       start=(k == 0), stop=(k == 15))
                stt(P, z.v(n * 512, (n + 1) * 512), xs3[sl].v(n * 512, (n + 1) * 512), float(ALPHA), bank.whole(), ALU.mult, ALU.add)
                P.op("dve", lambda e, n=n, z=z: e.bn_stats(out=st3.t[:, n * 6:(n + 1) * 6], in_=z.t[:, n * 512:(n + 1) * 512]),
                     outs=[st3.v(n * 6, (n + 1) * 6)], ins=[z.v(n * 512, (n + 1) * 512)])
            P.op("dve", lambda e: e.bn_aggr(out=mv3.t[:, :], in_=st3.t[:, :]), outs=[mv3.whole()], ins=[st3.whole()])
            ts(P, "pool", rs3.whole(), mv3.v(1, 2), LN_EPS, ALU.add)
            tt(P, "pool", rs3.whole(), rs3.whole(), mhalf.v(0, 1), ALU.pow)
            tt(P, "pool", nm3.whole(), mv3.v(0, 1), rs3.whole(), ALU.mult)
            ts(P, "pool", nm3.whole(), nm3.whole(), -1.0, ALU.mult)
            act(P, z.whole(), z.whole(), AF.Identity, scale=rs3.whole(), bias=nm3.whole())
            tt(P, "dve", z.whole(), z.whole(), lnw.whole(), ALU.mult)
            tt(P, "dve", z.whole(), z.whole(), lnb.whole(), ALU.add)
            dma(P, out_d[t * 128:(t + 1) * 128, :], z.whole())

    AR2 = Arena(None, R2f.shape[1], ap=R2f)
    xs = [AR2.alloc("xs%d" % i, D, F32) for i in range(2)]
    hT1 = [AR2.alloc("hT1_%d" % i, D, BF16) for i in range(2)]
    rp = [AR2.alloc("rp%d" % i, 256, F32) for i in range(2)]
    qs = AR2.alloc("qs", 512, F32)
    ks = AR2.alloc("ks", 512, F32)
    tA = AR2.alloc("tA", 512, F32)
    tB = AR2.alloc("tB", 512, F32)
    qhat = AR2.alloc("qhat", 1024, BF16)
    qkT = AR2.alloc("qkT", 1024, BF16)
    vbf = AR2.alloc("vbf", 1024, BF16)
    sg = AR2.alloc("sg", 1024, F32)
    sT = AR2.alloc("sT", 512, BF16)
    S32 = AR2.alloc("S32", 1024, F32)
    Sbf = AR2.alloc("Sbf", 1024, BF16)
    yn = AR2.alloc("yn", 1024, F32)
    retb = AR2.alloc("retb", 1024, BF16)
    mst = AR2.alloc("mst", 1024, BF16)
    gnw = A.alloc("gnw", RW, F32)
    gnb = A.alloc("gnb", RW, F32)
    st1 = A.alloc("st1", 24, F32)
    mv1 = A.alloc("mv1", 8, F32)
    rs1 = A.alloc("rs1", 4, F32)
    nm1 = A.alloc("nm1", 4, F32)
    KS = int(os.environ.get("KSUB", "0"))
    if not KS & 1:
        dma(P, gnw.whole(), gnw_d.partition_broadcast(128))
        dma(P, gnb.whole(), gnb_d.partition_broadcast(128))
    if not KS & 2:
        P.op("pool", lambda e: e.memset(S32.t[:, :], 0.0), outs=[S32.whole()])
        P.op("pool", lambda e: e.memset(Sbf.t[:, :], 0.0), outs=[Sbf.whole()])

    def make_hT(xbuf, hT):
        for g4 in range(4):
            bank = ps[g4 % 2]
            for kk in range(4):
                k = g4 * 4 + kk
                tr(P, bank.v(kk * 128, (kk + 1) * 128), xbuf.v(k * 128, (k + 1) * 128), idf)
            for kk in range(4):
                k = g4 * 4 + kk
                dst = hT.v(k * 128, (k + 1) * 128)
                src = bank.v(kk * 128, (kk + 1) * 128)
                KHT = int(os.environ.get("KHT", "0"))
                if (g4 % 2 == 0 and KHT == 0) or KHT == 1:
                    act(P, dst, src, AF.Identity, scale=sc1.v(k, k + 1), bias=modc.v(k, k + 1))
                else:
                    ts(P, "dve", dst, src, sc1.v(k, k + 1), ALU.mult, modc.v(k, k + 1), ALU.add)

    def rotary(src, dst, rpb):
        a = src.whole().ap
        rot = src.whole().with_ap(bass.AP(tensor=a.tensor, offset=a.offset + 64,
                                          ap=[list(a.ap[0]), [128, 4], [-64, 2], [1, 64]]))
        c2 = bcast_mid(rpb.v(0, 128), 4, 128)
        s = rpb.v(128, 256).ap
        s2 = rpb.v(128, 256).with_ap(bass.AP(tensor=s.tensor, offset=s.offset, ap=[list(s.ap[0]), [0, 4], [64, 2], [1, 64]]))
        tt(P, "dve", r3(tA.whole(), 4), r3(src.whole(), 4), c2, ALU.mult)
        tt(P, "pool", tB.whole().r("p (h t d) -> p h t d", h=4, t=2), rot, s2, ALU.mult)
        tt(P, "dve", dst, tA.whole(), tB.whole(), ALU.add)

    for t in range(nt if stop_after >= 1 else 0):
        sl = t % 2
        dma(P, xs[sl].whole(), x_d[t * 128:(t + 1) * 128, :])
        if not KS & 16:
            dma(P, rp[sl].whole(), rope_d[t * 128:(t + 1) * 128, :])
        H = hT1[sl]
        if not KS & 8:
            make_hT(xs[sl], H)
        if not KS & 4:
            dma(P, hT_d[t], H.whole())
        KL = int(os.environ.get("KLVL", "9"))
        if KL < 1:
            continue
        for n in range(6):
            bank = ps[2 + n % 2]
            for k in range(16):
                mm(P, bank.whole(), H.v(k * 128, (k + 1) * 128), R1.v(k * 3072 + n * 512, k * 3072 + (n + 1) * 512),
                   start=(k == 0), stop=(k == 15))
            if n == 0:
                tt(P, "dve", qs.whole(), bank.whole(), cf.v(C_GQ, C_GQ + 512), ALU.mult)
                rotary(qs, qhat.v(0, 512), rp[sl])
            elif n == 1:
                tt(P, "dve", ks.whole(), bank.whole(), cf.v(C_GK, C_GK + 512), ALU.mult)
                rotary(ks, qhat.v(512, 1024), rp[sl])
            elif n < 4:
                cp(P, "act", vbf.v((n - 2) * 512, (n - 1) * 512), bank.whole())
            else:
                act(P, sg.v((n - 4) * 512, (n - 3) * 512), bank.whole(), AF.Silu)
        if KL < 2:
            continue
        for m in range(8):
            tr(P, vb(ps[4], m * 128, (m + 1) * 128), qhat.v(m * 128, (m + 1) * 128), idb)
        cp(P, "dve", qkT.whole(), vb(ps[4], 0, 1024))
        if KL < 3:
            continue
        for h in range(4):
            mm(P, ps[5].v(h * 128, (h + 1) * 128), qkT.v(512 + h * 128, 512 + (h + 1) * 128), qkT.v(h * 128, (h + 1) * 128))
        tt(P, "dve", sT.whole(), ps[5].whole(), cf.v(C_RMASK, C_RMASK + 512), ALU.mult)
        obs = []
        for h in range(4):
            ob = ps[6 + h // 2].v((h % 2) * 256, (h % 2 + 1) * 256)
            obs.append(ob)
            mm(P, ob, sT.v(h * 128, (h + 1) * 128), vbf.v(h * 256, (h + 1) * 256), start=True, stop=False)
            mm(P, ob, qkT.v(h * 128, (h + 1) * 128), Sbf.v(h * 256, (h + 1) * 256), start=False, stop=True)
        ubs = []
        for h in range(4):
            ub = ps[h // 2].v((h % 2) * 256, (h % 2 + 1) * 256)
            ubs.append(ub)
            mm(P, ub, qhat.v(512 + h * 128, 512 + (h + 1) * 128), vbf.v(h * 256, (h + 1) * 256))
        if KL < 4:
            continue
        for h in range(4):
            P.op("dve", lambda e, h=h: e.bn_stats(out=st1.t[:, h * 6:(h + 1) * 6], in_=obs[h].ap),
                 outs=[st1.v(h * 6, (h + 1) * 6)], ins=[obs[h]])
        for h in range(4):
            P.op("dve", lambda e, h=h: e.bn_aggr(out=mv1.t[:, h * 2:(h + 1) * 2], in_=st1.t[:, h * 6:(h + 1) * 6]),
                 outs=[mv1.v(h * 2, (h + 1) * 2)], ins=[st1.v(h * 6, (h + 1) * 6)])
        if KL < 5:
            continue
        mvv = mv1.whole().r("p (h two) -> p h two", two=2)
        ts(P, "pool", rs1.whole(), mvv[:, :, 1], GN_EPS, ALU.add)
        tt(P, "pool", rs1.whole(), rs1.whole(), mhalf.v(0, 4), ALU.pow)
        tt(P, "pool", nm1.whole(), mvv[:, :, 0], rs1.whole(), ALU.mult)
        ts(P, "pool", nm1.whole(), nm1.whole(), -1.0, ALU.mult)
        if KL < 6:
            continue
        for h in range(4):
            act(P, yn.v(h * 256, (h + 1) * 256), obs[h], AF.Identity, scale=rs1.v(h, h + 1), bias=nm1.v(h, h + 1))
        for h in range(4):
            stt(P, S32.v(h * 256, (h + 1) * 256), S32.v(h * 256, (h + 1) * 256), float(GAMMAS[h] ** 128), ubs[h], ALU.mult, ALU.add)
        cp(P, "pool", Sbf.whole(), S32.whole())
        tt(P, "pool", yn.whole(), yn.whole(), gnw.whole(), ALU.mult)
        tt(P, "dve", yn.whole(), yn.whole(), gnb.whole(), ALU.add)
        tt(P, "pool", retb.whole(), yn.whole(), sg.whole(), ALU.mult)
        for m in range(8):
            tr(P, vb(ps[4], m * 128, (m + 1) * 128), retb.v(m * 128, (m + 1) * 128), idb)
        cp(P, "act", mst.whole(), vb(ps[4], 0, 1024))
        dma(P, mT_d[t, :, 0:1024], mst.whole())
    P.barrier()
    A.off = base_mark
    if stop_after >= 2:
        for hg in range(2):
            gdn_pass(hg)
    if stop_after >= 3:
        pass3()
    return nc, P, A, ps, locals()


def finish(nc, P):
    P.final_wait()
    P.finalize()
    sems = {te: nc.alloc_semaphore("s_" + te) for te in P.teops}
    P.emit(sems)
    return nc


_CONSTS = None


def prep_core_inputs(inp, b):
    global _CONSTS
    if _CONSTS is None:
        _CONSTS = host_consts()
    cf, cbt, rope = _CONSTS
    f = lambda a: np.ascontiguousarray(np.asarray(a, dtype=np.float32))
    b_ada = np.asarray(inp["b_ada"])[0]
    conv = np.asarray(inp["gdn_conv_w"])[0]
    return {
        "x": f(np.asarray(inp["x"])[b]),
        "ccol": f(np.asarray(inp["c"])[b].reshape(16, 128).T),
        "wada": f(np.asarray(inp["w_ada"])[0]),
        "badac": f(b_ada[:4096].reshape(32, 128).T),
        "badag": f(b_ada[4096:].reshape(1, D)),
        "win": f(np.asarray(inp["w_in"])[0]),
        "convw": f(conv.reshape(4, 24, 128).transpose(2, 1, 0).reshape(128, 96)),
        "alog": f(np.asarray(inp["gdn_a_log"])[0].reshape(1, 8)),
        "dtb": f(np.asarray(inp["gdn_dt_bias"])[0].reshape(1, 8)),
        "gnw": f(np.asarray(inp["ret_gn_w"])[0].reshape(1, RW)),
        "gnb": f(np.asarray(inp["ret_gn_b"])[0].reshape(1, RW)),
        "gdnw": f(np.tile(np.asarray(inp["gdn_norm_w"])[0], 8).reshape(1, GW)),
        "wout": f(np.asarray(inp["w_out"])[0]),
        "lnw": f(np.asarray(inp["ln_w"])[0].reshape(1, D)),
        "lnb": f(np.asarray(inp["ln_b"])[0].reshape(1, D)),
        "cf": cf, "cb": cbt, "rope": rope,
    }


def kernel(**inputs):
    nc, P, A, ps, _ = build(nt=NTILE, stop_after=3, debug=False)
    finish(nc, P)
    in_maps = [prep_core_inputs(inputs, b) for b in range(8)]
    res = run_bass_kernel_spmd(nc, in_maps, core_ids=list(range(8)))
    out = np.stack([np.asarray(r["out"], dtype=np.float32) for r in res.results], axis=0)
    return out.astype(np.asarray(inputs["x"]).dtype)
```

```python
import os
import numpy as np
import ml_dtypes
import concourse.bass as bass
import concourse.mybir as mybir
from concourse.bass_utils import run_bass_kernel_spmd

F32 = mybir.dt.float32
BF16 = mybir.dt.bfloat16
AF = mybir.ActivationFunctionType
ALU = mybir.AluOpType
AX = mybir.AxisListType


class Buf:
    def __init__(self, name, t, psum=False):
        self.name = name
        self.t = t
        self.psum = psum
        self.reads = {}
        self.writes = {}

    def v(self, lo, hi, p0=0, p1=None):
        if p1 is None:
            p1 = self.t.shape[0]
        if self.psum:
            return V(self.t[p0:p1, lo:hi], self, 0, 512)
        return V(self.t[p0:p1, lo:hi], self, lo, hi)

    def whole(self):
        return self.v(0, self.t.shape[1])


class V:
    def __init__(self, ap, buf, lo, hi):
        self.ap = ap
        self.buf = buf
        self.lo = lo
        self.hi = hi

    def r(self, pattern, **kw):
        return V(self.ap.rearrange(pattern, **kw), self.buf, self.lo, self.hi)

    def __getitem__(self, idx):
        return V(self.ap[idx], self.buf, self.lo, self.hi)

    def with_ap(self, ap):
        return V(ap, self.buf, self.lo, self.hi)


class Op:
    __slots__ = ("q", "te", "seq", "deps", "fn", "waits", "signal", "ndma")


class Prog:
    QUEUES = ("pe", "act", "dve", "pool", "sp")

    def __init__(self, nc, n_lanes=12):
        self.nc = nc
        self.qops = {q: [] for q in self.QUEUES}
        self.teops = {q: [] for q in self.QUEUES}
        self.lanes = ["L%d" % i for i in range(n_lanes)]
        for l in self.lanes:
            self.teops[l] = []
        self.next_lane = 0

    def _track(self, op, outs, ins):
        deps = {}

        def need(te, seq):
            if te == op.te:
                return
            if deps.get(te, -1) < seq:
                deps[te] = seq

        raw_same = -1
        for v in ins:
            for (te, lo, hi), seq in v.buf.writes.items():
                if lo < v.hi and v.lo < hi:
                    if te == op.te:
                        raw_same = max(raw_same, seq)
                    else:
                        need(te, seq)
            if v.buf.psum:
                for (te, lo, hi), seq in v.buf.reads.items():
                    need(te, seq)
        for v in outs:
            for d in (v.buf.writes, v.buf.reads):
                for (te, lo, hi), seq in d.items():
                    if lo < v.hi and v.lo < hi:
                        if te == op.te:
                            raw_same = max(raw_same, seq)
                        else:
                            need(te, seq)
        if raw_same >= 0 and op.te in ("act", "dve", "pool"):
            deps[op.te] = raw_same
        op.deps = deps
        for v in outs:
            for d in (v.buf.writes, v.buf.reads):
                for k in [k for k in d if v.lo <= k[1] and k[2] <= v.hi]:
                    del d[k]
            v.buf.writes[(op.te, v.lo, v.hi)] = op.seq
        for v in ins:
            v.buf.reads[(op.te, v.lo, v.hi)] = op.seq

    def op(self, q, fn, outs=(), ins=()):
        o = Op()
        o.q = q
        o.te = q
        o.seq = len(self.teops[q])
        o.fn = fn
        o.ndma = 0
        self._track(o, outs, ins)
        self.teops[q].append(o)
        self.qops[q].append(o)
        return o

    def dma(self, fns, outs=(), ins=(), q="sp"):
        lane = self.lanes[self.next_lane]
        self.next_lane = (self.next_lane + 1) % len(self.lanes)
        o = Op()
        o.q = q
        o.te = lane
        o.seq = len(self.teops[lane])
        o.fn = fns
        o.ndma = len(fns)
        self._track(o, outs, ins)
        if o.seq > 0:
            o.deps[lane] = o.seq - 1
        self.teops[lane].append(o)
        self.qops[q].append(o)
        return o

    def final_wait(self):
        o = Op()
        o.q = "sp"
        o.te = "sp"
        o.seq = len(self.teops["sp"])
        o.fn = None
        o.ndma = 0
        o.deps = {te: len(ops) - 1 for te, ops in self.teops.items() if te.startswith("L") and ops}
        self.teops["sp"].append(o)
        self.qops["sp"].append(o)

    def finalize(self):
        for q in self.QUEUES:
            seen = {}
            for o in self.qops[q]:
                w = []
                for te, seq in o.deps.items():
                    if seen.get(te, -1) < seq:
                        seen[te] = seq
                        w.append((te, seq))
                o.waits = w
        for te, ops in self.teops.items():
            for o in ops:
                o.signal = te.startswith("L")
        for q in self.QUEUES:
            for o in self.qops[q]:
                for te, seq in o.waits:
                    self.teops[te][seq].signal = True
        self.val = {}
        for te, ops in self.teops.items():
            c = 0
            vals = []
            for o in ops:
                if te.startswith("L"):
                    c += 16 * o.ndma
                elif o.signal:
                    c += 1
                vals.append(c)
            self.val[te] = vals

    def emit(self, sems):
        nc = self.nc
        with nc.Block() as block:

            def mk(q):
                def body(eng):
                    for o in self.qops[q]:
                        for te, seq in o.waits:
                            eng.wait_ge(sems[te], self.val[te][seq])
                        if o.ndma:
                            for f in o.fn:
                                f(eng).then_inc(sems[o.te], 16)
                        elif o.fn is not None:
                            ins = o.fn(eng)
                            if o.signal:
                                ins.then_inc(sems[o.te], 1)

                return body

            block.tensor(mk("pe"))
            block.scalar(mk("act"))
            block.vector(mk("dve"))
            block.gpsimd(mk("pool"))
            block.sync(mk("sp"))

    def barrier(self):
        last = {}
        for te, ops in self.teops.items():
            i = len(ops) - 1
            while i >= 0 and ops[i].fn is None:
                i -= 1
            if i >= 0:
                last[te] = i
        for q in self.QUEUES:
            o = Op()
            o.q = q
            o.te = q
            o.seq = len(self.teops[q])
            o.fn = None
            o.ndma = 0
            o.deps = {te: s for te, s in last.items() if te != q}
            self.teops[q].append(o)
            self.qops[q].append(o)


def _ap(x):
    return x.ap if isinstance(x, V) else x


def _vs(*xs):
    return [x for x in xs if isinstance(x, V)]


def bcast_mid(v, n, inner):
    a = v.ap
    return v.with_ap(bass.AP(tensor=a.tensor, offset=a.offset, ap=[list(a.ap[0]), [0, n], [1, inner]]))


def bcast_last(v, n, inner):
    a = v.ap
    return v.with_ap(bass.AP(tensor=a.tensor, offset=a.offset, ap=[list(a.ap[0]), [a.ap[1][0], n], [0, inner]]))


def mm(P, out, lhsT, rhs, start=True, stop=True):
    return P.op("pe", lambda e: e.matmul(out=out.ap, lhsT=lhsT.ap, rhs=rhs.ap, start=start, stop=stop),
                outs=[out], ins=[lhsT, rhs])


def tr(P, out, in_, ident):
    return P.op("pe", lambda e: e.transpose(out=out.ap, in_=in_.ap, identity=ident.ap), outs=[out], ins=[in_, ident])


def act(P, out, in_, func, scale=1.0, bias=0.0):
    return P.op("act", lambda e: e.activation(out=out.ap, in_=in_.ap, func=func, scale=_ap(scale), bias=_ap(bias)),
                outs=[out], ins=[in_] + _vs(scale, bias))


def tt(P, q, out, a, b, op):
    return P.op(q, lambda e: e.tensor_tensor(out=out.ap, in0=a.ap, in1=b.ap, op=op), outs=[out], ins=[a, b])


def ts(P, q, out, a, s1, op0, s2=None, op1=None):
    if op1 is None:
        return P.op(q, lambda e: e.tensor_scalar(out=out.ap, in0=a.ap, scalar1=_ap(s1), scalar2=None, op0=op0),
                    outs=[out], ins=[a] + _vs(s1))
    return P.op(q, lambda e: e.tensor_scalar(out=out.ap, in0=a.ap, scalar1=_ap(s1), scalar2=_ap(s2), op0=op0, op1=op1),
                outs=[out], ins=[a] + _vs(s1, s2))


def stt(P, out, a, s, b, op0, op1):
    return P.op("dve", lambda e: e.scalar_tensor_tensor(out=out.ap, in0=a.ap, scalar=_ap(s), in1=b.ap, op0=op0, op1=op1),
                outs=[out], ins=[a, b] + _vs(s))


def cp(P, q, out, in_):
    if q == "act":
        return P.op("act", lambda e: e.copy(out=out.ap, in_=in_.ap), outs=[out], ins=[in_])
    return P.op(q, lambda e: e.tensor_copy(out=out.ap, in_=in_.ap), outs=[out], ins=[in_])


def dma(P, out, in_, q="sp", outs=None, ins=None, **kw):
    o = out.ap if isinstance(out, V) else out
    i = in_.ap if isinstance(in_, V) else in_
    return P.dma([lambda e: e.dma_start(out=o, in_=i, **kw)], outs=_vs(out) if outs is None else outs,
                 ins=_vs(in_) if ins is None else ins, q=q)


T = 4096
D = 2048
NTILE = T // 128
RQW = 512
RW = 1024
GW = 1024
IN_COLS = 7184
RET_COLS = 3072
GDN_OFF = 3072
GDN_COLS = 4112
GN_EPS = 1e-5
RMS_EPS = 1e-6
LN_EPS = 1e-5
ALPHA = 2.0 ** 0.25
GAMMAS = [1.0 - 2.0 ** (-5.0 - h) for h in range(4)]

C_ID, C_TRIU, C_ONES, C_GQ, C_GK, C_RMASK, C_NSTRICT, C_D8, C_M8, C_M16, C_M32, C_M64, C_END = 0, 128, 256, 384, 896, 1408, 1920, 2048, 2176, 2304, 2432, 2560, 2688
B_ID, B_ONES, B_TRIU, B_NEG, B_NTRIU, B_END = 0, 128, 256, 384, 512, 640


def host_consts():
    p = np.arange(128)
    cf = np.zeros((128, C_END), np.float32)
    cf[:, C_ID:C_ID + 128] = np.eye(128)
    cf[:, C_TRIU:C_TRIU + 128] = (p[:, None] <= p[None, :])
    cf[:, C_ONES:C_ONES + 128] = 1.0
    for h, g in enumerate(GAMMAS):
        lg = np.log(np.float64(g))
        cf[:, C_GQ + h * 128:C_GQ + (h + 1) * 128] = np.exp(lg * (p + 1.0))[:, None]
        cf[:, C_GK + h * 128:C_GK + (h + 1) * 128] = (np.exp(lg * (127.0 - p)) * 128.0 ** -0.5)[:, None]
        cf[:, C_RMASK + h * 128:C_RMASK + (h + 1) * 128] = np.where(p[None, :] >= p[:, None], np.exp(-lg * 128.0), 0.0)
    cf[:, C_NSTRICT:C_NSTRICT + 128] = np.where(p[None, :] > p[:, None], -1.0, 0.0)
    cf[:, C_D8:C_D8 + 128] = (p[:, None] // 8 == p[None, :] // 8)
    for col, b in ((C_M8, 8), (C_M16, 16), (C_M32, 32), (C_M64, 64)):
        cf[:, col:col + 128] = (p[:, None] // (2 * b) == p[None, :] // (2 * b)) & (p[:, None] // b != p[None, :] // b)
    cb = np.zeros((128, B_END), np.float32)
    cb[:, B_ID:B_ID + 128] = np.eye(128)
    cb[:, B_ONES:B_ONES + 128] = 1.0
    cb[:, B_TRIU:B_TRIU + 128] = (p[:, None] <= p[None, :])
    cb[:, B_NEG:B_NEG + 128] = np.where(p[None, :] < p[:, None], -30000.0, 0.0)
    cb[:, B_NTRIU:B_NTRIU + 128] = -1.0 * (p[:, None] <= p[None, :])
    cb = cb.astype(ml_dtypes.bfloat16)
    inv_freq = (1.0 / (np.float32(10000.0) ** (np.arange(0, 128, 2, dtype=np.float32) / np.float32(128)))).astype(np.float32)
    ang = (np.arange(T, dtype=np.float32)[:, None] * inv_freq[None, :]).astype(np.float32)
    cos = np.cos(ang).astype(np.float32)
    sin = np.sin(ang).astype(np.float32)
    rope = np.concatenate([cos, cos, -sin, sin], axis=1).astype(np.float32)
    return cf, cb, np.ascontiguousarray(rope)


class Arena:
    def __init__(self, nc, words, ap=None):
        self.t = nc.alloc_sbuf_tensor("arena", [128, words], F32) if ap is None else ap
        self.words = words
        self.off = 0

    def alloc(self, name, n, dt):
        w = n if dt == F32 else (n + 1) // 2
        w = (w + 7) // 8 * 8
        assert self.off + w <= self.words, ("SBUF arena overflow", name, self.off, w)
        ap = self.t[:, self.off:self.off + w]
        self.off += w
        if dt != F32:
            ap = ap.bitcast(dt)
        return Buf(name, ap[:, 0:n])


def vb(buf, lo, hi):
    ap = buf.t[:, :].bitcast(BF16)[:, lo:hi]
    return V(ap, buf, 0, 512)


def r3(v, h):
    return v.r("p (h e) -> p h e", h=h)


def build(nt=NTILE, stop_after=99, debug=False):
    nc = bass.Bass("TRN2", target_bir_lowering=False)

    def din(name, shape, dtype=F32):
        return nc.dram_tensor(name, shape, dtype, kind="ExternalInput").ap()

    x_d = din("x", [T, D])
    ccol_d = din("ccol", [128, 16])
    wada_d = din("wada", [D, 3 * D])
    badac_d = din("badac", [128, 32])
    badag_d = din("badag", [1, D])
    win_d = din("win", [D, IN_COLS])
    convw_d = din("convw", [128, 96])
    alog_d = din("alog", [1, 8])
    dtb_d = din("dtb", [1, 8])
    gnw_d = din("gnw", [1, RW])
    gnb_d = din("gnb", [1, RW])
    gdnw_d = din("gdnw", [1, GW])
    wout_d = din("wout", [D, D])
    lnw_d = din("lnw", [1, D])
    lnb_d = din("lnb", [1, D])
    cf_d = din("cf", [128, C_END])
    cb_d = din("cb", [128, B_END], BF16)
    rope_d = din("rope", [T, 256])
    out_d = nc.dram_tensor("out", [T, D], F32, kind="ExternalOutput").ap()
    skind = "ExternalOutput" if debug else "Internal"
    mT_d = nc.dram_tensor("mT", [NTILE, 128, D], BF16, kind=skind).ap()
    hT_d = nc.dram_tensor("hTs", [NTILE, 128, D], BF16, kind=skind).ap()

    P = Prog(nc)
    A = Arena(nc, 51968)
    ps = [Buf("ps%d" % i, nc.alloc_psum_tensor("ps%d" % i, [128, 512], F32), psum=True) for i in range(8)]

    cf = A.alloc("cf", C_END, F32)
    cb = A.alloc("cb", B_END, BF16)
    dma(P, cf.whole(), cf_d)
    dma(P, cb.whole(), cb_d)
    idf = cf.v(C_ID, C_ID + 128)
    idb = cb.v(B_ID, B_ID + 128)
    mhalf = A.alloc("mhalf", 8, F32)
    P.op("pool", lambda e: e.memset(mhalf.t[:, :], -0.5), outs=[mhalf.whole()])
    epsb = A.alloc("epsb", 1, F32)
    lqb = A.alloc("lqb", 1, F32)
    zb0 = A.alloc("zb0", 1, F32)
    oneb = A.alloc("oneb", 1, F32)
    P.op("pool", lambda e: e.memset(epsb.t[:, :], RMS_EPS), outs=[epsb.whole()])
    P.op("pool", lambda e: e.memset(lqb.t[:, :], float(np.log(128.0 ** -0.5))), outs=[lqb.whole()])
    P.op("pool", lambda e: e.memset(zb0.t[:, :], 0.0), outs=[zb0.whole()])
    P.op("pool", lambda e: e.memset(oneb.t[:, :], 1.0), outs=[oneb.whole()])
    modc = A.alloc("modc", 32, F32)
    sc1 = A.alloc("sc1", 16, F32)
    grow = A.alloc("grow", D, F32)
    R1 = A.alloc("R1", 16 * 3072, BF16)
    R2 = A.alloc("R2", 16 * 2056, BF16)
    base_mark = A.off
    R1f = R1.t.bitcast(F32)
    R2f = R2.t.bitcast(F32)

    wstg = [A.alloc("wstg%d" % i, 1024, F32) for i in range(3)]
    base_mark = A.off
    wl_cnt = [0]

    def load_w(dst, dst_stride, src_cols, dst_col0=0):
        pieces = []
        for (c0, n) in src_cols:
            o = 0
            while o < n:
                m = min(1024, n - o)
                pieces.append((c0 + o, m))
                o += m
        dc = dst_col0
        for (c0, m) in pieces:
            for k in range(16):
                i = wl_cnt[0]
                wl_cnt[0] += 1
                stg = wstg[i % 3]
                dma(P, stg.v(0, m), win_d[k * 128:(k + 1) * 128, c0:c0 + m])
                cp(P, ("pool", "dve", "act")[i % 3], dst.v(k * dst_stride + dc, k * dst_stride + dc + m), stg.v(0, m))
            dc += m

    load_w(R1, 3072, [(0, RET_COLS)])

    ccol = A.alloc("ccol", 16, F32)
    scol = A.alloc("scol", 16, F32)
    badac = A.alloc("badac", 32, F32)
    badag = A.alloc("badag", D, F32)
    dma(P, ccol.whole(), ccol_d)
    dma(P, badac.whole(), badac_d)
    dma(P, badag.v(0, D, 0, 1), badag_d)
    act(P, scol.whole(), ccol.whole(), AF.Silu)
    AR2 = Arena(None, R2f.shape[1], ap=R2f)
    wab = [AR2.alloc("wab%d" % i, 16 * 512, F32) for i in range(2)]
    for blk in range(12):
        wb = wab[blk % 2]
        src = wada_d[:, blk * 512:(blk + 1) * 512].rearrange("(k p) n -> p k n", p=128)
        fns = []
        for k0 in range(0, 16, 4):
            o_ap = wb.t[:, k0 * 512:(k0 + 4) * 512].rearrange("p (k n) -> p k n", k=4)
            i_ap = src[:, k0:k0 + 4, :]
            fns.append(lambda e, o_ap=o_ap, i_ap=i_ap: e.dma_start(out=o_ap, in_=i_ap))
        P.dma(fns, outs=[wb.whole()])
        if blk < 8:
            for jj in range(4):
                j = blk * 4 + jj
                for k in range(16):
                    mm(P, ps[0].v(j, j + 1), wb.v(k * 512 + jj * 128, k * 512 + (jj + 1) * 128), scol.v(k, k + 1),
                       start=(k == 0), stop=(k == 15))
        else:
            g = blk - 8
            bank = ps[1 + g % 2]
            for k in range(16):
                mm(P, bank.v(0, 512, 0, 1), scol.v(k, k + 1), wb.v(k * 512, (k + 1) * 512), start=(k == 0), stop=(k == 15))
            tt(P, "dve", grow.v(g * 512, (g + 1) * 512, 0, 1), bank.v(0, 512, 0, 1), badag.v(g * 512, (g + 1) * 512, 0, 1), ALU.add)
    tt(P, "dve", modc.whole(), ps[0].v(0, 32), badac.whole(), ALU.add)
    ts(P, "dve", sc1.whole(), modc.v(16, 32), 1.0, ALU.add)
    P.barrier()
    A.off = base_mark

    def gdn_pass(hg):
        A.off = base_mark
        AR1 = Arena(None, R1f.shape[1], ap=R1f)
        al = AR1.alloc
        W = R2
        WS = 2056
        o = GDN_OFF
        load_w(R2, WS, [(o + hg * 512, 512), (o + 1024 + hg * 512, 512), (o + 2048 + hg * 512, 512),
                        (o + 3072 + hg * 512, 512), (o + 4096 + hg * 4, 4), (o + 4104 + hg * 4, 4)])
        hT2 = [al("hT2_%d" % i, 16 * 256, BF16) for i in range(2)]
        qkvT = al("qkvT", 12 * 256, BF16)
        raw = [al("raw%d" % i, 264, F32) for i in range(2)]
        a1 = [al("a1_%d" % i, 256, F32) for i in range(2)]
        halo = al("halo", 48, F32)
        convw = al("convw", 96, F32)
        sgg = al("sgg", 1024, F32)
        gab = al("gab", 16, F32)
        sq = al("sq", 512, BF16)
        lnr = al("lnr", 512, F32)
        nega = al("nega", 4, F32)
        dtb = al("dtb", 4, F32)
        gdnw = al("gdnw", 512, F32)
        Sg32 = al("Sg32", 512, F32)
        Sgbf = al("Sgbf", 512, BF16)
        nkcdT = al("nkcdT", 512, BF16)
        vnew = al("vnew", 512, BF16)
        o1s = al("o1s", 512, F32)
        osq = al("osq", 512, F32)
        gyb = al("gyb", 512, BF16)
        mst2 = al("mst2", 512, BF16)
        ss8 = al("ss8", 4, F32)

        class TB:
            pass

        tbs = []
        for i in range(2):
            b = TB()
            for nm in ("z8", "sp8", "g8", "bt", "egc", "egl", "dgl", "etl", "bege"):
                setattr(b, nm, al("%s_%d" % (nm, i), 4, F32))
            b.e8 = al("e8_%d" % i, 8, F32)
            b.gc = al("gc_%d" % i, 8, F32)
            for nm in ("Ghi", "Glo", "dm", "ndms", "kb", "kbe", "ktl", "vbt", "kbT", "attnT", "Bd", "Ad", "TUb", "W1s",
                       "Q0", "Q1", "QT0", "QT1", "X0", "X1", "Ao0", "Ao1", "Ao2", "Ao3"):
                setattr(b, nm, al("%s_%d" % (nm, i), 512, BF16))
            b.pA, b.pB, b.pC, b.pD = ps[4 * i], ps[4 * i + 1], ps[4 * i + 2], ps[4 * i + 3]
            tbs.append(b)

        ones_bf = cb.v(B_ONES, B_ONES + 128)
        triu_bf = cb.v(B_TRIU, B_TRIU + 128)
        ntriu_bf = cb.v(B_NTRIU, B_NTRIU + 128)
        neg_bf = cb.v(B_NEG, B_NEG + 128)
        dma(P, convw.whole(), convw_d)
        dma(P, nega.whole(), alog_d[:, hg * 4:(hg + 1) * 4].partition_broadcast(128))
        dma(P, dtb.whole(), dtb_d[:, hg * 4:(hg + 1) * 4].partition_broadcast(128))
        dma(P, gdnw.whole(), gdnw_d[:, hg * 512:(hg + 1) * 512].partition_broadcast(128))
        act(P, nega.whole(), nega.whole(), AF.Exp)
        ts(P, "dve", nega.whole(), nega.whole(), -1.0, ALU.mult)
        P.op("pool", lambda e: e.memset(halo.t[:, :], 0.0), outs=[halo.whole()])
        P.op("pool", lambda e: e.memset(Sg32.t[:, :], 0.0), outs=[Sg32.whole()])
        P.op("pool", lambda e: e.memset(Sgbf.t[:, :], 0.0), outs=[Sgbf.whole()])

        def hd(buf, h):
            return buf.v(h * 128, (h + 1) * 128)

        def pq(bank, h):
            return bank.v(h * 128, (h + 1) * 128)

        mk = lambda col: bcast_mid(cf.v(col, col + 128), 4, 128)

        def twork(c, b):
            qT = lambda h: qkvT.v(h * 256 + c * 128, h * 256 + (c + 1) * 128)
            kT = lambda h: qkvT.v((4 + h) * 256 + c * 128, (4 + h) * 256 + (c + 1) * 128)
            vT = lambda h: qkvT.v((8 + h) * 256 + c * 128, (8 + h) * 256 + (c + 1) * 128)
            ga = gab.v(c * 8, c * 8 + 4)
            gb = gab.v(c * 8 + 4, c * 8 + 8)
            tt(P, "dve", b.z8.whole(), ga, dtb.whole(), ALU.add)
            act(P, b.e8.v(0, 4), b.z8.whole(), AF.Exp)
            act(P, b.e8.v(4, 8), gb, AF.Exp, scale=-1.0)
            yield
            act(P, b.sp8.whole(), b.e8.v(0, 4), AF.Ln, bias=oneb.whole())
            ts(P, "dve", b.bt.whole(), b.e8.v(4, 8), 1.0, ALU.add)
            yield
            tt(P, "dve", b.g8.whole(), b.sp8.whole(), nega.whole(), ALU.mult)
            P.op("dve", lambda e: e.reciprocal(out=b.bt.t[:, :], in_=b.bt.t[:, :]), outs=[b.bt.whole()], ins=[b.bt.whole()])
            yield
            mm(P, b.pB.v(0, 4), cf.v(C_TRIU, C_TRIU + 128), b.g8.whole())
            mm(P, b.pB.v(4, 8), cf.v(C_ONES, C_ONES + 128), b.g8.whole())
            gbc_ = bcast_last(b.g8.whole(), 4, 128)
            cp(P, "dve", r3(b.Ghi.whole(), 4), gbc_)
            for h in range(4):
                tr(P, vb(b.pA, h * 128, (h + 1) * 128), kT(h), idb)
            yield
            cp(P, "dve", b.gc.whole(), b.pB.v(0, 8))
            tt(P, "dve", r3(b.Glo.whole(), 4), gbc_, r3(b.Ghi.whole(), 4), ALU.subtract)
            yield
            act(P, b.egc.whole(), b.gc.v(0, 4), AF.Exp)
            act(P, b.egl.whole(), b.gc.v(4, 8), AF.Exp)
            tt(P, "dve", b.dgl.whole(), b.gc.v(4, 8), b.gc.v(0, 4), ALU.subtract)
            for h in range(4):
                Db = pq(b.pB, h)
                mm(P, Db, hd(b.Ghi, h), triu_bf, start=True, stop=False)
                mm(P, Db, hd(b.Glo, h), triu_bf, start=False, stop=False)
                mm(P, Db, ntriu_bf, hd(b.Ghi, h), start=False, stop=False)
                mm(P, Db, ntriu_bf, hd(b.Glo, h), start=False, stop=False)
                mm(P, Db, idb, neg_bf, start=False, stop=True)
            yield
            act(P, b.etl.whole(), b.dgl.whole(), AF.Exp)
            tt(P, "dve", b.bege.whole(), b.bt.whole(), b.egc.whole(), ALU.mult)
            act(P, b.dm.whole(), b.pB.whole(), AF.Exp)
            k3 = r3(vb(b.pA, 0, 512), 4)
            tt(P, "dve", r3(b.kb.whole(), 4), k3, bcast_last(b.bt.whole(), 4, 128), ALU.mult)
            yield
            tt(P, "dve", r3(b.kbe.whole(), 4), k3, bcast_last(b.bege.whole(), 4, 128), ALU.mult)
            tt(P, "dve", r3(b.ktl.whole(), 4), k3, bcast_last(b.etl.whole(), 4, 128), ALU.mult)
            for h in range(4):
                tr(P, vb(b.pC, h * 128, (h + 1) * 128), hd(b.kb, h), idb)
            yield
            cp(P, "act", b.kbT.whole(), vb(b.pC, 0, 512))
            tt(P, "dve", r3(b.ndms.whole(), 4), r3(b.dm.whole(), 4), mk(C_NSTRICT), ALU.mult)
            for h in range(4):
                tr(P, vb(b.pA, h * 128, (h + 1) * 128), vT(h), idb)
            yield
            tt(P, "dve", r3(b.vbt.whole(), 4), r3(vb(b.pA, 0, 512), 4), bcast_last(b.bt.whole(), 4, 128), ALU.mult)
            for h in range(4):
                mm(P, pq(b.pC, h), kT(h), hd(b.kbT, h))
            for h in range(4):
                mm(P, pq(b.pD, h), kT(h), qT(h))
            yield
            tt(P, "dve", b.Q0.whole(), b.pC.whole(), b.ndms.whole(), ALU.mult)
            tt(P, "dve", b.attnT.whole(), b.pD.whole(), b.dm.whole(), ALU.mult)
            yield
            for h in range(4):
                tr(P, vb(b.pA, h * 128, (h + 1) * 128), hd(b.Q0, h), idb)
            tt(P, "dve", r3(b.Bd.whole(), 4), r3(b.Q0.whole(), 4), mk(C_D8), ALU.mult)
            yield
            cp(P, "act", b.QT0.whole(), vb(b.pA, 0, 512))
            tt(P, "dve", r3(b.X0.whole(), 4), r3(b.Bd.whole(), 4), mk(C_ID), ALU.add)
            yield
            tt(P, "dve", r3(b.Ad.whole(), 4), r3(b.QT0.whole(), 4), mk(C_D8), ALU.mult)
            yield
            for h in range(4):
                mm(P, pq(b.pB, h), hd(b.Ad, h), hd(b.Bd, h))
            for h in range(4):
                mm(P, pq(b.pC, h), hd(b.Bd, h), hd(b.Ad, h))
            aos = (b.Ao0, b.Ao1, b.Ao2, b.Ao3)
            for li, col in enumerate((C_M8, C_M16)):
                tt(P, "dve", r3(aos[li].whole(), 4), r3(b.QT0.whole(), 4), mk(col), ALU.mult)
            yield
            cp(P, "act", b.Q1.whole(), b.pB.whole())
            cp(P, "dve", b.QT1.whole(), b.pC.whole())
            yield
            for h in range(4):
                mm(P, pq(b.pD, h), hd(b.QT1, h), hd(b.X0, h))
            for h in range(4):
                mm(P, pq(b.pC, h), hd(b.Q1, h), hd(b.QT1, h))
            for li, col in ((2, C_M32), (3, C_M64)):
                tt(P, "dve", r3(aos[li].whole(), 4), r3(b.QT0.whole(), 4), mk(col), ALU.mult)
            yield
            tt(P, "dve", b.X1.whole(), b.pD.whole(), b.X0.whole(), ALU.add)
            cp(P, "act", b.Ad.whole(), b.pC.whole())
            yield
            for h in range(4):
                mm(P, pq(b.pD, h), hd(b.Ad, h), hd(b.X1, h))
            yield
            tt(P, "dve", b.X0.whole(), b.pD.whole(), b.X1.whole(), ALU.add)
            yield
            xs_ = [b.X0, b.X1]
            cur = 0
            for li in range(4):
                U = xs_[cur]
                for h in range(4):
                    tr(P, vb(b.pA, h * 128, (h + 1) * 128), hd(U, h), idb)
                for h in range(4):
                    mm(P, pq(b.pB, h), hd(aos[li], h), hd(U, h))
                yield
                cp(P, "act", b.TUb.whole(), vb(b.pA, 0, 512))
                cp(P, "act", b.W1s.whole(), b.pB.whole())
                yield
                for h in range(4):
                    mm(P, pq(b.pD, h), hd(b.TUb, h), hd(b.W1s, h))
                yield
                tt(P, "dve", xs_[1 - cur].whole(), b.pD.whole(), U.whole(), ALU.add)
                yield
                cur = 1 - cur
            b.TT = xs_[cur]
            for h in range(4):
                mm(P, pq(b.pB, h), hd(b.kbe, h), hd(b.TT, h))
            yield

        def chain(c, b, t):
            qT = lambda h: qkvT.v(h * 256 + c * 128, h * 256 + (c + 1) * 128)
            TT = b.TT
            act(P, nkcdT.whole(), b.pB.whole(), AF.Identity, scale=-1.0)
            for h in range(4):
                pv = pq(b.pC, h)
                mm(P, pv, hd(TT, h), hd(b.vbt, h), start=True, stop=False)
                mm(P, pv, hd(nkcdT, h), hd(Sgbf, h), start=False, stop=True)
            cp(P, "act", vnew.whole(), b.pC.whole())
            for h in range(4):
                mm(P, pq(b.pA, h), qT(h), hd(Sgbf, h))
            for h in range(4):
                mm(P, pq(b.pD, h), hd(b.ktl, h), hd(vnew, h))
            for h in range(4):
                mm(P, pq(b.pB, h), hd(b.attnT, h), hd(vnew, h))
            for h in range(4):
                stt(P, hd(Sg32, h), hd(Sg32, h), b.egl.v(h, h + 1), pq(b.pD, h), ALU.mult, ALU.add)
            cp(P, "act", Sgbf.whole(), Sg32.whole())
            tt(P, "dve", r3(o1s.whole(), 4), r3(b.pA.whole(), 4), bcast_last(b.egc.whole(), 4, 128), ALU.mult)
            tt(P, "dve", o1s.whole(), b.pB.whole(), o1s.whole(), ALU.add)
            act(P, osq.whole(), o1s.whole(), AF.Square)
            P.op("dve", lambda e: e.tensor_reduce(out=ss8.t[:, :], in_=osq.t[:, :].rearrange("p (h e) -> p h e", h=4),
                                                  axis=AX.X, op=ALU.add), outs=[ss8.whole()], ins=[osq.whole()])
            ts(P, "pool", ss8.whole(), ss8.whole(), 1.0 / 128.0, ALU.mult, RMS_EPS, ALU.add)
            tt(P, "pool", ss8.whole(), ss8.whole(), mhalf.v(0, 4), ALU.pow)
            tt(P, "dve", osq.whole(), sgg.v(c * 512, (c + 1) * 512), gdnw.whole(), ALU.mult)
            tt(P, "dve", r3(o1s.whole(), 4), r3(o1s.whole(), 4), bcast_last(ss8.whole(), 4, 128), ALU.mult)
            tt(P, "dve", gyb.whole(), o1s.whole(), osq.whole(), ALU.mult)
            for h in range(4):
                tr(P, vb(b.pA, h * 128, (h + 1) * 128), hd(gyb, h), idb)
            cp(P, "act", mst2.whole(), vb(b.pA, 0, 512))
            dma(P, mT_d[t, :, 1024 + hg * 512:1024 + (hg + 1) * 512], mst2.whole())

        def p2_loads(s):
            H = hT2[s % 2]
            for c in range(2):
                t = 2 * s + c
                dst = V(H.t[:, :].rearrange("p (k w) -> p k w", k=16)[:, :, c * 128:(c + 1) * 128], H, 0, 16 * 256)
                dma(P, dst, hT_d[t].rearrange("p (k w) -> p k w", k=16))

        p2_loads(0)
        for s in range(nt // 2):
            H = hT2[s % 2]
            if s + 1 < nt // 2:
                p2_loads(s + 1)
            for c in range(2):
                bank = ps[2 + c]
                for k in range(16):
                    mm(P, bank.whole(), H.v(k * 256 + c * 128, k * 256 + (c + 1) * 128), W.v(k * WS + 1536, k * WS + 2048),
                       start=(k == 0), stop=(k == 15))
                act(P, sgg.v(c * 512, (c + 1) * 512), bank.whole(), AF.Silu)
            for c in range(2):
                for k in range(16):
                    mm(P, ps[1].v(c * 8, (c + 1) * 8), H.v(k * 256 + c * 128, k * 256 + (c + 1) * 128),
                       W.v(k * WS + 2048, k * WS + 2056), start=(k == 0), stop=(k == 15))
            cp(P, "dve", gab.whole(), ps[1].v(0, 16))
            for ch in range(12):
                bank = ps[4 + ch % 2]
                for k in range(16):
                    mm(P, bank.v(0, 256), W.v(k * WS + ch * 128, k * WS + (ch + 1) * 128), H.v(k * 256, (k + 1) * 256),
                       start=(k == 0), stop=(k == 15))
                gch = (ch // 4) * 8 + hg * 4 + ch % 4
                R = raw[ch % 2]
                b1 = a1[ch % 2]
                cw = lambda tap, gch=gch: convw.v(gch * 4 + tap, gch * 4 + tap + 1)
                cp(P, "pool", R.v(0, 3), halo.v(ch * 4, ch * 4 + 3))
                cp(P, "act", R.v(3, 259), bank.v(0, 256))
                cp(P, "pool", halo.v(ch * 4, ch * 4 + 3), R.v(256, 259))
                act(P, b1.whole(), R.v(0, 256), AF.Identity, scale=cw(0), bias=zb0.whole())
                stt(P, b1.whole(), R.v(1, 257), cw(1), b1.whole(), ALU.mult, ALU.add)
                stt(P, b1.whole(), R.v(2, 258), cw(2), b1.whole(), ALU.mult, ALU.add)
                stt(P, b1.whole(), R.v(3, 259), cw(3), b1.whole(), ALU.mult, ALU.add)
                act(P, qkvT.v(ch * 256, (ch + 1) * 256), b1.whole(), AF.Silu)
            for pr in range(4):
                reg = qkvT.v(pr * 512, (pr + 1) * 512)
                act(P, sq.whole(), reg, AF.Square)
                bank = ps[6 + pr % 2]
                for u in range(2):
                    mm(P, bank.v(u * 256, (u + 1) * 256), ones_bf, sq.v(u * 256, (u + 1) * 256))
                act(P, lnr.whole(), bank.whole(), AF.Ln, bias=epsb.whole())
                act(P, lnr.whole(), lnr.whole(), AF.Exp, scale=-0.5, bias=(lqb.whole() if pr < 2 else zb0.whole()))
                tt(P, "dve", reg, reg, lnr.whole(), ALU.mult)
            gens = [twork(0, tbs[0]), twork(1, tbs[1])]
            while gens:
                for g in list(gens):
                    try:
                        next(g)
                    except StopIteration:
                        gens.remove(g)
            for c in range(2):
                chain(c, tbs[c], 2 * s + c)
        P.barrier()

    def pass3():
        A.off = base_mark
        AR1 = Arena(None, R1f.shape[1], ap=R1f)
        AR2 = Arena(None, R2f.shape[1], ap=R2f)
        Wo = AR1.alloc("Wo", 16 * D, BF16)
        gbc = AR2.alloc("gbc", D, F32)
        wst = [AR2.alloc("wst%d" % i, D, F32) for i in range(2)]
        xs3 = [AR2.alloc("x3_%d" % i, D, F32) for i in range(2)]
        mts = [AR2.alloc("mts%d" % i, D, BF16) for i in range(2)]
        zb = [AR2.alloc("zb%d" % i, D, F32) for i in range(2)]
        lnw = AR1.alloc("lnw", D, F32)
        lnb = AR1.alloc("lnb", D, F32)
        st3 = A.alloc("st3", 24, F32)
        mv3 = A.alloc("mv3", 2, F32)
        rs3 = A.alloc("rs3", 1, F32)
        nm3 = A.alloc("nm3", 1, F32)
        dma(P, lnw.whole(), lnw_d.partition_broadcast(128))
        dma(P, lnb.whole(), lnb_d.partition_broadcast(128))
        for g in range(4):
            bank = ps[g % 2]
            mm(P, bank.whole(), cf.v(C_ONES, C_ONES + 128, 0, 1), grow.v(g * 512, (g + 1) * 512, 0, 1))
            cp(P, "act", gbc.v(g * 512, (g + 1) * 512), bank.whole())
        for k in range(16):
            dma(P, wst[k % 2].whole(), wout_d[k * 128:(k + 1) * 128, :])
            tt(P, "dve" if k % 2 else "pool", Wo.v(k * D, (k + 1) * D), wst[k % 2].whole(), gbc.whole(), ALU.mult)
        def p3_loads(t):
            dma(P, xs3[t % 2].whole(), x_d[t * 128:(t + 1) * 128, :])
            dma(P, mts[t % 2].whole(), mT_d[t])

        p3_loads(0)
        for t in range(nt):
            sl = t % 2
            if t + 1 < nt:
                p3_loads(t + 1)
            z = zb[sl]
            for n in range(4):
                bank = ps[(t % 2) * 4 + n]
                for k in range(16):
                    mm(P, bank.whole(), mts[sl].v(k * 128, (k + 1) * 128), Wo.v(k * D + n * 512, k * D + (n + 1) * 512),
                       start=(k == 0), stop=(k == 15))
                stt(P, z.v(n * 512, (n + 1) * 512), xs3[sl].v(n * 512, (n + 1) * 512), float(ALPHA), bank.whole(), ALU.mult, ALU.add)
                P.op("dve", lambda e, n=n, z=z: e.bn_stats(out=st3.t[:, n * 6:(n + 1) * 6], in_=z.t[:, n * 512:(n + 1) * 512]),
                     outs=[st3.v(n * 6, (n + 1) * 6)], ins=[z.v(n * 512, (n + 1) * 512)])
            P.op("dve", lambda e: e.bn_aggr(out=mv3.t[:, :], in_=st3.t[:, :]), outs=[mv3.whole()], ins=[st3.whole()])
            ts(P, "pool", rs3.whole(), mv3.v(1, 2), LN_EPS, ALU.add)
            tt(P, "pool", rs3.whole(), rs3.whole(), mhalf.v(0, 1), ALU.pow)
            tt(P, "pool", nm3.whole(), mv3.v(0, 1), rs3.whole(), ALU.mult)
            ts(P, "pool", nm3.whole(), nm3.whole(), -1.0, ALU.mult)
            act(P, z.whole(), z.whole(), AF.Identity, scale=rs3.whole(), bias=nm3.whole())
            tt(P, "dve", z.whole(), z.whole(), lnw.whole(), ALU.mult)
            tt(P, "dve", z.whole(), z.whole(), lnb.whole(), ALU.add)
            dma(P, out_d[t * 128:(t + 1) * 128, :], z.whole())

    AR2 = Arena(None, R2f.shape[1], ap=R2f)
    xs = [AR2.alloc("xs%d" % i, D, F32) for i in range(2)]
    hT1 = [AR2.alloc("hT1_%d" % i, D, BF16) for i in range(2)]
    rp = [AR2.alloc("rp%d" % i, 256, F32) for i in range(2)]
    qs = AR2.alloc("qs", 512, F32)
    ks = AR2.alloc("ks", 512, F32)
    tA = AR2.alloc("tA", 512, F32)
    tB = AR2.alloc("tB", 512, F32)
    qhat = AR2.alloc("qhat", 1024, BF16)
    qkT = AR2.alloc("qkT", 1024, BF16)
    vbf = AR2.alloc("vbf", 1024, BF16)
    sg = AR2.alloc("sg", 1024, F32)
    sT = AR2.alloc("sT", 512, BF16)
    S32 = AR2.alloc("S32", 1024, F32)
    Sbf = AR2.alloc("Sbf", 1024, BF16)
    yn = AR2.alloc("yn", 1024, F32)
    retb = AR2.alloc("retb", 1024, BF16)
    mst = AR2.alloc("mst", 1024, BF16)
    gnw = A.alloc("gnw", RW, F32)
    gnb = A.alloc("gnb", RW, F32)
    st1 = A.alloc("st1", 24, F32)
    mv1 = A.alloc("mv1", 8, F32)
    rs1 = A.alloc("rs1", 4, F32)
    nm1 = A.alloc("nm1", 4, F32)
    KS = int(os.environ.get("KSUB", "0"))
    if not KS & 1:
        dma(P, gnw.whole(), gnw_d.partition_broadcast(128))
        dma(P, gnb.whole(), gnb_d.partition_broadcast(128))
    if not KS & 2:
        P.op("pool", lambda e: e.memset(S32.t[:, :], 0.0), outs=[S32.whole()])
        P.op("pool", lambda e: e.memset(Sbf.t[:, :], 0.0), outs=[Sbf.whole()])

    def make_hT(xbuf, hT):
        for g4 in range(4):
            bank = ps[g4 % 2]
            for kk in range(4):
                k = g4 * 4 + kk
                tr(P, bank.v(kk * 128, (kk + 1) * 128), xbuf.v(k * 128, (k + 1) * 128), idf)
            for kk in range(4):
                k = g4 * 4 + kk
                dst = hT.v(k * 128, (k + 1) * 128)
                src = bank.v(kk * 128, (kk + 1) * 128)
                KHT = int(os.environ.get("KHT", "0"))
                if (g4 % 2 == 0 and KHT == 0) or KHT == 1:
                    act(P, dst, src, AF.Identity, scale=sc1.v(k, k + 1), bias=modc.v(k, k + 1))
                else:
                    ts(P, "dve", dst, src, sc1.v(k, k + 1), ALU.mult, modc.v(k, k + 1), ALU.add)

    def rotary(src, dst, rpb):
        a = src.whole().ap
        rot = src.whole().with_ap(bass.AP(tensor=a.tensor, offset=a.offset + 64,
                                          ap=[list(a.ap[0]), [128, 4], [-64, 2], [1, 64]]))
        c2 = bcast_mid(rpb.v(0, 128), 4, 128)
        s = rpb.v(128, 256).ap
        s2 = rpb.v(128, 256).with_ap(bass.AP(tensor=s.tensor, offset=s.offset, ap=[list(s.ap[0]), [0, 4], [64, 2], [1, 64]]))
        tt(P, "dve", r3(tA.whole(), 4), r3(src.whole(), 4), c2, ALU.mult)
        tt(P, "pool", tB.whole().r("p (h t d) -> p h t d", h=4, t=2), rot, s2, ALU.mult)
        tt(P, "dve", dst, tA.whole(), tB.whole(), ALU.add)

    def p1_loads(t):
        dma(P, xs[t % 2].whole(), x_d[t * 128:(t + 1) * 128, :])
        dma(P, rp[t % 2].whole(), rope_d[t * 128:(t + 1) * 128, :])

    qhat2 = Buf("qhat2", wstg[0].t[:, 0:512].bitcast(BF16))
    vbf2 = Buf("vbf2", wstg[0].t[:, 512:1024].bitcast(BF16))
    sg2 = Buf("sg2", wstg[1].t[:, 0:1024])
    qhats, vbfs, sgs = [qhat, qhat2], [vbf, vbf2], [sg, sg2]

    def p1_front(t):
        sl = t % 2
        H = hT1[sl]
        qh, vv, sgb = qhats[sl], vbfs[sl], sgs[sl]
        for g4 in range(4):
            bank = ps[g4 % 2]
            for kk in range(4):
                k = g4 * 4 + kk
                tr(P, bank.v(kk * 128, (kk + 1) * 128), xs[sl].v(k * 128, (k + 1) * 128), idf)
            for kk in range(4):
                k = g4 * 4 + kk
                dst = H.v(k * 128, (k + 1) * 128)
                src = bank.v(kk * 128, (kk + 1) * 128)
                if g4 % 2 == 0:
                    act(P, dst, src, AF.Identity, scale=sc1.v(k, k + 1), bias=modc.v(k, k + 1))
                else:
                    ts(P, "dve", dst, src, sc1.v(k, k + 1), ALU.mult, modc.v(k, k + 1), ALU.add)
            yield
        dma(P, hT_d[t], H.whole())
        for n in range(6):
            bank = ps[2 + n % 2]
            for k in range(16):
                mm(P, bank.whole(), H.v(k * 128, (k + 1) * 128), R1.v(k * 3072 + n * 512, k * 3072 + (n + 1) * 512),
                   start=(k == 0), stop=(k == 15))
            if n == 0:
                tt(P, "dve", qs.whole(), bank.whole(), cf.v(C_GQ, C_GQ + 512), ALU.mult)
                yield
                rotary(qs, qh.v(0, 512), rp[sl])
            elif n == 1:
                tt(P, "dve", ks.whole(), bank.whole(), cf.v(C_GK, C_GK + 512), ALU.mult)
                yield
                rotary(ks, qh.v(512, 1024), rp[sl])
            elif n < 4:
                cp(P, "act", vv.v((n - 2) * 512, (n - 1) * 512), bank.whole())
            else:
                act(P, sgb.v((n - 4) * 512, (n - 3) * 512), bank.whole(), AF.Silu)
            yield

    def p1_back(t):
        sl = t % 2
        qh, vv, sgb = qhats[sl], vbfs[sl], sgs[sl]
        for m in range(8):
            tr(P, vb(ps[4], m * 128, (m + 1) * 128), qh.v(m * 128, (m + 1) * 128), idb)
        yield
        cp(P, "dve", qkT.whole(), vb(ps[4], 0, 1024))
        yield
        for h in range(4):
            mm(P, ps[5].v(h * 128, (h + 1) * 128), qkT.v(512 + h * 128, 512 + (h + 1) * 128), qkT.v(h * 128, (h + 1) * 128))
        yield
        tt(P, "dve", sT.whole(), ps[5].whole(), cf.v(C_RMASK, C_RMASK + 512), ALU.mult)
        yield
        obs = []
        for h in range(4):
            ob = ps[6 + h // 2].v((h % 2) * 256, (h % 2 + 1) * 256)
            obs.append(ob)
            mm(P, ob, sT.v(h * 128, (h + 1) * 128), vv.v(h * 256, (h + 1) * 256), start=True, stop=False)
            mm(P, ob, qkT.v(h * 128, (h + 1) * 128), Sbf.v(h * 256, (h + 1) * 256), start=False, stop=True)
        ubs = []
        for h in range(4):
            ub = ps[5 - h // 2].v((h % 2) * 256, (h % 2 + 1) * 256)
            ubs.append(ub)
            mm(P, ub, qh.v(512 + h * 128, 512 + (h + 1) * 128), vv.v(h * 256, (h + 1) * 256))
        yield
        for h in range(4):
            P.op("dve", lambda e, h=h: e.bn_stats(out=st1.t[:, h * 6:(h + 1) * 6], in_=obs[h].ap),
                 outs=[st1.v(h * 6, (h + 1) * 6)], ins=[obs[h]])
        for h in range(4):
            stt(P, S32.v(h * 256, (h + 1) * 256), S32.v(h * 256, (h + 1) * 256), float(GAMMAS[h] ** 128), ubs[h], ALU.mult, ALU.add)
        yield
        for h in range(4):
            P.op("dve", lambda e, h=h: e.bn_aggr(out=mv1.t[:, h * 2:(h + 1) * 2], in_=st1.t[:, h * 6:(h + 1) * 6]),
                 outs=[mv1.v(h * 2, (h + 1) * 2)], ins=[st1.v(h * 6, (h + 1) * 6)])
        cp(P, "act", Sbf.whole(), S32.whole())
        yield
        mvv = mv1.whole().r("p (h two) -> p h two", two=2)
        ts(P, "pool", rs1.whole(), mvv[:, :, 1], GN_EPS, ALU.add)
        yield
        tt(P, "pool", rs1.whole(), rs1.whole(), mhalf.v(0, 4), ALU.pow)
        yield
        tt(P, "pool", nm1.whole(), mvv[:, :, 0], rs1.whole(), ALU.mult)
        yield
        ts(P, "pool", nm1.whole(), nm1.whole(), -1.0, ALU.mult)
        yield
        for h in range(4):
            act(P, yn.v(h * 256, (h + 1) * 256), obs[h], AF.Identity, scale=rs1.v(h, h + 1), bias=nm1.v(h, h + 1))
        yield
        tt(P, "pool", yn.whole(), yn.whole(), gnw.whole(), ALU.mult)
        yield
        tt(P, "dve", yn.whole(), yn.whole(), gnb.whole(), ALU.add)
        yield
        tt(P, "pool", retb.whole(), yn.whole(), sgb.whole(), ALU.mult)
        yield
        for m in range(8):
            tr(P, vb(ps[4], m * 128, (m + 1) * 128), retb.v(m * 128, (m + 1) * 128), idb)
        yield
        cp(P, "act", mst.whole(), vb(ps[4], 0, 1024))
        dma(P, mT_d[t, :, 0:1024], mst.whole())
        yield

    def interleave(gens):
        gens = list(gens)
        while gens:
            for g in list(gens):
                try:
                    next(g)
                except StopIteration:
                    gens.remove(g)

    if stop_after >= 1:
        p1_loads(0)
        if nt > 1:
            p1_loads(1)
        interleave([p1_front(0)])
        for t in range(nt):
            if t + 2 < nt:
                p1_loads(t + 2)
            gl = [p1_back(t)]
            if t + 1 < nt:
                gl.append(p1_front(t + 1))
            interleave(gl)
    P.barrier()
    A.off = base_mark
    if stop_after >= 2:
        for hg in range(2):
            gdn_pass(hg)
    if stop_after >= 3:
        pass3()
    return nc, P, A, ps, locals()


def finish(nc, P):
    P.final_wait()
    P.finalize()
    sems = {te: nc.alloc_semaphore("s_" + te) for te in P.teops}
    P.emit(sems)
    return nc


_CONSTS = None


def prep_core_inputs(inp, b):
    global _CONSTS
    if _CONSTS is None:
        _CONSTS = host_consts()
    cf, cbt, rope = _CONSTS
    f = lambda a: np.ascontiguousarray(np.asarray(a, dtype=np.float32))
    b_ada = np.asarray(inp["b_ada"])[0]
    conv = np.asarray(inp["gdn_conv_w"])[0]
    return {
        "x": f(np.asarray(inp["x"])[b]),
        "ccol": f(np.asarray(inp["c"])[b].reshape(16, 128).T),
        "wada": f(np.asarray(inp["w_ada"])[0]),
        "badac": f(b_ada[:4096].reshape(32, 128).T),
        "badag": f(b_ada[4096:].reshape(1, D)),
        "win": f(np.asarray(inp["w_in"])[0]),
        "convw": f(conv.reshape(4, 24, 128).transpose(2, 1, 0).reshape(128, 96)),
        "alog": f(np.asarray(inp["gdn_a_log"])[0].reshape(1, 8)),
        "dtb": f(np.asarray(inp["gdn_dt_bias"])[0].reshape(1, 8)),
        "gnw": f(np.asarray(inp["ret_gn_w"])[0].reshape(1, RW)),
        "gnb": f(np.asarray(inp["ret_gn_b"])[0].reshape(1, RW)),
        "gdnw": f(np.tile(np.asarray(inp["gdn_norm_w"])[0], 8).reshape(1, GW)),
        "wout": f(np.asarray(inp["w_out"])[0]),
        "lnw": f(np.asarray(inp["ln_w"])[0].reshape(1, D)),
        "lnb": f(np.asarray(inp["ln_b"])[0].reshape(1, D)),
        "cf": cf, "cb": cbt, "rope": rope,
    }


def kernel(**inputs):
    nc, P, A, ps, _ = build(nt=NTILE, stop_after=3, debug=False)
    finish(nc, P)
    in_maps = [prep_core_inputs(inputs, b) for b in range(8)]
    res = run_bass_kernel_spmd(nc, in_maps, core_ids=list(range(8)))
    out = np.stack([np.asarray(r["out"], dtype=np.float32) for r in res.results], axis=0)
    return out.astype(np.asarray(inputs["x"]).dtype)
```

```python
import os
import numpy as np
import ml_dtypes
import concourse.bass as bass
import concourse.mybir as mybir
from concourse.bass_utils import run_bass_kernel_spmd

F32 = mybir.dt.float32
BF16 = mybir.dt.bfloat16
AF = mybir.ActivationFunctionType
ALU = mybir.AluOpType
AX = mybir.AxisListType


class Buf:
    def __init__(self, name, t, psum=False):
        self.name = name
        self.t = t
        self.psum = psum
        self.reads = {}
        self.writes = {}

    def v(self, lo, hi, p0=0, p1=None):
        if p1 is None:
            p1 = self.t.shape[0]
        if self.psum:
            return V(self.t[p0:p1, lo:hi], self, 0, 512)
        return V(self.t[p0:p1, lo:hi], self, lo, hi)

    def whole(self):
        return self.v(0, self.t.shape[1])


class V:
    def __init__(self, ap, buf, lo, hi):
        self.ap = ap
        self.buf = buf
        self.lo = lo
        self.hi = hi

    def r(self, pattern, **kw):
        return V(self.ap.rearrange(pattern, **kw), self.buf, self.lo, self.hi)

    def __getitem__(self, idx):
        return V(self.ap[idx], self.buf, self.lo, self.hi)

    def with_ap(self, ap):
        return V(ap, self.buf, self.lo, self.hi)


class Op:
    __slots__ = ("q", "te", "seq", "deps", "fn", "waits", "signal", "ndma")


class Prog:
    QUEUES = ("pe", "act", "dve", "pool", "sp")

    def __init__(self, nc, n_lanes=12):
        self.nc = nc
        self.qops = {q: [] for q in self.QUEUES}
        self.teops = {q: [] for q in self.QUEUES}
        self.lanes = ["L%d" % i for i in range(n_lanes)]
        for l in self.lanes:
            self.teops[l] = []
        self.next_lane = 0

    def _track(self, op, outs, ins):
        deps = {}

        def need(te, seq):
            if te == op.te:
                return
            if deps.get(te, -1) < seq:
                deps[te] = seq

        raw_same = -1
        for v in ins:
            for (te, lo, hi), seq in v.buf.writes.items():
                if lo < v.hi and v.lo < hi:
                    if te == op.te:
                        raw_same = max(raw_same, seq)
                    else:
                        need(te, seq)
            if v.buf.psum:
                for (te, lo, hi), seq in v.buf.reads.items():
                    need(te, seq)
        for v in outs:
            for d in (v.buf.writes, v.buf.reads):
                for (te, lo, hi), seq in d.items():
                    if lo < v.hi and v.lo < hi:
                        if te == op.te:
                            raw_same = max(raw_same, seq)
                        else:
                            need(te, seq)
        if raw_same >= 0 and op.te in ("act", "dve", "pool"):
            deps[op.te] = raw_same
        op.deps = deps
        for v in outs:
            for d in (v.buf.writes, v.buf.reads):
                for k in [k for k in d if v.lo <= k[1] and k[2] <= v.hi]:
                    del d[k]
            v.buf.writes[(op.te, v.lo, v.hi)] = op.seq
        for v in ins:
            v.buf.reads[(op.te, v.lo, v.hi)] = op.seq

    def op(self, q, fn, outs=(), ins=()):
        o = Op()
        o.q = q
        o.te = q
        o.seq = len(self.teops[q])
        o.fn = fn
        o.ndma = 0
        self._track(o, outs, ins)
        self.teops[q].append(o)
        self.qops[q].append(o)
        return o

    def dma(self, fns, outs=(), ins=(), q="sp"):
        lane = self.lanes[self.next_lane]
        self.next_lane = (self.next_lane + 1) % len(self.lanes)
        o = Op()
        o.q = q
        o.te = lane
        o.seq = len(self.teops[lane])
        o.fn = fns
        o.ndma = len(fns)
        self._track(o, outs, ins)
        if o.seq > 0:
            o.deps[lane] = o.seq - 1
        self.teops[lane].append(o)
        self.qops[q].append(o)
        return o

    def final_wait(self):
        o = Op()
        o.q = "sp"
        o.te = "sp"
        o.seq = len(self.teops["sp"])
        o.fn = None
        o.ndma = 0
        o.deps = {te: len(ops) - 1 for te, ops in self.teops.items() if te.startswith("L") and ops}
        self.teops["sp"].append(o)
        self.qops["sp"].append(o)

    def finalize(self):
        for q in self.QUEUES:
            seen = {}
            for o in self.qops[q]:
                w = []
                for te, seq in o.deps.items():
                    if seen.get(te, -1) < seq:
                        seen[te] = seq
                        w.append((te, seq))
                o.waits = w
        for te, ops in self.teops.items():
            for o in ops:
                o.signal = te.startswith("L")
        for q in self.QUEUES:
            for o in self.qops[q]:
                for te, seq in o.waits:
                    self.teops[te][seq].signal = True
        self.val = {}
        for te, ops in self.teops.items():
            c = 0
            vals = []
            for o in ops:
                if te.startswith("L"):
                    c += 16 * o.ndma
                elif o.signal:
                    c += 1
                vals.append(c)
            self.val[te] = vals

    def emit(self, sems):
        nc = self.nc
        with nc.Block() as block:

            def mk(q):
                def body(eng):
                    for o in self.qops[q]:
                        for te, seq in o.waits:
                            eng.wait_ge(sems[te], self.val[te][seq])
                        if o.ndma:
                            for f in o.fn:
                                f(eng).then_inc(sems[o.te], 16)
                        elif o.fn is not None:
                            ins = o.fn(eng)
                            if o.signal:
                                ins.then_inc(sems[o.te], 1)

                return body

            block.tensor(mk("pe"))
            block.scalar(mk("act"))
            block.vector(mk("dve"))
            block.gpsimd(mk("pool"))
            block.sync(mk("sp"))

    def barrier(self):
        last = {}
        for te, ops in self.teops.items():
            i = len(ops) - 1
            while i >= 0 and ops[i].fn is None:
                i -= 1
            if i >= 0:
                last[te] = i
        for q in self.QUEUES:
            o = Op()
            o.q = q
            o.te = q
            o.seq = len(self.teops[q])
            o.fn = None
            o.ndma = 0
            o.deps = {te: s for te, s in last.items() if te != q}
            self.teops[q].append(o)
            self.qops[q].append(o)


def _ap(x):
    return x.ap if isinstance(x, V) else x


def _vs(*xs):
    return [x for x in xs if isinstance(x, V)]


def bcast_mid(v, n, inner):
    a = v.ap
    return v.with_ap(bass.AP(tensor=a.tensor, offset=a.offset, ap=[list(a.ap[0]), [0, n], [1, inner]]))


def bcast_last(v, n, inner):
    a = v.ap
    return v.with_ap(bass.AP(tensor=a.tensor, offset=a.offset, ap=[list(a.ap[0]), [a.ap[1][0], n], [0, inner]]))


def mm(P, out, lhsT, rhs, start=True, stop=True):
    return P.op("pe", lambda e: e.matmul(out=out.ap, lhsT=lhsT.ap, rhs=rhs.ap, start=start, stop=stop),
                outs=[out], ins=[lhsT, rhs])


def tr(P, out, in_, ident):
    return P.op("pe", lambda e: e.transpose(out=out.ap, in_=in_.ap, identity=ident.ap), outs=[out], ins=[in_, ident])


def act(P, out, in_, func, scale=1.0, bias=0.0):
    return P.op("act", lambda e: e.activation(out=out.ap, in_=in_.ap, func=func, scale=_ap(scale), bias=_ap(bias)),
                outs=[out], ins=[in_] + _vs(scale, bias))


def tt(P, q, out, a, b, op):
    return P.op(q, lambda e: e.tensor_tensor(out=out.ap, in0=a.ap, in1=b.ap, op=op), outs=[out], ins=[a, b])


def ts(P, q, out, a, s1, op0, s2=None, op1=None):
    if op1 is None:
        return P.op(q, lambda e: e.tensor_scalar(out=out.ap, in0=a.ap, scalar1=_ap(s1), scalar2=None, op0=op0),
                    outs=[out], ins=[a] + _vs(s1))
    return P.op(q, lambda e: e.tensor_scalar(out=out.ap, in0=a.ap, scalar1=_ap(s1), scalar2=_ap(s2), op0=op0, op1=op1),
                outs=[out], ins=[a] + _vs(s1, s2))


def stt(P, out, a, s, b, op0, op1):
    return P.op("dve", lambda e: e.scalar_tensor_tensor(out=out.ap, in0=a.ap, scalar=_ap(s), in1=b.ap, op0=op0, op1=op1),
                outs=[out], ins=[a, b] + _vs(s))


def cp(P, q, out, in_):
    if q == "act":
        return P.op("act", lambda e: e.copy(out=out.ap, in_=in_.ap), outs=[out], ins=[in_])
    return P.op(q, lambda e: e.tensor_copy(out=out.ap, in_=in_.ap), outs=[out], ins=[in_])


def dma(P, out, in_, q="sp", outs=None, ins=None, **kw):
    o = out.ap if isinstance(out, V) else out
    i = in_.ap if isinstance(in_, V) else in_
    return P.dma([lambda e: e.dma_start(out=o, in_=i, **kw)], outs=_vs(out) if outs is None else outs,
                 ins=_vs(in_) if ins is None else ins, q=q)


T = 4096
D = 2048
NTILE = T // 128
RQW = 512
RW = 1024
GW = 1024
IN_COLS = 7184
RET_COLS = 3072
GDN_OFF = 3072
GDN_COLS = 4112
GN_EPS = 1e-5
RMS_EPS = 1e-6
LN_EPS = 1e-5
ALPHA = 2.0 ** 0.25
GAMMAS = [1.0 - 2.0 ** (-5.0 - h) for h in range(4)]

C_ID, C_TRIU, C_ONES, C_GQ, C_GK, C_RMASK, C_NSTRICT, C_D8, C_M8, C_M16, C_M32, C_M64, C_END = 0, 128, 256, 384, 896, 1408, 1920, 2048, 2176, 2304, 2432, 2560, 2688
B_ID, B_ONES, B_TRIU, B_NEG, B_NTRIU, B_END = 0, 128, 256, 384, 512, 640


def host_consts():
    p = np.arange(128)
    cf = np.zeros((128, C_END), np.float32)
    cf[:, C_ID:C_ID + 128] = np.eye(128)
    cf[:, C_TRIU:C_TRIU + 128] = (p[:, None] <= p[None, :])
    cf[:, C_ONES:C_ONES + 128] = 1.0
    for h, g in enumerate(GAMMAS):
        lg = np.log(np.float64(g))
        cf[:, C_GQ + h * 128:C_GQ + (h + 1) * 128] = np.exp(lg * (p + 1.0))[:, None]
        cf[:, C_GK + h * 128:C_GK + (h + 1) * 128] = (np.exp(lg * (127.0 - p)) * 128.0 ** -0.5)[:, None]
        cf[:, C_RMASK + h * 128:C_RMASK + (h + 1) * 128] = np.where(p[None, :] >= p[:, None], np.exp(-lg * 128.0), 0.0)
    cf[:, C_NSTRICT:C_NSTRICT + 128] = np.where(p[None, :] > p[:, None], -1.0, 0.0)
    cf[:, C_D8:C_D8 + 128] = (p[:, None] // 8 == p[None, :] // 8)
    for col, b in ((C_M8, 8), (C_M16, 16), (C_M32, 32), (C_M64, 64)):
        cf[:, col:col + 128] = (p[:, None] // (2 * b) == p[None, :] // (2 * b)) & (p[:, None] // b != p[None, :] // b)
    cb = np.zeros((128, B_END), np.float32)
    cb[:, B_ID:B_ID + 128] = np.eye(128)
    cb[:, B_ONES:B_ONES + 128] = 1.0
    cb[:, B_TRIU:B_TRIU + 128] = (p[:, None] <= p[None, :])
    cb[:, B_NEG:B_NEG + 128] = np.where(p[None, :] < p[:, None], -30000.0, 0.0)
    cb[:, B_NTRIU:B_NTRIU + 128] = -1.0 * (p[:, None] <= p[None, :])
    cb = cb.astype(ml_dtypes.bfloat16)
    inv_freq = (1.0 / (np.float32(10000.0) ** (np.arange(0, 128, 2, dtype=np.float32) / np.float32(128)))).astype(np.float32)
    ang = (np.arange(T, dtype=np.float32)[:, None] * inv_freq[None, :]).astype(np.float32)
    cos = np.cos(ang).astype(np.float32)
    sin = np.sin(ang).astype(np.float32)
    rope = np.concatenate([cos, cos, -sin, sin], axis=1).astype(np.float32)
    return cf, cb, np.ascontiguousarray(rope)


class Arena:
    def __init__(self, nc, words, ap=None):
        self.t = nc.alloc_sbuf_tensor("arena", [128, words], F32) if ap is None else ap
        self.words = words
        self.off = 0

    def alloc(self, name, n, dt):
        w = n if dt == F32 else (n + 1) // 2
        w = (w + 7) // 8 * 8
        assert self.off + w <= self.words, ("SBUF arena overflow", name, self.off, w)
        ap = self.t[:, self.off:self.off + w]
        self.off += w
        if dt != F32:
            ap = ap.bitcast(dt)
        return Buf(name, ap[:, 0:n])


def vb(buf, lo, hi):
    ap = buf.t[:, :].bitcast(BF16)[:, lo:hi]
    return V(ap, buf, 0, 512)


def r3(v, h):
    return v.r("p (h e) -> p h e", h=h)


def build(nt=NTILE, stop_after=99, debug=False):
    nc = bass.Bass("TRN2", target_bir_lowering=False)

    def din(name, shape, dtype=F32):
        return nc.dram_tensor(name, shape, dtype, kind="ExternalInput").ap()

    x_d = din("x", [T, D])
    ccol_d = din("ccol", [128, 16])
    wada_d = din("wada", [D, 3 * D])
    badac_d = din("badac", [128, 32])
    badag_d = din("badag", [1, D])
    win_d = din("win", [D, IN_COLS])
    convw_d = din("convw", [128, 96])
    alog_d = din("alog", [1, 8])
    dtb_d = din("dtb", [1, 8])
    gnw_d = din("gnw", [1, RW])
    gnb_d = din("gnb", [1, RW])
    gdnw_d = din("gdnw", [1, GW])
    wout_d = din("wout", [D, D])
    lnw_d = din("lnw", [1, D])
    lnb_d = din("lnb", [1, D])
    cf_d = din("cf", [128, C_END])
    cb_d = din("cb", [128, B_END], BF16)
    rope_d = din("rope", [T, 256])
    out_d = nc.dram_tensor("out", [T, D], F32, kind="ExternalOutput").ap()
    skind = "ExternalOutput" if debug else "Internal"
    mT_d = nc.dram_tensor("mT", [NTILE, 128, D], BF16, kind=skind).ap()
    hT_d = nc.dram_tensor("hTs", [NTILE, 128, D], BF16, kind=skind).ap()

    P = Prog(nc)
    A = Arena(nc, 51968)
    ps = [Buf("ps%d" % i, nc.alloc_psum_tensor("ps%d" % i, [128, 512], F32), psum=True) for i in range(8)]

    cf = A.alloc("cf", C_END, F32)
    cb = A.alloc("cb", B_END, BF16)
    dma(P, cf.whole(), cf_d)
    dma(P, cb.whole(), cb_d)
    idf = cf.v(C_ID, C_ID + 128)
    idb = cb.v(B_ID, B_ID + 128)
    mhalf = A.alloc("mhalf", 8, F32)
    P.op("pool", lambda e: e.memset(mhalf.t[:, :], -0.5), outs=[mhalf.whole()])
    epsb = A.alloc("epsb", 1, F32)
    lqb = A.alloc("lqb", 1, F32)
    zb0 = A.alloc("zb0", 1, F32)
    oneb = A.alloc("oneb", 1, F32)
    P.op("pool", lambda e: e.memset(epsb.t[:, :], RMS_EPS), outs=[epsb.whole()])
    P.op("pool", lambda e: e.memset(lqb.t[:, :], float(np.log(128.0 ** -0.5))), outs=[lqb.whole()])
    P.op("pool", lambda e: e.memset(zb0.t[:, :], 0.0), outs=[zb0.whole()])
    P.op("pool", lambda e: e.memset(oneb.t[:, :], 1.0), outs=[oneb.whole()])
    modc = A.alloc("modc", 32, F32)
    sc1 = A.alloc("sc1", 16, F32)
    grow = A.alloc("grow", D, F32)
    R1 = A.alloc("R1", 16 * 3072, BF16)
    R2 = A.alloc("R2", 16 * 2056, BF16)
    base_mark = A.off
    R1f = R1.t.bitcast(F32)
    R2f = R2.t.bitcast(F32)

    wstg_off = A.off
    wstg = [A.alloc("wstg%d" % i, 1024, F32) for i in range(3)]
    base_mark = A.off
    wl_cnt = [0]

    def load_w(dst, dst_stride, src_cols, dst_col0=0):
        pieces = []
        for (c0, n) in src_cols:
            o = 0
            while o < n:
                m = min(1024, n - o)
                pieces.append((c0 + o, m))
                o += m
        dc = dst_col0
        for (c0, m) in pieces:
            for k in range(16):
                i = wl_cnt[0]
                wl_cnt[0] += 1
                stg = wstg[i % 3]
                dma(P, stg.v(0, m), win_d[k * 128:(k + 1) * 128, c0:c0 + m])
                cp(P, ("pool", "dve", "act")[i % 3], dst.v(k * dst_stride + dc, k * dst_stride + dc + m), stg.v(0, m))
            dc += m

    load_w(R1, 3072, [(0, RET_COLS)])

    ccol = A.alloc("ccol", 16, F32)
    scol = A.alloc("scol", 16, F32)
    badac = A.alloc("badac", 32, F32)
    badag = A.alloc("badag", D, F32)
    dma(P, ccol.whole(), ccol_d)
    dma(P, badac.whole(), badac_d)
    dma(P, badag.v(0, D, 0, 1), badag_d)
    act(P, scol.whole(), ccol.whole(), AF.Silu)
    AR2 = Arena(None, R2f.shape[1], ap=R2f)
    wab = [AR2.alloc("wab%d" % i, 16 * 512, F32) for i in range(2)]
    for blk in range(12):
        wb = wab[blk % 2]
        src = wada_d[:, blk * 512:(blk + 1) * 512].rearrange("(k p) n -> p k n", p=128)
        fns = []
        for k0 in range(0, 16, 4):
            o_ap = wb.t[:, k0 * 512:(k0 + 4) * 512].rearrange("p (k n) -> p k n", k=4)
            i_ap = src[:, k0:k0 + 4, :]
            fns.append(lambda e, o_ap=o_ap, i_ap=i_ap: e.dma_start(out=o_ap, in_=i_ap))
        P.dma(fns, outs=[wb.whole()])
        if blk < 8:
            for jj in range(4):
                j = blk * 4 + jj
                for k in range(16):
                    mm(P, ps[0].v(j, j + 1), wb.v(k * 512 + jj * 128, k * 512 + (jj + 1) * 128), scol.v(k, k + 1),
                       start=(k == 0), stop=(k == 15))
        else:
            g = blk - 8
            bank = ps[1 + g % 2]
            for k in range(16):
                mm(P, bank.v(0, 512, 0, 1), scol.v(k, k + 1), wb.v(k * 512, (k + 1) * 512), start=(k == 0), stop=(k == 15))
            tt(P, "dve", grow.v(g * 512, (g + 1) * 512, 0, 1), bank.v(0, 512, 0, 1), badag.v(g * 512, (g + 1) * 512, 0, 1), ALU.add)
    tt(P, "dve", modc.whole(), ps[0].v(0, 32), badac.whole(), ALU.add)
    ts(P, "dve", sc1.whole(), modc.v(16, 32), 1.0, ALU.add)
    P.barrier()
    A.off = base_mark

    def gdn_pass(hg):
        A.off = base_mark
        AR1 = Arena(None, R1f.shape[1], ap=R1f)
        al = AR1.alloc
        W = R2
        WS = 2056
        o = GDN_OFF
        load_w(R2, WS, [(o + hg * 512, 512), (o + 1024 + hg * 512, 512), (o + 2048 + hg * 512, 512),
                        (o + 3072 + hg * 512, 512), (o + 4096 + hg * 4, 4), (o + 4104 + hg * 4, 4)])
        hT2 = [al("hT2_%d" % i, 16 * 256, BF16) for i in range(2)]
        P.barrier()
        wsall = A.t[:, wstg_off:wstg_off + 3072]
        qkvTs = [al("qkvT", 12 * 256, BF16), Buf("qkvT1", wsall[:, 0:1536].bitcast(BF16))]
        sggs = [al("sgg", 1024, F32), Buf("sgg1", wsall[:, 1536:2560])]
        gabs = [al("gab", 16, F32), al("gab1", 16, F32)]
        raw = [al("raw%d" % i, 264, F32) for i in range(2)]
        a1 = [al("a1_%d" % i, 256, F32) for i in range(2)]
        halo = al("halo", 48, F32)
        convw = al("convw", 96, F32)
        sq = al("sq", 512, BF16)
        lnr = al("lnr", 512, F32)
        nega = al("nega", 4, F32)
        dtb = al("dtb", 4, F32)
        gdnw = al("gdnw", 512, F32)
        Sg32 = al("Sg32", 512, F32)
        Sgbf = al("Sgbf", 512, BF16)
        nkcdT = al("nkcdT", 512, BF16)
        vnew = al("vnew", 512, BF16)
        o1s = al("o1s", 512, F32)
        osq = al("osq", 512, F32)
        gyb = al("gyb", 512, BF16)
        mst2 = al("mst2", 512, BF16)
        ss8 = al("ss8", 4, F32)

        class TB:
            pass

        tbs = []
        for i in range(2):
            b = TB()
            for nm in ("z8", "sp8", "g8", "bt", "egc", "egl", "dgl", "etl", "bege"):
                setattr(b, nm, al("%s_%d" % (nm, i), 4, F32))
            b.e8 = al("e8_%d" % i, 8, F32)
            b.gc = al("gc_%d" % i, 8, F32)
            for nm in ("Ghi", "Glo", "dm", "ndms", "kb", "kbe", "ktl", "vbt", "kbT", "attnT", "Bd", "Ad",
                       "Q0", "Q1", "QT0", "QT1", "X0", "X1", "Ao0", "Ao1"):
                setattr(b, nm, al("%s_%d" % (nm, i), 512, BF16))
            b.TUb, b.W1s = b.Q1, b.QT1
            b.pA, b.pB, b.pC = ps[3 * i], ps[3 * i + 1], ps[3 * i + 2]
            b.pD = b.pA
            tbs.append(b)
        tbs[0].other, tbs[1].other = tbs[1], tbs[0]

        ones_bf = cb.v(B_ONES, B_ONES + 128)
        triu_bf = cb.v(B_TRIU, B_TRIU + 128)
        ntriu_bf = cb.v(B_NTRIU, B_NTRIU + 128)
        neg_bf = cb.v(B_NEG, B_NEG + 128)
        dma(P, convw.whole(), convw_d)
        dma(P, nega.whole(), alog_d[:, hg * 4:(hg + 1) * 4].partition_broadcast(128))
        dma(P, dtb.whole(), dtb_d[:, hg * 4:(hg + 1) * 4].partition_broadcast(128))
        dma(P, gdnw.whole(), gdnw_d[:, hg * 512:(hg + 1) * 512].partition_broadcast(128))
        act(P, nega.whole(), nega.whole(), AF.Exp)
        ts(P, "dve", nega.whole(), nega.whole(), -1.0, ALU.mult)
        P.op("pool", lambda e: e.memset(halo.t[:, :], 0.0), outs=[halo.whole()])
        P.op("pool", lambda e: e.memset(Sg32.t[:, :], 0.0), outs=[Sg32.whole()])
        P.op("pool", lambda e: e.memset(Sgbf.t[:, :], 0.0), outs=[Sgbf.whole()])

        def hd(buf, h):
            return buf.v(h * 128, (h + 1) * 128)

        def pq(bank, h):
            return bank.v(h * 128, (h + 1) * 128)

        mk = lambda col: bcast_mid(cf.v(col, col + 128), 4, 128)

        def twork(c, b, qkvT, gab):
            qT = lambda h: qkvT.v(h * 256 + c * 128, h * 256 + (c + 1) * 128)
            kT = lambda h: qkvT.v((4 + h) * 256 + c * 128, (4 + h) * 256 + (c + 1) * 128)
            vT = lambda h: qkvT.v((8 + h) * 256 + c * 128, (8 + h) * 256 + (c + 1) * 128)
            ga = gab.v(c * 8, c * 8 + 4)
            gb = gab.v(c * 8 + 4, c * 8 + 8)
            tt(P, "dve", b.z8.whole(), ga, dtb.whole(), ALU.add)
            act(P, b.e8.v(0, 4), b.z8.whole(), AF.Exp)
            act(P, b.e8.v(4, 8), gb, AF.Exp, scale=-1.0)
            yield
            act(P, b.sp8.whole(), b.e8.v(0, 4), AF.Ln, bias=oneb.whole())
            ts(P, "dve", b.bt.whole(), b.e8.v(4, 8), 1.0, ALU.add)
            yield
            tt(P, "dve", b.g8.whole(), b.sp8.whole(), nega.whole(), ALU.mult)
            P.op("dve", lambda e: e.reciprocal(out=b.bt.t[:, :], in_=b.bt.t[:, :]), outs=[b.bt.whole()], ins=[b.bt.whole()])
            yield
            mm(P, b.pB.v(0, 4), cf.v(C_TRIU, C_TRIU + 128), b.g8.whole())
            mm(P, b.pB.v(4, 8), cf.v(C_ONES, C_ONES + 128), b.g8.whole())
            gbc_ = bcast_last(b.g8.whole(), 4, 128)
            cp(P, "dve", r3(b.Ghi.whole(), 4), gbc_)
            for h in range(4):
                tr(P, vb(b.pA, h * 128, (h + 1) * 128), kT(h), idb)
            yield
            cp(P, "dve", b.gc.whole(), b.pB.v(0, 8))
            tt(P, "dve", r3(b.Glo.whole(), 4), gbc_, r3(b.Ghi.whole(), 4), ALU.subtract)
            yield
            act(P, b.egc.whole(), b.gc.v(0, 4), AF.Exp)
            act(P, b.egl.whole(), b.gc.v(4, 8), AF.Exp)
            tt(P, "dve", b.dgl.whole(), b.gc.v(4, 8), b.gc.v(0, 4), ALU.subtract)
            for h in range(4):
                Db = pq(b.pB, h)
                mm(P, Db, hd(b.Ghi, h), triu_bf, start=True, stop=False)
                mm(P, Db, hd(b.Glo, h), triu_bf, start=False, stop=False)
                mm(P, Db, ntriu_bf, hd(b.Ghi, h), start=False, stop=False)
                mm(P, Db, ntriu_bf, hd(b.Glo, h), start=False, stop=False)
                mm(P, Db, idb, neg_bf, start=False, stop=True)
            yield
            act(P, b.etl.whole(), b.dgl.whole(), AF.Exp)
            tt(P, "dve", b.bege.whole(), b.bt.whole(), b.egc.whole(), ALU.mult)
            act(P, b.dm.whole(), b.pB.whole(), AF.Exp)
            k3 = r3(vb(b.pA, 0, 512), 4)
            tt(P, "dve", r3(b.kb.whole(), 4), k3, bcast_last(b.bt.whole(), 4, 128), ALU.mult)
            yield
            tt(P, "dve", r3(b.kbe.whole(), 4), k3, bcast_last(b.bege.whole(), 4, 128), ALU.mult)
            tt(P, "dve", r3(b.ktl.whole(), 4), k3, bcast_last(b.etl.whole(), 4, 128), ALU.mult)
            for h in range(4):
                tr(P, vb(b.pC, h * 128, (h + 1) * 128), hd(b.kb, h), idb)
            yield
            cp(P, "act", b.kbT.whole(), vb(b.pC, 0, 512))
            tt(P, "dve", r3(b.ndms.whole(), 4), r3(b.dm.whole(), 4), mk(C_NSTRICT), ALU.mult)
            for h in range(4):
                tr(P, vb(b.pA, h * 128, (h + 1) * 128), vT(h), idb)
            yield
            tt(P, "dve", r3(b.vbt.whole(), 4), r3(vb(b.pA, 0, 512), 4), bcast_last(b.bt.whole(), 4, 128), ALU.mult)
            for h in range(4):
                mm(P, pq(b.pC, h), kT(h), hd(b.kbT, h))
            for h in range(4):
                mm(P, pq(b.pD, h), kT(h), qT(h))
            yield
            tt(P, "dve", b.Q0.whole(), b.pC.whole(), b.ndms.whole(), ALU.mult)
            tt(P, "dve", b.attnT.whole(), b.pD.whole(), b.dm.whole(), ALU.mult)
            yield
            for h in range(4):
                tr(P, vb(b.pA, h * 128, (h + 1) * 128), hd(b.Q0, h), idb)
            tt(P, "dve", r3(b.Bd.whole(), 4), r3(b.Q0.whole(), 4), mk(C_D8), ALU.mult)
            yield
            cp(P, "act", b.QT0.whole(), vb(b.pA, 0, 512))
            tt(P, "dve", r3(b.X0.whole(), 4), r3(b.Bd.whole(), 4), mk(C_ID), ALU.add)
            yield
            tt(P, "dve", r3(b.Ad.whole(), 4), r3(b.QT0.whole(), 4), mk(C_D8), ALU.mult)
            yield
            for h in range(4):
                mm(P, pq(b.pB, h), hd(b.Ad, h), hd(b.Bd, h))
            for h in range(4):
                mm(P, pq(b.pC, h), hd(b.Bd, h), hd(b.Ad, h))
            aos = (b.Ao0, b.Ao1, b.Ao0, b.Ao1)
            mcols = (C_M8, C_M16, C_M32, C_M64)
            for li in range(2):
                tt(P, "dve", r3(aos[li].whole(), 4), r3(b.QT0.whole(), 4), mk(mcols[li]), ALU.mult)
            yield
            cp(P, "act", b.Q1.whole(), b.pB.whole())
            cp(P, "dve", b.QT1.whole(), b.pC.whole())
            yield
            for h in range(4):
                mm(P, pq(b.pD, h), hd(b.QT1, h), hd(b.X0, h))
            for h in range(4):
                mm(P, pq(b.pC, h), hd(b.Q1, h), hd(b.QT1, h))
            yield
            tt(P, "dve", b.X1.whole(), b.pD.whole(), b.X0.whole(), ALU.add)
            cp(P, "act", b.Ad.whole(), b.pC.whole())
            yield
            for h in range(4):
                mm(P, pq(b.pD, h), hd(b.Ad, h), hd(b.X1, h))
            yield
            tt(P, "dve", b.X0.whole(), b.pD.whole(), b.X1.whole(), ALU.add)
            yield
            xs_ = [b.X0, b.X1]
            cur = 0
            for li in range(4):
                U = xs_[cur]
                for h in range(4):
                    tr(P, vb(b.pA, h * 128, (h + 1) * 128), hd(U, h), idb)
                for h in range(4):
                    mm(P, pq(b.pB, h), hd(aos[li], h), hd(U, h))
                if li + 2 < 4:
                    tt(P, "dve", r3(aos[li].whole(), 4), r3(b.QT0.whole(), 4), mk(mcols[li + 2]), ALU.mult)
                yield
                cp(P, "act", b.TUb.whole(), vb(b.pA, 0, 512))
                cp(P, "act", b.W1s.whole(), b.pB.whole())
                yield
                for h in range(4):
                    mm(P, pq(b.pD, h), hd(b.TUb, h), hd(b.W1s, h))
                yield
                tt(P, "dve", xs_[1 - cur].whole(), b.pD.whole(), U.whole(), ALU.add)
                yield
                cur = 1 - cur
            b.TT = xs_[cur]
            for h in range(4):
                mm(P, pq(b.pB, h), hd(b.kbe, h), hd(b.TT, h))
            yield

        def chain(c, b, t, qkvT, sgg):
            qT = lambda h: qkvT.v(h * 256 + c * 128, h * 256 + (c + 1) * 128)
            TT = b.TT
            oA, oC = b.other.pA, b.other.pC
            act(P, nkcdT.whole(), b.pB.whole(), AF.Identity, scale=-1.0)
            yield
            for h in range(4):
                pv = pq(b.pC, h)
                mm(P, pv, hd(TT, h), hd(b.vbt, h), start=True, stop=False)
                mm(P, pv, hd(nkcdT, h), hd(Sgbf, h), start=False, stop=True)
            for h in range(4):
                mm(P, pq(b.pA, h), qT(h), hd(Sgbf, h))
            yield
            cp(P, "act", vnew.whole(), b.pC.whole())
            tt(P, "dve", r3(o1s.whole(), 4), r3(b.pA.whole(), 4), bcast_last(b.egc.whole(), 4, 128), ALU.mult)
            yield
            for h in range(4):
                mm(P, pq(oA, h), hd(b.ktl, h), hd(vnew, h))
            for h in range(4):
                mm(P, pq(b.pB, h), hd(b.attnT, h), hd(vnew, h))
            yield
            for h in range(4):
                stt(P, hd(Sg32, h), hd(Sg32, h), b.egl.v(h, h + 1), pq(oA, h), ALU.mult, ALU.add)
            yield
            cp(P, "act", Sgbf.whole(), Sg32.whole())
            tt(P, "dve", o1s.whole(), b.pB.whole(), o1s.whole(), ALU.add)
            yield
            act(P, osq.whole(), o1s.whole(), AF.Square)
            yield
            P.op("dve", lambda e: e.tensor_reduce(out=ss8.t[:, :], in_=osq.t[:, :].rearrange("p (h e) -> p h e", h=4),
                                                  axis=AX.X, op=ALU.add), outs=[ss8.whole()], ins=[osq.whole()])
            yield
            ts(P, "pool", ss8.whole(), ss8.whole(), 1.0 / 128.0, ALU.mult, RMS_EPS, ALU.add)
            tt(P, "dve", osq.whole(), sgg.v(c * 512, (c + 1) * 512), gdnw.whole(), ALU.mult)
            yield
            tt(P, "pool", ss8.whole(), ss8.whole(), mhalf.v(0, 4), ALU.pow)
            yield
            tt(P, "dve", r3(o1s.whole(), 4), r3(o1s.whole(), 4), bcast_last(ss8.whole(), 4, 128), ALU.mult)
            yield
            tt(P, "dve", gyb.whole(), o1s.whole(), osq.whole(), ALU.mult)
            yield
            for h in range(4):
                tr(P, vb(oC, h * 128, (h + 1) * 128), hd(gyb, h), idb)
            yield
            cp(P, "act", mst2.whole(), vb(oC, 0, 512))
            dma(P, mT_d[t, :, 1024 + hg * 512:1024 + (hg + 1) * 512], mst2.whole())
            yield

        def p2_loads(s):
            H = hT2[s % 2]
            for c in range(2):
                t = 2 * s + c
                dst = V(H.t[:, :].rearrange("p (k w) -> p k w", k=16)[:, :, c * 128:(c + 1) * 128], H, 0, 16 * 256)
                dma(P, dst, hT_d[t].rearrange("p (k w) -> p k w", k=16))

        def stage_a(s):
            H = hT2[s % 2]
            qkvT, sgg, gab = qkvTs[s % 2], sggs[s % 2], gabs[s % 2]
            for c in range(2):
                bank = ps[6 + c]
                for k in range(16):
                    mm(P, bank.whole(), H.v(k * 256 + c * 128, k * 256 + (c + 1) * 128), W.v(k * WS + 1536, k * WS + 2048),
                       start=(k == 0), stop=(k == 15))
                act(P, sgg.v(c * 512, (c + 1) * 512), bank.whole(), AF.Silu)
                yield
            for c in range(2):
                for k in range(16):
                    mm(P, ps[6].v(c * 8, (c + 1) * 8), H.v(k * 256 + c * 128, k * 256 + (c + 1) * 128),
                       W.v(k * WS + 2048, k * WS + 2056), start=(k == 0), stop=(k == 15))
            cp(P, "dve", gab.whole(), ps[6].v(0, 16))
            yield
            for ch in range(12):
                bank = ps[6 + (ch + 1) % 2]
                for k in range(16):
                    mm(P, bank.v(0, 256), W.v(k * WS + ch * 128, k * WS + (ch + 1) * 128), H.v(k * 256, (k + 1) * 256),
                       start=(k == 0), stop=(k == 15))
                gch = (ch // 4) * 8 + hg * 4 + ch % 4
                R = raw[ch % 2]
                b1 = a1[ch % 2]
                cw = lambda tap, gch=gch: convw.v(gch * 4 + tap, gch * 4 + tap + 1)
                cp(P, "pool", R.v(0, 3), halo.v(ch * 4, ch * 4 + 3))
                cp(P, "act", R.v(3, 259), bank.v(0, 256))
                yield
                cp(P, "pool", halo.v(ch * 4, ch * 4 + 3), R.v(256, 259))
                act(P, b1.whole(), R.v(0, 256), AF.Identity, scale=cw(0), bias=zb0.whole())
                stt(P, b1.whole(), R.v(1, 257), cw(1), b1.whole(), ALU.mult, ALU.add)
                yield
                stt(P, b1.whole(), R.v(2, 258), cw(2), b1.whole(), ALU.mult, ALU.add)
                stt(P, b1.whole(), R.v(3, 259), cw(3), b1.whole(), ALU.mult, ALU.add)
                act(P, qkvT.v(ch * 256, (ch + 1) * 256), b1.whole(), AF.Silu)
                yield
            for pr in range(4):
                reg = qkvT.v(pr * 512, (pr + 1) * 512)
                act(P, sq.whole(), reg, AF.Square)
                bank = ps[6 + pr % 2]
                for u in range(2):
                    mm(P, bank.v(u * 256, (u + 1) * 256), ones_bf, sq.v(u * 256, (u + 1) * 256))
                yield
                act(P, lnr.whole(), bank.whole(), AF.Ln, bias=epsb.whole())
                act(P, lnr.whole(), lnr.whole(), AF.Exp, scale=-0.5, bias=(lqb.whole() if pr < 2 else zb0.whole()))
                yield
                tt(P, "dve", reg, reg, lnr.whole(), ALU.mult)
                yield

        def stage_b(s):
            qkvT, sgg, gab = qkvTs[s % 2], sggs[s % 2], gabs[s % 2]
            gens = [twork(0, tbs[0], qkvT, gab), twork(1, tbs[1], qkvT, gab)]
            while gens:
                for g_ in list(gens):
                    try:
                        next(g_)
                    except StopIteration:
                        gens.remove(g_)
                yield
            for c in range(2):
                for _ in chain(c, tbs[c], 2 * s + c, qkvT, sgg):
                    yield

        ns = nt // 2
        p2_loads(0)
        if ns > 1:
            p2_loads(1)
        interleave([stage_a(0)])
        for s in range(ns):
            if s + 2 < ns:
                p2_loads(s + 2)
            gl = [stage_b(s)]
            if s + 1 < ns:
                gl.append(stage_a(s + 1))
            interleave(gl)
        P.barrier()

    def pass3():
        A.off = base_mark
        AR1 = Arena(None, R1f.shape[1], ap=R1f)
        AR2 = Arena(None, R2f.shape[1], ap=R2f)
        Wo = AR1.alloc("Wo", 16 * D, BF16)
        gbc = AR2.alloc("gbc", D, F32)
        wst = [AR2.alloc("wst%d" % i, D, F32) for i in range(2)]
        xs3 = [AR2.alloc("x3_%d" % i, D, F32) for i in range(2)]
        mts = [AR2.alloc("mts%d" % i, D, BF16) for i in range(2)]
        zb = [AR2.alloc("zb%d" % i, D, F32) for i in range(2)]
        lnw = AR1.alloc("lnw", D, F32)
        lnb = AR1.alloc("lnb", D, F32)
        st3 = A.alloc("st3", 24, F32)
        mv3 = A.alloc("mv3", 2, F32)
        rs3 = A.alloc("rs3", 1, F32)
        nm3 = A.alloc("nm3", 1, F32)
        dma(P, lnw.whole(), lnw_d.partition_broadcast(128))
        dma(P, lnb.whole(), lnb_d.partition_broadcast(128))
        for g in range(4):
            bank = ps[g % 2]
            mm(P, bank.whole(), cf.v(C_ONES, C_ONES + 128, 0, 1), grow.v(g * 512, (g + 1) * 512, 0, 1))
            cp(P, "act", gbc.v(g * 512, (g + 1) * 512), bank.whole())
        for k in range(16):
            dma(P, wst[k % 2].whole(), wout_d[k * 128:(k + 1) * 128, :])
            tt(P, "dve" if k % 2 else "pool", Wo.v(k * D, (k + 1) * D), wst[k % 2].whole(), gbc.whole(), ALU.mult)
        def p3_loads(t):
            dma(P, xs3[t % 2].whole(), x_d[t * 128:(t + 1) * 128, :])
            dma(P, mts[t % 2].whole(), mT_d[t])

        p3_loads(0)
        for t in range(nt):
            sl = t % 2
            if t + 1 < nt:
                p3_loads(t + 1)
            z = zb[sl]
            for n in range(4):
                bank = ps[(t % 2) * 4 + n]
                for k in range(16):
                    mm(P, bank.whole(), mts[sl].v(k * 128, (k + 1) * 128), Wo.v(k * D + n * 512, k * D + (n + 1) * 512),
                       start=(k == 0), stop=(k == 15))
                stt(P, z.v(n * 512, (n + 1) * 512), xs3[sl].v(n * 512, (n + 1) * 512), float(ALPHA), bank.whole(), ALU.mult, ALU.add)
                P.op("dve", lambda e, n=n, z=z: e.bn_stats(out=st3.t[:, n * 6:(n + 1) * 6], in_=z.t[:, n * 512:(n + 1) * 512]),
                     outs=[st3.v(n * 6, (n + 1) * 6)], ins=[z.v(n * 512, (n + 1) * 512)])
            P.op("dve", lambda e: e.bn_aggr(out=mv3.t[:, :], in_=st3.t[:, :]), outs=[mv3.whole()], ins=[st3.whole()])
            ts(P, "pool", rs3.whole(), mv3.v(1, 2), LN_EPS, ALU.add)
            tt(P, "pool", rs3.whole(), rs3.whole(), mhalf.v(0, 1), ALU.pow)
            tt(P, "pool", nm3.whole(), mv3.v(0, 1), rs3.whole(), ALU.mult)
            ts(P, "pool", nm3.whole(), nm3.whole(), -1.0, ALU.mult)
            act(P, z.whole(), z.whole(), AF.Identity, scale=rs3.whole(), bias=nm3.whole())
            tt(P, "dve", z.whole(), z.whole(), lnw.whole(), ALU.mult)
            tt(P, "dve", z.whole(), z.whole(), lnb.whole(), ALU.add)
            dma(P, out_d[t * 128:(t + 1) * 128, :], z.whole())

    AR2 = Arena(None, R2f.shape[1], ap=R2f)
    xs = [AR2.alloc("xs%d" % i, D, F32) for i in range(2)]
    hT1 = [AR2.alloc("hT1_%d" % i, D, BF16) for i in range(2)]
    rp = [AR2.alloc("rp%d" % i, 256, F32) for i in range(2)]
    qs = AR2.alloc("qs", 512, F32)
    ks = AR2.alloc("ks", 512, F32)
    tA = AR2.alloc("tA", 512, F32)
    tB = AR2.alloc("tB", 512, F32)
    qhat = AR2.alloc("qhat", 1024, BF16)
    qkT = AR2.alloc("qkT", 1024, BF16)
    vbf = AR2.alloc("vbf", 1024, BF16)
    sg = AR2.alloc("sg", 1024, F32)
    sT = AR2.alloc("sT", 512, BF16)
    S32 = AR2.alloc("S32", 1024, F32)
    Sbf = AR2.alloc("Sbf", 1024, BF16)
    yn = AR2.alloc("yn", 1024, F32)
    retb = AR2.alloc("retb", 1024, BF16)
    mst = AR2.alloc("mst", 1024, BF16)
    gnw = A.alloc("gnw", RW, F32)
    gnb = A.alloc("gnb", RW, F32)
    st1 = A.alloc("st1", 24, F32)
    mv1 = A.alloc("mv1", 8, F32)
    rs1 = A.alloc("rs1", 4, F32)
    nm1 = A.alloc("nm1", 4, F32)
    KS = int(os.environ.get("KSUB", "0"))
    if not KS & 1:
        dma(P, gnw.whole(), gnw_d.partition_broadcast(128))
        dma(P, gnb.whole(), gnb_d.partition_broadcast(128))
    if not KS & 2:
        P.op("pool", lambda e: e.memset(S32.t[:, :], 0.0), outs=[S32.whole()])
        P.op("pool", lambda e: e.memset(Sbf.t[:, :], 0.0), outs=[Sbf.whole()])

    def make_hT(xbuf, hT):
        for g4 in range(4):
            bank = ps[g4 % 2]
            for kk in range(4):
                k = g4 * 4 + kk
                tr(P, bank.v(kk * 128, (kk + 1) * 128), xbuf.v(k * 128, (k + 1) * 128), idf)
            for kk in range(4):
                k = g4 * 4 + kk
                dst = hT.v(k * 128, (k + 1) * 128)
                src = bank.v(kk * 128, (kk + 1) * 128)
                KHT = int(os.environ.get("KHT", "0"))
                if (g4 % 2 == 0 and KHT == 0) or KHT == 1:
                    act(P, dst, src, AF.Identity, scale=sc1.v(k, k + 1), bias=modc.v(k, k + 1))
                else:
                    ts(P, "dve", dst, src, sc1.v(k, k + 1), ALU.mult, modc.v(k, k + 1), ALU.add)

    def rotary(src, dst, rpb):
        a = src.whole().ap
        rot = src.whole().with_ap(bass.AP(tensor=a.tensor, offset=a.offset + 64,
                                          ap=[list(a.ap[0]), [128, 4], [-64, 2], [1, 64]]))
        c2 = bcast_mid(rpb.v(0, 128), 4, 128)
        s = rpb.v(128, 256).ap
        s2 = rpb.v(128, 256).with_ap(bass.AP(tensor=s.tensor, offset=s.offset, ap=[list(s.ap[0]), [0, 4], [64, 2], [1, 64]]))
        tt(P, "dve", r3(tA.whole(), 4), r3(src.whole(), 4), c2, ALU.mult)
        tt(P, "pool", tB.whole().r("p (h t d) -> p h t d", h=4, t=2), rot, s2, ALU.mult)
        tt(P, "dve", dst, tA.whole(), tB.whole(), ALU.add)

    def p1_loads(t):
        dma(P, xs[t % 2].whole(), x_d[t * 128:(t + 1) * 128, :])
        dma(P, rp[t % 2].whole(), rope_d[t * 128:(t + 1) * 128, :])

    qhat2 = Buf("qhat2", wstg[0].t[:, 0:512].bitcast(BF16))
    vbf2 = Buf("vbf2", wstg[0].t[:, 512:1024].bitcast(BF16))
    sg2 = Buf("sg2", wstg[1].t[:, 0:1024])
    qhats, vbfs, sgs = [qhat, qhat2], [vbf, vbf2], [sg, sg2]

    def p1_front(t):
        sl = t % 2
        H = hT1[sl]
        qh, vv, sgb = qhats[sl], vbfs[sl], sgs[sl]
        for g4 in range(4):
            bank = ps[g4 % 2]
            for kk in range(4):
                k = g4 * 4 + kk
                tr(P, bank.v(kk * 128, (kk + 1) * 128), xs[sl].v(k * 128, (k + 1) * 128), idf)
            for kk in range(4):
                k = g4 * 4 + kk
                dst = H.v(k * 128, (k + 1) * 128)
                src = bank.v(kk * 128, (kk + 1) * 128)
                if g4 % 2 == 0:
                    act(P, dst, src, AF.Identity, scale=sc1.v(k, k + 1), bias=modc.v(k, k + 1))
                else:
                    ts(P, "dve", dst, src, sc1.v(k, k + 1), ALU.mult, modc.v(k, k + 1), ALU.add)
            yield
        dma(P, hT_d[t], H.whole())
        for n in range(6):
            bank = ps[2 + n % 2]
            for k in range(16):
                mm(P, bank.whole(), H.v(k * 128, (k + 1) * 128), R1.v(k * 3072 + n * 512, k * 3072 + (n + 1) * 512),
                   start=(k == 0), stop=(k == 15))
            if n == 0:
                tt(P, "dve", qs.whole(), bank.whole(), cf.v(C_GQ, C_GQ + 512), ALU.mult)
                yield
                rotary(qs, qh.v(0, 512), rp[sl])
            elif n == 1:
                tt(P, "dve", ks.whole(), bank.whole(), cf.v(C_GK, C_GK + 512), ALU.mult)
                yield
                rotary(ks, qh.v(512, 1024), rp[sl])
            elif n < 4:
                cp(P, "act", vv.v((n - 2) * 512, (n - 1) * 512), bank.whole())
            else:
                act(P, sgb.v((n - 4) * 512, (n - 3) * 512), bank.whole(), AF.Silu)
            yield

    def p1_back(t):
        sl = t % 2
        qh, vv, sgb = qhats[sl], vbfs[sl], sgs[sl]
        for m in range(8):
            tr(P, vb(ps[4], m * 128, (m + 1) * 128), qh.v(m * 128, (m + 1) * 128), idb)
        yield
        cp(P, "dve", qkT.whole(), vb(ps[4], 0, 1024))
        yield
        for h in range(4):
            mm(P, ps[5].v(h * 128, (h + 1) * 128), qkT.v(512 + h * 128, 512 + (h + 1) * 128), qkT.v(h * 128, (h + 1) * 128))
        yield
        tt(P, "dve", sT.whole(), ps[5].whole(), cf.v(C_RMASK, C_RMASK + 512), ALU.mult)
        yield
        obs = []
        for h in range(4):
            ob = ps[6 + h // 2].v((h % 2) * 256, (h % 2 + 1) * 256)
            obs.append(ob)
            mm(P, ob, sT.v(h * 128, (h + 1) * 128), vv.v(h * 256, (h + 1) * 256), start=True, stop=False)
            mm(P, ob, qkT.v(h * 128, (h + 1) * 128), Sbf.v(h * 256, (h + 1) * 256), start=False, stop=True)
        ubs = []
        for h in range(4):
            ub = ps[5 - h // 2].v((h % 2) * 256, (h % 2 + 1) * 256)
            ubs.append(ub)
            mm(P, ub, qh.v(512 + h * 128, 512 + (h + 1) * 128), vv.v(h * 256, (h + 1) * 256))
        yield
        for h in range(4):
            P.op("dve", lambda e, h=h: e.bn_stats(out=st1.t[:, h * 6:(h + 1) * 6], in_=obs[h].ap),
                 outs=[st1.v(h * 6, (h + 1) * 6)], ins=[obs[h]])
        for h in range(4):
            stt(P, S32.v(h * 256, (h + 1) * 256), S32.v(h * 256, (h + 1) * 256), float(GAMMAS[h] ** 128), ubs[h], ALU.mult, ALU.add)
        yield
        for h in range(4):
            P.op("dve", lambda e, h=h: e.bn_aggr(out=mv1.t[:, h * 2:(h + 1) * 2], in_=st1.t[:, h * 6:(h + 1) * 6]),
                 outs=[mv1.v(h * 2, (h + 1) * 2)], ins=[st1.v(h * 6, (h + 1) * 6)])
        cp(P, "act", Sbf.whole(), S32.whole())
        yield
        mvv = mv1.whole().r("p (h two) -> p h two", two=2)
        ts(P, "pool", rs1.whole(), mvv[:, :, 1], GN_EPS, ALU.add)
        yield
        tt(P, "pool", rs1.whole(), rs1.whole(), mhalf.v(0, 4), ALU.pow)
        yield
        tt(P, "pool", nm1.whole(), mvv[:, :, 0], rs1.whole(), ALU.mult)
        yield
        ts(P, "pool", nm1.whole(), nm1.whole(), -1.0, ALU.mult)
        yield
        for h in range(4):
            act(P, yn.v(h * 256, (h + 1) * 256), obs[h], AF.Identity, scale=rs1.v(h, h + 1), bias=nm1.v(h, h + 1))
        yield
        tt(P, "pool", yn.whole(), yn.whole(), gnw.whole(), ALU.mult)
        yield
        tt(P, "dve", yn.whole(), yn.whole(), gnb.whole(), ALU.add)
        yield
        tt(P, "pool", retb.whole(), yn.whole(), sgb.whole(), ALU.mult)
        yield
        for m in range(8):
            tr(P, vb(ps[4], m * 128, (m + 1) * 128), retb.v(m * 128, (m + 1) * 128), idb)
        yield
        cp(P, "act", mst.whole(), vb(ps[4], 0, 1024))
        dma(P, mT_d[t, :, 0:1024], mst.whole())
        yield

    def interleave(gens):
        gens = list(gens)
        while gens:
            for g in list(gens):
                try:
                    next(g)
                except StopIteration:
                    gens.remove(g)

    if stop_after >= 1:
        p1_loads(0)
        if nt > 1:
            p1_loads(1)
        interleave([p1_front(0)])
        for t in range(nt):
            if t + 2 < nt:
                p1_loads(t + 2)
            gl = [p1_back(t)]
            if t + 1 < nt:
                gl.append(p1_front(t + 1))
            interleave(gl)
    P.barrier()
    A.off = base_mark
    if stop_after >= 2:
        for hg in range(2):
            gdn_pass(hg)
    if stop_after >= 3:
        pass3()
    return nc, P, A, ps, locals()


def finish(nc, P):
    P.final_wait()
    P.finalize()
    sems = {te: nc.alloc_semaphore("s_" + te) for te in P.teops}
    P.emit(sems)
    return nc


_CONSTS = None


def prep_core_inputs(inp, b):
    global _CONSTS
    if _CONSTS is None:
        _CONSTS = host_consts()
    cf, cbt, rope = _CONSTS
    f = lambda a: np.ascontiguousarray(np.asarray(a, dtype=np.float32))
    b_ada = np.asarray(inp["b_ada"])[0]
    conv = np.asarray(inp["gdn_conv_w"])[0]
    return {
        "x": f(np.asarray(inp["x"])[b]),
        "ccol": f(np.asarray(inp["c"])[b].reshape(16, 128).T),
        "wada": f(np.asarray(inp["w_ada"])[0]),
        "badac": f(b_ada[:4096].reshape(32, 128).T),
        "badag": f(b_ada[4096:].reshape(1, D)),
        "win": f(np.asarray(inp["w_in"])[0]),
        "convw": f(conv.reshape(4, 24, 128).transpose(2, 1, 0).reshape(128, 96)),
        "alog": f(np.asarray(inp["gdn_a_log"])[0].reshape(1, 8)),
        "dtb": f(np.asarray(inp["gdn_dt_bias"])[0].reshape(1, 8)),
        "gnw": f(np.asarray(inp["ret_gn_w"])[0].reshape(1, RW)),
        "gnb": f(np.asarray(inp["ret_gn_b"])[0].reshape(1, RW)),
        "gdnw": f(np.tile(np.asarray(inp["gdn_norm_w"])[0], 8).reshape(1, GW)),
        "wout": f(np.asarray(inp["w_out"])[0]),
        "lnw": f(np.asarray(inp["ln_w"])[0].reshape(1, D)),
        "lnb": f(np.asarray(inp["ln_b"])[0].reshape(1, D)),
        "cf": cf, "cb": cbt, "rope": rope,
    }


def kernel(**inputs):
    nc, P, A, ps, _ = build(nt=NTILE, stop_after=3, debug=False)
    finish(nc, P)
    in_maps = [prep_core_inputs(inputs, b) for b in range(8)]
    res = run_bass_kernel_spmd(nc, in_maps, core_ids=list(range(8)))
    out = np.stack([np.asarray(r["out"], dtype=np.float32) for r in res.results], axis=0)
    return out.astype(np.asarray(inputs["x"]).dtype)
```

```python
import os
import numpy as np
import ml_dtypes
import concourse.bass as bass
import concourse.mybir as mybir
from concourse.bass_utils import run_bass_kernel_spmd

F32 = mybir.dt.float32
BF16 = mybir.dt.bfloat16
AF = mybir.ActivationFunctionType
ALU = mybir.AluOpType
AX = mybir.AxisListType


class Buf:
    def __init__(self, name, t, psum=False):
        self.name = name
        self.t = t
        self.psum = psum
        self.reads = {}
        self.writes = {}

    def v(self, lo, hi, p0=0, p1=None):
        if p1 is None:
            p1 = self.t.shape[0]
        if self.psum:
            return V(self.t[p0:p1, lo:hi], self, 0, 512)
        return V(self.t[p0:p1, lo:hi], self, lo, hi)

    def whole(self):
        return self.v(0, self.t.shape[1])


class V:
    def __init__(self, ap, buf, lo, hi):
        self.ap = ap
        self.buf = buf
        self.lo = lo
        self.hi = hi

    def r(self, pattern, **kw):
        return V(self.ap.rearrange(pattern, **kw), self.buf, self.lo, self.hi)

    def __getitem__(self, idx):
        return V(self.ap[idx], self.buf, self.lo, self.hi)

    def with_ap(self, ap):
        return V(ap, self.buf, self.lo, self.hi)


class Op:
    __slots__ = ("q", "te", "seq", "deps", "fn", "waits", "signal", "ndma")


class Prog:
    QUEUES = ("pe", "act", "dve", "pool", "sp")

    def __init__(self, nc, n_lanes=12):
        self.nc = nc
        self.qops = {q: [] for q in self.QUEUES}
        self.teops = {q: [] for q in self.QUEUES}
        self.lanes = ["L%d" % i for i in range(n_lanes)]
        for l in self.lanes:
            self.teops[l] = []
        self.next_lane = 0

    def _track(self, op, outs, ins):
        deps = {}

        def need(te, seq):
            if te == op.te:
                return
            if deps.get(te, -1) < seq:
                deps[te] = seq

        raw_same = -1
        for v in ins:
            for (te, lo, hi), seq in v.buf.writes.items():
                if lo < v.hi and v.lo < hi:
                    if te == op.te:
                        raw_same = max(raw_same, seq)
                    else:
                        need(te, seq)
            if v.buf.psum:
                for (te, lo, hi), seq in v.buf.reads.items():
                    need(te, seq)
        for v in outs:
            for d in (v.buf.writes, v.buf.reads):
                for (te, lo, hi), seq in d.items():
                    if lo < v.hi and v.lo < hi:
                        if te == op.te:
                            raw_same = max(raw_same, seq)
                        else:
                            need(te, seq)
        if raw_same >= 0 and op.te in ("act", "dve", "pool"):
            deps[op.te] = raw_same
        op.deps = deps
        for v in outs:
            for d in (v.buf.writes, v.buf.reads):
                for k in [k for k in d if v.lo <= k[1] and k[2] <= v.hi]:
                    del d[k]
            v.buf.writes[(op.te, v.lo, v.hi)] = op.seq
        for v in ins:
            v.buf.reads[(op.te, v.lo, v.hi)] = op.seq

    def op(self, q, fn, outs=(), ins=()):
        o = Op()
        o.q = q
        o.te = q
        o.seq = len(self.teops[q])
        o.fn = fn
        o.ndma = 0
        self._track(o, outs, ins)
        self.teops[q].append(o)
        self.qops[q].append(o)
        return o

    def dma(self, fns, outs=(), ins=(), q="sp"):
        lane = self.lanes[self.next_lane]
        self.next_lane = (self.next_lane + 1) % len(self.lanes)
        o = Op()
        o.q = q
        o.te = lane
        o.seq = len(self.teops[lane])
        o.fn = fns
        o.ndma = len(fns)
        self._track(o, outs, ins)
        if o.seq > 0:
            o.deps[lane] = o.seq - 1
        self.teops[lane].append(o)
        self.qops[q].append(o)
        return o

    def final_wait(self):
        o = Op()
        o.q = "sp"
        o.te = "sp"
        o.seq = len(self.teops["sp"])
        o.fn = None
        o.ndma = 0
        o.deps = {te: len(ops) - 1 for te, ops in self.teops.items() if te.startswith("L") and ops}
        self.teops["sp"].append(o)
        self.qops["sp"].append(o)

    def finalize(self):
        for q in self.QUEUES:
            seen = {}
            for o in self.qops[q]:
                w = []
                for te, seq in o.deps.items():
                    if seen.get(te, -1) < seq:
                        seen[te] = seq
                        w.append((te, seq))
                o.waits = w
        for te, ops in self.teops.items():
            for o in ops:
                o.signal = te.startswith("L")
        for q in self.QUEUES:
            for o in self.qops[q]:
                for te, seq in o.waits:
                    self.teops[te][seq].signal = True
        self.val = {}
        for te, ops in self.teops.items():
            c = 0
            vals = []
            for o in ops:
                if te.startswith("L"):
                    c += 16 * o.ndma
                elif o.signal:
                    c += 1
                vals.append(c)
            self.val[te] = vals

    def emit(self, sems):
        nc = self.nc
        with nc.Block() as block:

            def mk(q):
                def body(eng):
                    for o in self.qops[q]:
                        for te, seq in o.waits:
                            eng.wait_ge(sems[te], self.val[te][seq])
                        if o.ndma:
                            for f in o.fn:
                                f(eng).then_inc(sems[o.te], 16)
                        elif o.fn is not None:
                            ins = o.fn(eng)
                            if o.signal:
                                ins.then_inc(sems[o.te], 1)

                return body

            block.tensor(mk("pe"))
            block.scalar(mk("act"))
            block.vector(mk("dve"))
            block.gpsimd(mk("pool"))
            block.sync(mk("sp"))

    def barrier(self):
        last = {}
        for te, ops in self.teops.items():
            i = len(ops) - 1
            while i >= 0 and ops[i].fn is None:
                i -= 1
            if i >= 0:
                last[te] = i
        for q in self.QUEUES:
            o = Op()
            o.q = q
            o.te = q
            o.seq = len(self.teops[q])
            o.fn = None
            o.ndma = 0
            o.deps = {te: s for te, s in last.items() if te != q}
            self.teops[q].append(o)
            self.qops[q].append(o)


def _ap(x):
    return x.ap if isinstance(x, V) else x


def _vs(*xs):
    return [x for x in xs if isinstance(x, V)]


def bcast_mid(v, n, inner):
    a = v.ap
    return v.with_ap(bass.AP(tensor=a.tensor, offset=a.offset, ap=[list(a.ap[0]), [0, n], [1, inner]]))


def bcast_last(v, n, inner):
    a = v.ap
    return v.with_ap(bass.AP(tensor=a.tensor, offset=a.offset, ap=[list(a.ap[0]), [a.ap[1][0], n], [0, inner]]))


def mm(P, out, lhsT, rhs, start=True, stop=True):
    return P.op("pe", lambda e: e.matmul(out=out.ap, lhsT=lhsT.ap, rhs=rhs.ap, start=start, stop=stop),
                outs=[out], ins=[lhsT, rhs])


def tr(P, out, in_, ident):
    return P.op("pe", lambda e: e.transpose(out=out.ap, in_=in_.ap, identity=ident.ap), outs=[out], ins=[in_, ident])


def act(P, out, in_, func, scale=1.0, bias=0.0):
    return P.op("act", lambda e: e.activation(out=out.ap, in_=in_.ap, func=func, scale=_ap(scale), bias=_ap(bias)),
                outs=[out], ins=[in_] + _vs(scale, bias))


def tt(P, q, out, a, b, op):
    return P.op(q, lambda e: e.tensor_tensor(out=out.ap, in0=a.ap, in1=b.ap, op=op), outs=[out], ins=[a, b])


def ts(P, q, out, a, s1, op0, s2=None, op1=None):
    if op1 is None:
        return P.op(q, lambda e: e.tensor_scalar(out=out.ap, in0=a.ap, scalar1=_ap(s1), scalar2=None, op0=op0),
                    outs=[out], ins=[a] + _vs(s1))
    return P.op(q, lambda e: e.tensor_scalar(out=out.ap, in0=a.ap, scalar1=_ap(s1), scalar2=_ap(s2), op0=op0, op1=op1),
                outs=[out], ins=[a] + _vs(s1, s2))


def stt(P, out, a, s, b, op0, op1):
    return P.op("dve", lambda e: e.scalar_tensor_tensor(out=out.ap, in0=a.ap, scalar=_ap(s), in1=b.ap, op0=op0, op1=op1),
                outs=[out], ins=[a, b] + _vs(s))


def cp(P, q, out, in_):
    if q == "act":
        return P.op("act", lambda e: e.copy(out=out.ap, in_=in_.ap), outs=[out], ins=[in_])
    return P.op(q, lambda e: e.tensor_copy(out=out.ap, in_=in_.ap), outs=[out], ins=[in_])


def dma(P, out, in_, q="sp", outs=None, ins=None, **kw):
    o = out.ap if isinstance(out, V) else out
    i = in_.ap if isinstance(in_, V) else in_
    return P.dma([lambda e: e.dma_start(out=o, in_=i, **kw)], outs=_vs(out) if outs is None else outs,
                 ins=_vs(in_) if ins is None else ins, q=q)


T = 4096
D = 2048
NTILE = T // 128
RQW = 512
RW = 1024
GW = 1024
IN_COLS = 7184
RET_COLS = 3072
GDN_OFF = 3072
GDN_COLS = 4112
GN_EPS = 1e-5
RMS_EPS = 1e-6
LN_EPS = 1e-5
ALPHA = 2.0 ** 0.25
GAMMAS = [1.0 - 2.0 ** (-5.0 - h) for h in range(4)]

C_ID, C_TRIU, C_ONES, C_GQ, C_GK, C_RMASK, C_NSTRICT, C_D8, C_M8, C_M16, C_M32, C_M64, C_END = 0, 128, 256, 384, 896, 1408, 1920, 2048, 2176, 2304, 2432, 2560, 2688
B_ID, B_ONES, B_TRIU, B_NEG, B_NTRIU, B_END = 0, 128, 256, 384, 512, 640


def host_consts():
    p = np.arange(128)
    cf = np.zeros((128, C_END), np.float32)
    cf[:, C_ID:C_ID + 128] = np.eye(128)
    cf[:, C_TRIU:C_TRIU + 128] = (p[:, None] <= p[None, :])
    cf[:, C_ONES:C_ONES + 128] = 1.0
    for h, g in enumerate(GAMMAS):
        lg = np.log(np.float64(g))
        cf[:, C_GQ + h * 128:C_GQ + (h + 1) * 128] = np.exp(lg * (p + 1.0))[:, None]
        cf[:, C_GK + h * 128:C_GK + (h + 1) * 128] = (np.exp(lg * (127.0 - p)) * 128.0 ** -0.5)[:, None]
        cf[:, C_RMASK + h * 128:C_RMASK + (h + 1) * 128] = np.where(p[None, :] >= p[:, None], np.exp(-lg * 128.0), 0.0)
    cf[:, C_NSTRICT:C_NSTRICT + 128] = np.where(p[None, :] > p[:, None], -1.0, 0.0)
    cf[:, C_D8:C_D8 + 128] = (p[:, None] // 8 == p[None, :] // 8)
    for col, b in ((C_M8, 8), (C_M16, 16), (C_M32, 32), (C_M64, 64)):
        cf[:, col:col + 128] = (p[:, None] // (2 * b) == p[None, :] // (2 * b)) & (p[:, None] // b != p[None, :] // b)
    cb = np.zeros((128, B_END), np.float32)
    cb[:, B_ID:B_ID + 128] = np.eye(128)
    cb[:, B_ONES:B_ONES + 128] = 1.0
    cb[:, B_TRIU:B_TRIU + 128] = (p[:, None] <= p[None, :])
    cb[:, B_NEG:B_NEG + 128] = np.where(p[None, :] < p[:, None], -30000.0, 0.0)
    cb[:, B_NTRIU:B_NTRIU + 128] = -1.0 * (p[:, None] <= p[None, :])
    cb = cb.astype(ml_dtypes.bfloat16)
    inv_freq = (1.0 / (np.float32(10000.0) ** (np.arange(0, 128, 2, dtype=np.float32) / np.float32(128)))).astype(np.float32)
    ang = (np.arange(T, dtype=np.float32)[:, None] * inv_freq[None, :]).astype(np.float32)
    cos = np.cos(ang).astype(np.float32)
    sin = np.sin(ang).astype(np.float32)
    rope = np.concatenate([cos, cos, -sin, sin], axis=1).astype(np.float32)
    return cf, cb, np.ascontiguousarray(rope)


class Arena:
    def __init__(self, nc, words, ap=None):
        self.t = nc.alloc_sbuf_tensor("arena", [128, words], F32) if ap is None else ap
        self.words = words
        self.off = 0

    def alloc(self, name, n, dt):
        w = n if dt == F32 else (n + 1) // 2
        w = (w + 7) // 8 * 8
        assert self.off + w <= self.words, ("SBUF arena overflow", name, self.off, w)
        ap = self.t[:, self.off:self.off + w]
        self.off += w
        if dt != F32:
            ap = ap.bitcast(dt)
        return Buf(name, ap[:, 0:n])


def vb(buf, lo, hi):
    ap = buf.t[:, :].bitcast(BF16)[:, lo:hi]
    return V(ap, buf, 0, 512)


def r3(v, h):
    return v.r("p (h e) -> p h e", h=h)


def build(nt=NTILE, stop_after=99, debug=False):
    nc = bass.Bass("TRN2", target_bir_lowering=False)

    def din(name, shape, dtype=F32):
        return nc.dram_tensor(name, shape, dtype, kind="ExternalInput").ap()

    x_d = din("x", [T, D])
    ccol_d = din("ccol", [128, 16])
    wada_d = din("wada", [D, 3 * D])
    badac_d = din("badac", [128, 32])
    badag_d = din("badag", [1, D])
    win_d = din("win", [D, IN_COLS])
    convw_d = din("convw", [128, 96])
    alog_d = din("alog", [1, 8])
    dtb_d = din("dtb", [1, 8])
    gnw_d = din("gnw", [1, RW])
    gnb_d = din("gnb", [1, RW])
    gdnw_d = din("gdnw", [1, GW])
    wout_d = din("wout", [D, D])
    lnw_d = din("lnw", [1, D])
    lnb_d = din("lnb", [1, D])
    cf_d = din("cf", [128, C_END])
    cb_d = din("cb", [128, B_END], BF16)
    rope_d = din("rope", [T, 256])
    out_d = nc.dram_tensor("out", [T, D], F32, kind="ExternalOutput").ap()
    skind = "ExternalOutput" if debug else "Internal"
    mT_d = nc.dram_tensor("mT", [NTILE, 128, D], BF16, kind=skind).ap()
    hT_d = nc.dram_tensor("hTs", [NTILE, 128, D], BF16, kind=skind).ap()

    P = Prog(nc)
    A = Arena(nc, 51968)
    ps = [Buf("ps%d" % i, nc.alloc_psum_tensor("ps%d" % i, [128, 512], F32), psum=True) for i in range(8)]

    cf = A.alloc("cf", C_END, F32)
    cb = A.alloc("cb", B_END, BF16)
    dma(P, cf.whole(), cf_d)
    dma(P, cb.whole(), cb_d)
    idf = cf.v(C_ID, C_ID + 128)
    idb = cb.v(B_ID, B_ID + 128)
    mhalf = A.alloc("mhalf", 8, F32)
    P.op("pool", lambda e: e.memset(mhalf.t[:, :], -0.5), outs=[mhalf.whole()])
    epsb = A.alloc("epsb", 1, F32)
    lqb = A.alloc("lqb", 1, F32)
    zb0 = A.alloc("zb0", 1, F32)
    oneb = A.alloc("oneb", 1, F32)
    P.op("pool", lambda e: e.memset(epsb.t[:, :], RMS_EPS), outs=[epsb.whole()])
    P.op("pool", lambda e: e.memset(lqb.t[:, :], float(np.log(128.0 ** -0.5))), outs=[lqb.whole()])
    P.op("pool", lambda e: e.memset(zb0.t[:, :], 0.0), outs=[zb0.whole()])
    P.op("pool", lambda e: e.memset(oneb.t[:, :], 1.0), outs=[oneb.whole()])
    modc = A.alloc("modc", 32, F32)
    sc1 = A.alloc("sc1", 16, F32)
    grow = A.alloc("grow", D, F32)
    R1 = A.alloc("R1", 16 * 3072, BF16)
    R2 = A.alloc("R2", 16 * 2056, BF16)
    base_mark = A.off
    R1f = R1.t.bitcast(F32)
    R2f = R2.t.bitcast(F32)

    wstg_off = A.off
    wstg = [A.alloc("wstg%d" % i, 1024, F32) for i in range(3)]
    base_mark = A.off
    wl_cnt = [0]
    wstg6 = [Buf("wstg6_%d" % j, A.t[:, wstg_off + j * 512:wstg_off + (j + 1) * 512]) for j in range(6)]

    def load_w(dst, dst_stride, src_cols, dst_col0=0):
        pieces = []
        for (c0, n) in src_cols:
            o = 0
            while o < n:
                m = min(512, n - o)
                pieces.append((c0 + o, m))
                o += m
        dc = dst_col0
        for (c0, m) in pieces:
            for k in range(16):
                i = wl_cnt[0]
                wl_cnt[0] += 1
                stg = wstg6[i % 6]
                dma(P, stg.v(0, m), win_d[k * 128:(k + 1) * 128, c0:c0 + m])
                cp(P, ("dve", "act")[i % 2], dst.v(k * dst_stride + dc, k * dst_stride + dc + m), stg.v(0, m))
            dc += m

    load_w(R1, 3072, [(0, RET_COLS)])

    ccol = A.alloc("ccol", 16, F32)
    scol = A.alloc("scol", 16, F32)
    badac = A.alloc("badac", 32, F32)
    badag = A.alloc("badag", D, F32)
    dma(P, ccol.whole(), ccol_d)
    dma(P, badac.whole(), badac_d)
    dma(P, badag.v(0, D, 0, 1), badag_d)
    act(P, scol.whole(), ccol.whole(), AF.Silu)
    AR2 = Arena(None, R2f.shape[1], ap=R2f)
    wab = [AR2.alloc("wab%d" % i, 16 * 512, F32) for i in range(2)]
    for blk in range(12):
        wb = wab[blk % 2]
        src = wada_d[:, blk * 512:(blk + 1) * 512].rearrange("(k p) n -> p k n", p=128)
        fns = []
        for k0 in range(0, 16, 4):
            o_ap = wb.t[:, k0 * 512:(k0 + 4) * 512].rearrange("p (k n) -> p k n", k=4)
            i_ap = src[:, k0:k0 + 4, :]
            fns.append(lambda e, o_ap=o_ap, i_ap=i_ap: e.dma_start(out=o_ap, in_=i_ap))
        P.dma(fns, outs=[wb.whole()])
        if blk < 8:
            for jj in range(4):
                j = blk * 4 + jj
                for k in range(16):
                    mm(P, ps[0].v(j, j + 1), wb.v(k * 512 + jj * 128, k * 512 + (jj + 1) * 128), scol.v(k, k + 1),
                       start=(k == 0), stop=(k == 15))
        else:
            g = blk - 8
            bank = ps[1 + g % 2]
            for k in range(16):
                mm(P, bank.v(0, 512, 0, 1), scol.v(k, k + 1), wb.v(k * 512, (k + 1) * 512), start=(k == 0), stop=(k == 15))
            tt(P, "dve", grow.v(g * 512, (g + 1) * 512, 0, 1), bank.v(0, 512, 0, 1), badag.v(g * 512, (g + 1) * 512, 0, 1), ALU.add)
    tt(P, "dve", modc.whole(), ps[0].v(0, 32), badac.whole(), ALU.add)
    ts(P, "dve", sc1.whole(), modc.v(16, 32), 1.0, ALU.add)
    P.barrier()
    A.off = base_mark

    def gdn_pass(hg):
        A.off = base_mark
        AR1 = Arena(None, R1f.shape[1], ap=R1f)
        al = AR1.alloc
        W = R2
        WS = 2056
        o = GDN_OFF
        load_w(R2, WS, [(o + hg * 512, 512), (o + 1024 + hg * 512, 512), (o + 2048 + hg * 512, 512),
                        (o + 3072 + hg * 512, 512), (o + 4096 + hg * 4, 4), (o + 4104 + hg * 4, 4)])
        hT2 = [al("hT2_%d" % i, 16 * 256, BF16) for i in range(2)]
        P.barrier()
        wsall = A.t[:, wstg_off:wstg_off + 3072]
        qkvTs = [al("qkvT", 12 * 256, BF16), Buf("qkvT1", wsall[:, 0:1536].bitcast(BF16))]
        sggs = [al("sgg", 1024, F32), Buf("sgg1", wsall[:, 1536:2560])]
        gabs = [al("gab", 16, F32), al("gab1", 16, F32)]
        raw = [al("raw%d" % i, 264, F32) for i in range(2)]
        a1 = [al("a1_%d" % i, 256, F32) for i in range(2)]
        halo = al("halo", 48, F32)
        convw = al("convw", 96, F32)
        sq = al("sq", 512, BF16)
        lnr = al("lnr", 512, F32)
        nega = al("nega", 4, F32)
        dtb = al("dtb", 4, F32)
        gdnw = al("gdnw", 512, F32)
        Sg32 = al("Sg32", 512, F32)
        Sgbf = al("Sgbf", 512, BF16)
        cbufs = []
        for i in range(2):
            cbufs.append(dict(nkcdT=al("nkcdT%d" % i, 512, BF16), vnew=al("vnew%d" % i, 512, BF16),
                              o1s=al("o1s%d" % i, 512, F32), osq=al("osq%d" % i, 512, F32),
                              gyb=al("gyb%d" % i, 512, BF16), mst2=al("mst2%d" % i, 512, BF16), ss8=al("ss8%d" % i, 4, F32)))

        class TB:
            pass

        tbs = []
        for i in range(2):
            b = TB()
            for nm in ("z8", "sp8", "g8", "bt", "egc", "egl", "dgl", "etl", "bege"):
                setattr(b, nm, al("%s_%d" % (nm, i), 4, F32))
            b.e8 = al("e8_%d" % i, 8, F32)
            b.gc = al("gc_%d" % i, 8, F32)
            for nm in ("Ghi", "Glo", "dm", "ndms", "kb", "kbe", "ktl", "vbt", "kbT", "attnT", "Bd", "Ad",
                       "Q0", "Q1", "QT0", "QT1", "X0", "X1", "Ao0", "Ao1"):
                setattr(b, nm, al("%s_%d" % (nm, i), 512, BF16))
            b.TUb, b.W1s = b.Q1, b.QT1
            b.pA, b.pB, b.pC = ps[3 * i], ps[3 * i + 1], ps[3 * i + 2]
            b.pD = b.pA
            tbs.append(b)
        tbs[0].other, tbs[1].other = tbs[1], tbs[0]

        ones_bf = cb.v(B_ONES, B_ONES + 128)
        triu_bf = cb.v(B_TRIU, B_TRIU + 128)
        ntriu_bf = cb.v(B_NTRIU, B_NTRIU + 128)
        neg_bf = cb.v(B_NEG, B_NEG + 128)
        dma(P, convw.whole(), convw_d)
        dma(P, nega.whole(), alog_d[:, hg * 4:(hg + 1) * 4].partition_broadcast(128))
        dma(P, dtb.whole(), dtb_d[:, hg * 4:(hg + 1) * 4].partition_broadcast(128))
        dma(P, gdnw.whole(), gdnw_d[:, hg * 512:(hg + 1) * 512].partition_broadcast(128))
        act(P, nega.whole(), nega.whole(), AF.Exp)
        ts(P, "dve", nega.whole(), nega.whole(), -1.0, ALU.mult)
        P.op("pool", lambda e: e.memset(halo.t[:, :], 0.0), outs=[halo.whole()])
        P.op("pool", lambda e: e.memset(Sg32.t[:, :], 0.0), outs=[Sg32.whole()])
        P.op("pool", lambda e: e.memset(Sgbf.t[:, :], 0.0), outs=[Sgbf.whole()])

        def hd(buf, h):
            return buf.v(h * 128, (h + 1) * 128)

        def pq(bank, h):
            return bank.v(h * 128, (h + 1) * 128)

        mk = lambda col: bcast_mid(cf.v(col, col + 128), 4, 128)

        def twork(c, b, qkvT, gab):
            qT = lambda h: qkvT.v(h * 256 + c * 128, h * 256 + (c + 1) * 128)
            kT = lambda h: qkvT.v((4 + h) * 256 + c * 128, (4 + h) * 256 + (c + 1) * 128)
            vT = lambda h: qkvT.v((8 + h) * 256 + c * 128, (8 + h) * 256 + (c + 1) * 128)
            ga = gab.v(c * 8, c * 8 + 4)
            gb = gab.v(c * 8 + 4, c * 8 + 8)
            tt(P, "dve", b.z8.whole(), ga, dtb.whole(), ALU.add)
            act(P, b.e8.v(0, 4), b.z8.whole(), AF.Exp)
            act(P, b.e8.v(4, 8), gb, AF.Exp, scale=-1.0)
            yield
            act(P, b.sp8.whole(), b.e8.v(0, 4), AF.Ln, bias=oneb.whole())
            ts(P, "dve", b.bt.whole(), b.e8.v(4, 8), 1.0, ALU.add)
            yield
            tt(P, "dve", b.g8.whole(), b.sp8.whole(), nega.whole(), ALU.mult)
            P.op("dve", lambda e: e.reciprocal(out=b.bt.t[:, :], in_=b.bt.t[:, :]), outs=[b.bt.whole()], ins=[b.bt.whole()])
            yield
            mm(P, b.pB.v(0, 4), cf.v(C_TRIU, C_TRIU + 128), b.g8.whole())
            mm(P, b.pB.v(4, 8), cf.v(C_ONES, C_ONES + 128), b.g8.whole())
            gbc_ = bcast_last(b.g8.whole(), 4, 128)
            cp(P, "dve", r3(b.Ghi.whole(), 4), gbc_)
            for h in range(4):
                tr(P, vb(b.pA, h * 128, (h + 1) * 128), kT(h), idb)
            yield
            cp(P, "dve", b.gc.whole(), b.pB.v(0, 8))
            tt(P, "dve", r3(b.Glo.whole(), 4), gbc_, r3(b.Ghi.whole(), 4), ALU.subtract)
            yield
            act(P, b.egc.whole(), b.gc.v(0, 4), AF.Exp)
            act(P, b.egl.whole(), b.gc.v(4, 8), AF.Exp)
            tt(P, "dve", b.dgl.whole(), b.gc.v(4, 8), b.gc.v(0, 4), ALU.subtract)
            for h in range(4):
                Db = pq(b.pB, h)
                mm(P, Db, hd(b.Ghi, h), triu_bf, start=True, stop=False)
                mm(P, Db, hd(b.Glo, h), triu_bf, start=False, stop=False)
                mm(P, Db, ntriu_bf, hd(b.Ghi, h), start=False, stop=False)
                mm(P, Db, ntriu_bf, hd(b.Glo, h), start=False, stop=False)
                mm(P, Db, idb, neg_bf, start=False, stop=True)
            yield
            act(P, b.etl.whole(), b.dgl.whole(), AF.Exp)
            tt(P, "dve", b.bege.whole(), b.bt.whole(), b.egc.whole(), ALU.mult)
            act(P, b.dm.whole(), b.pB.whole(), AF.Exp)
            k3 = r3(vb(b.pA, 0, 512), 4)
            tt(P, "dve", r3(b.kb.whole(), 4), k3, bcast_last(b.bt.whole(), 4, 128), ALU.mult)
            yield
            tt(P, "dve", r3(b.kbe.whole(), 4), k3, bcast_last(b.bege.whole(), 4, 128), ALU.mult)
            tt(P, "dve", r3(b.ktl.whole(), 4), k3, bcast_last(b.etl.whole(), 4, 128), ALU.mult)
            for h in range(4):
                tr(P, vb(b.pC, h * 128, (h + 1) * 128), hd(b.kb, h), idb)
            yield
            cp(P, "act", b.kbT.whole(), vb(b.pC, 0, 512))
            tt(P, "dve", r3(b.ndms.whole(), 4), r3(b.dm.whole(), 4), mk(C_NSTRICT), ALU.mult)
            for h in range(4):
                tr(P, vb(b.pA, h * 128, (h + 1) * 128), vT(h), idb)
            yield
            tt(P, "dve", r3(b.vbt.whole(), 4), r3(vb(b.pA, 0, 512), 4), bcast_last(b.bt.whole(), 4, 128), ALU.mult)
            for h in range(4):
                mm(P, pq(b.pC, h), kT(h), hd(b.kbT, h))
            for h in range(4):
                mm(P, pq(b.pD, h), kT(h), qT(h))
            yield
            tt(P, "dve", b.Q0.whole(), b.pC.whole(), b.ndms.whole(), ALU.mult)
            tt(P, "dve", b.attnT.whole(), b.pD.whole(), b.dm.whole(), ALU.mult)
            yield
            for h in range(4):
                tr(P, vb(b.pA, h * 128, (h + 1) * 128), hd(b.Q0, h), idb)
            tt(P, "dve", r3(b.Bd.whole(), 4), r3(b.Q0.whole(), 4), mk(C_D8), ALU.mult)
            yield
            cp(P, "act", b.QT0.whole(), vb(b.pA, 0, 512))
            tt(P, "dve", r3(b.X0.whole(), 4), r3(b.Bd.whole(), 4), mk(C_ID), ALU.add)
            yield
            tt(P, "dve", r3(b.Ad.whole(), 4), r3(b.QT0.whole(), 4), mk(C_D8), ALU.mult)
            yield
            for h in range(4):
                mm(P, pq(b.pB, h), hd(b.Ad, h), hd(b.Bd, h))
            for h in range(4):
                mm(P, pq(b.pC, h), hd(b.Bd, h), hd(b.Ad, h))
            aos = (b.Ao0, b.Ao1, b.Ao0, b.Ao1)
            mcols = (C_M8, C_M16, C_M32, C_M64)
            for li in range(2):
                tt(P, "dve", r3(aos[li].whole(), 4), r3(b.QT0.whole(), 4), mk(mcols[li]), ALU.mult)
            yield
            cp(P, "act", b.Q1.whole(), b.pB.whole())
            cp(P, "dve", b.QT1.whole(), b.pC.whole())
            yield
            for h in range(4):
                mm(P, pq(b.pD, h), hd(b.QT1, h), hd(b.X0, h))
            for h in range(4):
                mm(P, pq(b.pC, h), hd(b.Q1, h), hd(b.QT1, h))
            yield
            tt(P, "dve", b.X1.whole(), b.pD.whole(), b.X0.whole(), ALU.add)
            cp(P, "act", b.Ad.whole(), b.pC.whole())
            yield
            for h in range(4):
                mm(P, pq(b.pD, h), hd(b.Ad, h), hd(b.X1, h))
            yield
            tt(P, "dve", b.X0.whole(), b.pD.whole(), b.X1.whole(), ALU.add)
            yield
            xs_ = [b.X0, b.X1]
            cur = 0
            for li in range(4):
                U = xs_[cur]
                for h in range(4):
                    tr(P, vb(b.pA, h * 128, (h + 1) * 128), hd(U, h), idb)
                for h in range(4):
                    mm(P, pq(b.pB, h), hd(aos[li], h), hd(U, h))
                if li + 2 < 4:
                    tt(P, "dve", r3(aos[li].whole(), 4), r3(b.QT0.whole(), 4), mk(mcols[li + 2]), ALU.mult)
                yield
                cp(P, "act", b.TUb.whole(), vb(b.pA, 0, 512))
                cp(P, "act", b.W1s.whole(), b.pB.whole())
                yield
                for h in range(4):
                    mm(P, pq(b.pD, h), hd(b.TUb, h), hd(b.W1s, h))
                yield
                tt(P, "dve", xs_[1 - cur].whole(), b.pD.whole(), U.whole(), ALU.add)
                yield
                cur = 1 - cur
            b.TT = xs_[cur]
            for h in range(4):
                mm(P, pq(b.pB, h), hd(b.kbe, h), hd(b.TT, h))
            yield

        def chain(c, b, t, qkvT, sgg):
            qT = lambda h: qkvT.v(h * 256 + c * 128, h * 256 + (c + 1) * 128)
            TT = b.TT
            oA, oC = b.other.pA, b.other.pC
            cbf = cbufs[c]
            nkcdT, vnew, o1s, osq, gyb, mst2, ss8 = (cbf["nkcdT"], cbf["vnew"], cbf["o1s"], cbf["osq"], cbf["gyb"],
                                                     cbf["mst2"], cbf["ss8"])
            act(P, nkcdT.whole(), b.pB.whole(), AF.Identity, scale=-1.0)
            yield
            for h in range(4):
                pv = pq(b.pC, h)
                mm(P, pv, hd(TT, h), hd(b.vbt, h), start=True, stop=False)
                mm(P, pv, hd(nkcdT, h), hd(Sgbf, h), start=False, stop=True)
            for h in range(4):
                mm(P, pq(b.pA, h), qT(h), hd(Sgbf, h))
            yield
            cp(P, "act", vnew.whole(), b.pC.whole())
            tt(P, "dve", r3(o1s.whole(), 4), r3(b.pA.whole(), 4), bcast_last(b.egc.whole(), 4, 128), ALU.mult)
            yield
            for h in range(4):
                mm(P, pq(oA, h), hd(b.ktl, h), hd(vnew, h))
            for h in range(4):
                mm(P, pq(b.pB, h), hd(b.attnT, h), hd(vnew, h))
            yield
            for h in range(4):
                stt(P, hd(Sg32, h), hd(Sg32, h), b.egl.v(h, h + 1), pq(oA, h), ALU.mult, ALU.add)
            yield
            cp(P, "act", Sgbf.whole(), Sg32.whole())
            tt(P, "dve", o1s.whole(), b.pB.whole(), o1s.whole(), ALU.add)
            yield
            act(P, osq.whole(), o1s.whole(), AF.Square)
            yield
            P.op("dve", lambda e: e.tensor_reduce(out=ss8.t[:, :], in_=osq.t[:, :].rearrange("p (h e) -> p h e", h=4),
                                                  axis=AX.X, op=ALU.add), outs=[ss8.whole()], ins=[osq.whole()])
            yield
            ts(P, "pool", ss8.whole(), ss8.whole(), 1.0 / 128.0, ALU.mult, RMS_EPS, ALU.add)
            tt(P, "dve", osq.whole(), sgg.v(c * 512, (c + 1) * 512), gdnw.whole(), ALU.mult)
            yield
            tt(P, "pool", ss8.whole(), ss8.whole(), mhalf.v(0, 4), ALU.pow)
            yield
            tt(P, "dve", r3(o1s.whole(), 4), r3(o1s.whole(), 4), bcast_last(ss8.whole(), 4, 128), ALU.mult)
            yield
            tt(P, "dve", gyb.whole(), o1s.whole(), osq.whole(), ALU.mult)
            yield
            for h in range(4):
                tr(P, vb(oC, h * 128, (h + 1) * 128), hd(gyb, h), idb)
            yield
            cp(P, "act", mst2.whole(), vb(oC, 0, 512))
            dma(P, mT_d[t, :, 1024 + hg * 512:1024 + (hg + 1) * 512], mst2.whole())
            yield

        def p2_loads(s):
            H = hT2[s % 2]
            for c in range(2):
                t = 2 * s + c
                dst = V(H.t[:, :].rearrange("p (k w) -> p k w", k=16)[:, :, c * 128:(c + 1) * 128], H, 0, 16 * 256)
                dma(P, dst, hT_d[t].rearrange("p (k w) -> p k w", k=16))

        def stage_a(s):
            H = hT2[s % 2]
            qkvT, sgg, gab = qkvTs[s % 2], sggs[s % 2], gabs[s % 2]
            for c in range(2):
                bank = ps[6 + c]
                for k in range(16):
                    mm(P, bank.whole(), H.v(k * 256 + c * 128, k * 256 + (c + 1) * 128), W.v(k * WS + 1536, k * WS + 2048),
                       start=(k == 0), stop=(k == 15))
                act(P, sgg.v(c * 512, (c + 1) * 512), bank.whole(), AF.Silu)
                yield
            for c in range(2):
                for k in range(16):
                    mm(P, ps[6].v(c * 8, (c + 1) * 8), H.v(k * 256 + c * 128, k * 256 + (c + 1) * 128),
                       W.v(k * WS + 2048, k * WS + 2056), start=(k == 0), stop=(k == 15))
            cp(P, "dve", gab.whole(), ps[6].v(0, 16))
            yield
            for ch in range(12):
                bank = ps[6 + (ch + 1) % 2]
                for k in range(16):
                    mm(P, bank.v(0, 256), W.v(k * WS + ch * 128, k * WS + (ch + 1) * 128), H.v(k * 256, (k + 1) * 256),
                       start=(k == 0), stop=(k == 15))
                gch = (ch // 4) * 8 + hg * 4 + ch % 4
                R = raw[ch % 2]
                b1 = a1[ch % 2]
                cw = lambda tap, gch=gch: convw.v(gch * 4 + tap, gch * 4 + tap + 1)
                cp(P, "pool", R.v(0, 3), halo.v(ch * 4, ch * 4 + 3))
                cp(P, "act", R.v(3, 259), bank.v(0, 256))
                yield
                cp(P, "pool", halo.v(ch * 4, ch * 4 + 3), R.v(256, 259))
                act(P, b1.whole(), R.v(0, 256), AF.Identity, scale=cw(0), bias=zb0.whole())
                stt(P, b1.whole(), R.v(1, 257), cw(1), b1.whole(), ALU.mult, ALU.add)
                yield
                stt(P, b1.whole(), R.v(2, 258), cw(2), b1.whole(), ALU.mult, ALU.add)
                stt(P, b1.whole(), R.v(3, 259), cw(3), b1.whole(), ALU.mult, ALU.add)
                act(P, qkvT.v(ch * 256, (ch + 1) * 256), b1.whole(), AF.Silu)
                yield
            for pr in range(4):
                reg = qkvT.v(pr * 512, (pr + 1) * 512)
                act(P, sq.whole(), reg, AF.Square)
                bank = ps[6 + pr % 2]
                for u in range(2):
                    mm(P, bank.v(u * 256, (u + 1) * 256), ones_bf, sq.v(u * 256, (u + 1) * 256))
                yield
                act(P, lnr.whole(), bank.whole(), AF.Ln, bias=epsb.whole())
                act(P, lnr.whole(), lnr.whole(), AF.Exp, scale=-0.5, bias=(lqb.whole() if pr < 2 else zb0.whole()))
                yield
                tt(P, "dve", reg, reg, lnr.whole(), ALU.mult)
                yield

        def stage_b(s):
            qkvT, sgg, gab = qkvTs[s % 2], sggs[s % 2], gabs[s % 2]
            gens = [twork(0, tbs[0], qkvT, gab), twork(1, tbs[1], qkvT, gab)]
            while gens:
                for g_ in list(gens):
                    try:
                        next(g_)
                    except StopIteration:
                        gens.remove(g_)
                yield
            g0 = chain(0, tbs[0], 2 * s, qkvT, sgg)
            g1 = chain(1, tbs[1], 2 * s + 1, qkvT, sgg)
            for _ in range(6):
                next(g0)
                yield
            cg = [g0, g1]
            while cg:
                for g_ in list(cg):
                    try:
                        next(g_)
                    except StopIteration:
                        cg.remove(g_)
                yield

        ns = nt // 2
        p2_loads(0)
        if ns > 1:
            p2_loads(1)
        interleave([stage_a(0)])
        for s in range(ns):
            if s + 2 < ns:
                p2_loads(s + 2)
            gb_ = stage_b(s)
            ga_ = stage_a(s + 1) if s + 1 < ns else None
            rnd = 0
            while gb_ is not None or ga_ is not None:
                if gb_ is not None:
                    try:
                        next(gb_)
                    except StopIteration:
                        gb_ = None
                if ga_ is not None and (rnd % 3 != 2 or gb_ is None):
                    try:
                        next(ga_)
                    except StopIteration:
                        ga_ = None
                rnd += 1
        P.barrier()

    def pass3():
        A.off = base_mark
        AR1 = Arena(None, R1f.shape[1], ap=R1f)
        AR2 = Arena(None, R2f.shape[1], ap=R2f)
        Wo = AR1.alloc("Wo", 16 * D, BF16)
        gbc = AR2.alloc("gbc", D, F32)
        wst = [AR2.alloc("wst%d" % i, D, F32) for i in range(2)]
        xs3 = [AR2.alloc("x3_%d" % i, D, F32) for i in range(2)]
        mts = [AR2.alloc("mts%d" % i, D, BF16) for i in range(2)]
        zb = [AR2.alloc("zb%d" % i, D, F32) for i in range(2)]
        lnw = AR1.alloc("lnw", D, F32)
        lnb = AR1.alloc("lnb", D, F32)
        st3 = A.alloc("st3", 24, F32)
        mv3 = A.alloc("mv3", 2, F32)
        rs3 = A.alloc("rs3", 1, F32)
        nm3 = A.alloc("nm3", 1, F32)
        dma(P, lnw.whole(), lnw_d.partition_broadcast(128))
        dma(P, lnb.whole(), lnb_d.partition_broadcast(128))
        for g in range(4):
            bank = ps[g % 2]
            mm(P, bank.whole(), cf.v(C_ONES, C_ONES + 128, 0, 1), grow.v(g * 512, (g + 1) * 512, 0, 1))
            cp(P, "act", gbc.v(g * 512, (g + 1) * 512), bank.whole())
        for k in range(16):
            for hf in range(2):
                j = (2 * k + hf) % 4
                stg_ = wst[j // 2].v((j % 2) * 1024, (j % 2 + 1) * 1024)
                dma(P, stg_, wout_d[k * 128:(k + 1) * 128, hf * 1024:(hf + 1) * 1024])
                tt(P, "dve", Wo.v(k * D + hf * 1024, k * D + (hf + 1) * 1024), stg_, gbc.v(hf * 1024, (hf + 1) * 1024), ALU.mult)
        def p3_loads(t):
            dma(P, xs3[t % 2].whole(), x_d[t * 128:(t + 1) * 128, :])
            dma(P, mts[t % 2].whole(), mT_d[t])

        p3_loads(0)
        for t in range(nt):
            sl = t % 2
            if t + 1 < nt:
                p3_loads(t + 1)
            z = zb[sl]
            for n in range(4):
                bank = ps[(t % 2) * 4 + n]
                for k in range(16):
                    mm(P, bank.whole(), mts[sl].v(k * 128, (k + 1) * 128), Wo.v(k * D + n * 512, k * D + (n + 1) * 512),
                       start=(k == 0), stop=(k == 15))
                stt(P, z.v(n * 512, (n + 1) * 512), xs3[sl].v(n * 512, (n + 1) * 512), float(ALPHA), bank.whole(), ALU.mult, ALU.add)
                P.op("dve", lambda e, n=n, z=z: e.bn_stats(out=st3.t[:, n * 6:(n + 1) * 6], in_=z.t[:, n * 512:(n + 1) * 512]),
                     outs=[st3.v(n * 6, (n + 1) * 6)], ins=[z.v(n * 512, (n + 1) * 512)])
            P.op("dve", lambda e: e.bn_aggr(out=mv3.t[:, :], in_=st3.t[:, :]), outs=[mv3.whole()], ins=[st3.whole()])
            ts(P, "pool", rs3.whole(), mv3.v(1, 2), LN_EPS, ALU.add)
            tt(P, "pool", rs3.whole(), rs3.whole(), mhalf.v(0, 1), ALU.pow)
            tt(P, "pool", nm3.whole(), mv3.v(0, 1), rs3.whole(), ALU.mult)
            ts(P, "pool", nm3.whole(), nm3.whole(), -1.0, ALU.mult)
            act(P, z.whole(), z.whole(), AF.Identity, scale=rs3.whole(), bias=nm3.whole())
            tt(P, "dve", z.whole(), z.whole(), lnw.whole(), ALU.mult)
            tt(P, "dve", z.whole(), z.whole(), lnb.whole(), ALU.add)
            dma(P, out_d[t * 128:(t + 1) * 128, :], z.whole())

    AR2 = Arena(None, R2f.shape[1], ap=R2f)
    xs = [AR2.alloc("xs%d" % i, D, F32) for i in range(2)]
    hT1 = [AR2.alloc("hT1_%d" % i, D, BF16) for i in range(2)]
    rp = [AR2.alloc("rp%d" % i, 256, F32) for i in range(2)]
    qs = AR2.alloc("qs", 512, F32)
    ks = AR2.alloc("ks", 512, F32)
    tA = AR2.alloc("tA", 512, F32)
    tB = AR2.alloc("tB", 512, F32)
    qhat = AR2.alloc("qhat", 1024, BF16)
    qkT = AR2.alloc("qkT", 1024, BF16)
    vbf = AR2.alloc("vbf", 1024, BF16)
    sg = AR2.alloc("sg", 1024, F32)
    sT = AR2.alloc("sT", 512, BF16)
    S32 = AR2.alloc("S32", 1024, F32)
    Sbf = AR2.alloc("Sbf", 1024, BF16)
    yn = AR2.alloc("yn", 1024, F32)
    retb = AR2.alloc("retb", 1024, BF16)
    mst = AR2.alloc("mst", 1024, BF16)
    gnw = A.alloc("gnw", RW, F32)
    gnb = A.alloc("gnb", RW, F32)
    st1 = A.alloc("st1", 24, F32)
    mv1 = A.alloc("mv1", 8, F32)
    rs1 = A.alloc("rs1", 4, F32)
    nm1 = A.alloc("nm1", 4, F32)
    KS = int(os.environ.get("KSUB", "0"))
    if not KS & 1:
        dma(P, gnw.whole(), gnw_d.partition_broadcast(128))
        dma(P, gnb.whole(), gnb_d.partition_broadcast(128))
    if not KS & 2:
        P.op("pool", lambda e: e.memset(S32.t[:, :], 0.0), outs=[S32.whole()])
        P.op("pool", lambda e: e.memset(Sbf.t[:, :], 0.0), outs=[Sbf.whole()])

    def make_hT(xbuf, hT):
        for g4 in range(4):
            bank = ps[g4 % 2]
            for kk in range(4):
                k = g4 * 4 + kk
                tr(P, bank.v(kk * 128, (kk + 1) * 128), xbuf.v(k * 128, (k + 1) * 128), idf)
            for kk in range(4):
                k = g4 * 4 + kk
                dst = hT.v(k * 128, (k + 1) * 128)
                src = bank.v(kk * 128, (kk + 1) * 128)
                KHT = int(os.environ.get("KHT", "0"))
                if (g4 % 2 == 0 and KHT == 0) or KHT == 1:
                    act(P, dst, src, AF.Identity, scale=sc1.v(k, k + 1), bias=modc.v(k, k + 1))
                else:
                    ts(P, "dve", dst, src, sc1.v(k, k + 1), ALU.mult, modc.v(k, k + 1), ALU.add)

    def rotary(src, dst, rpb):
        a = src.whole().ap
        rot = src.whole().with_ap(bass.AP(tensor=a.tensor, offset=a.offset + 64,
                                          ap=[list(a.ap[0]), [128, 4], [-64, 2], [1, 64]]))
        c2 = bcast_mid(rpb.v(0, 128), 4, 128)
        s = rpb.v(128, 256).ap
        s2 = rpb.v(128, 256).with_ap(bass.AP(tensor=s.tensor, offset=s.offset, ap=[list(s.ap[0]), [0, 4], [64, 2], [1, 64]]))
        tt(P, "dve", r3(tA.whole(), 4), r3(src.whole(), 4), c2, ALU.mult)
        tt(P, "pool", tB.whole().r("p (h t d) -> p h t d", h=4, t=2), rot, s2, ALU.mult)
        tt(P, "dve", dst, tA.whole(), tB.whole(), ALU.add)

    def p1_loads(t):
        dma(P, xs[t % 2].whole(), x_d[t * 128:(t + 1) * 128, :])
        dma(P, rp[t % 2].whole(), rope_d[t * 128:(t + 1) * 128, :])

    qhat2 = Buf("qhat2", wstg[0].t[:, 0:512].bitcast(BF16))
    vbf2 = Buf("vbf2", wstg[0].t[:, 512:1024].bitcast(BF16))
    sg2 = Buf("sg2", wstg[1].t[:, 0:1024])
    qhats, vbfs, sgs = [qhat, qhat2], [vbf, vbf2], [sg, sg2]

    def p1_front(t):
        sl = t % 2
        H = hT1[sl]
        qh, vv, sgb = qhats[sl], vbfs[sl], sgs[sl]
        for g4 in range(4):
            bank = ps[g4 % 2]
            for kk in range(4):
                k = g4 * 4 + kk
                tr(P, bank.v(kk * 128, (kk + 1) * 128), xs[sl].v(k * 128, (k + 1) * 128), idf)
            for kk in range(4):
                k = g4 * 4 + kk
                dst = H.v(k * 128, (k + 1) * 128)
                src = bank.v(kk * 128, (kk + 1) * 128)
                if g4 % 2 == 0:
                    act(P, dst, src, AF.Identity, scale=sc1.v(k, k + 1), bias=modc.v(k, k + 1))
                else:
                    ts(P, "dve", dst, src, sc1.v(k, k + 1), ALU.mult, modc.v(k, k + 1), ALU.add)
            yield
        dma(P, hT_d[t], H.whole())
        for n in range(6):
            bank = ps[2 + n % 2]
            for k in range(16):
                mm(P, bank.whole(), H.v(k * 128, (k + 1) * 128), R1.v(k * 3072 + n * 512, k * 3072 + (n + 1) * 512),
                   start=(k == 0), stop=(k == 15))
            if n == 0:
                tt(P, "dve", qs.whole(), bank.whole(), cf.v(C_GQ, C_GQ + 512), ALU.mult)
                yield
                rotary(qs, qh.v(0, 512), rp[sl])
            elif n == 1:
                tt(P, "dve", ks.whole(), bank.whole(), cf.v(C_GK, C_GK + 512), ALU.mult)
                yield
                rotary(ks, qh.v(512, 1024), rp[sl])
            elif n < 4:
                cp(P, "act", vv.v((n - 2) * 512, (n - 1) * 512), bank.whole())
            else:
                act(P, sgb.v((n - 4) * 512, (n - 3) * 512), bank.whole(), AF.Silu)
            yield

    def p1_back(t):
        sl = t % 2
        qh, vv, sgb = qhats[sl], vbfs[sl], sgs[sl]
        for m in range(8):
            tr(P, vb(ps[4], m * 128, (m + 1) * 128), qh.v(m * 128, (m + 1) * 128), idb)
        yield
        cp(P, "dve", qkT.whole(), vb(ps[4], 0, 1024))
        yield
        for h in range(4):
            mm(P, ps[5].v(h * 128, (h + 1) * 128), qkT.v(512 + h * 128, 512 + (h + 1) * 128), qkT.v(h * 128, (h + 1) * 128))
        yield
        tt(P, "dve", sT.whole(), ps[5].whole(), cf.v(C_RMASK, C_RMASK + 512), ALU.mult)
        yield
        obs = []
        for h in range(4):
            ob = ps[6 + h // 2].v((h % 2) * 256, (h % 2 + 1) * 256)
            obs.append(ob)
            mm(P, ob, sT.v(h * 128, (h + 1) * 128), vv.v(h * 256, (h + 1) * 256), start=True, stop=False)
            mm(P, ob, qkT.v(h * 128, (h + 1) * 128), Sbf.v(h * 256, (h + 1) * 256), start=False, stop=True)
        ubs = []
        for h in range(4):
            ub = ps[5 - h // 2].v((h % 2) * 256, (h % 2 + 1) * 256)
            ubs.append(ub)
            mm(P, ub, qh.v(512 + h * 128, 512 + (h + 1) * 128), vv.v(h * 256, (h + 1) * 256))
        yield
        for h in range(4):
            P.op("dve", lambda e, h=h: e.bn_stats(out=st1.t[:, h * 6:(h + 1) * 6], in_=obs[h].ap),
                 outs=[st1.v(h * 6, (h + 1) * 6)], ins=[obs[h]])
        for h in range(4):
            stt(P, S32.v(h * 256, (h + 1) * 256), S32.v(h * 256, (h + 1) * 256), float(GAMMAS[h] ** 128), ubs[h], ALU.mult, ALU.add)
        yield
        for h in range(4):
            P.op("dve", lambda e, h=h: e.bn_aggr(out=mv1.t[:, h * 2:(h + 1) * 2], in_=st1.t[:, h * 6:(h + 1) * 6]),
                 outs=[mv1.v(h * 2, (h + 1) * 2)], ins=[st1.v(h * 6, (h + 1) * 6)])
        cp(P, "act", Sbf.whole(), S32.whole())
        yield
        mvv = mv1.whole().r("p (h two) -> p h two", two=2)
        ts(P, "pool", rs1.whole(), mvv[:, :, 1], GN_EPS, ALU.add)
        yield
        tt(P, "pool", rs1.whole(), rs1.whole(), mhalf.v(0, 4), ALU.pow)
        yield
        tt(P, "pool", nm1.whole(), mvv[:, :, 0], rs1.whole(), ALU.mult)
        yield
        ts(P, "pool", nm1.whole(), nm1.whole(), -1.0, ALU.mult)
        yield
        for h in range(4):
            act(P, yn.v(h * 256, (h + 1) * 256), obs[h], AF.Identity, scale=rs1.v(h, h + 1), bias=nm1.v(h, h + 1))
        yield
        tt(P, "pool", yn.whole(), yn.whole(), gnw.whole(), ALU.mult)
        yield
        tt(P, "dve", yn.whole(), yn.whole(), gnb.whole(), ALU.add)
        yield
        tt(P, "pool", retb.whole(), yn.whole(), sgb.whole(), ALU.mult)
        yield
        for m in range(8):
            tr(P, vb(ps[4], m * 128, (m + 1) * 128), retb.v(m * 128, (m + 1) * 128), idb)
        yield
        cp(P, "act", mst.whole(), vb(ps[4], 0, 1024))
        dma(P, mT_d[t, :, 0:1024], mst.whole())
        yield

    def interleave(gens):
        gens = list(gens)
        while gens:
            for g in list(gens):
                try:
                    next(g)
                except StopIteration:
                    gens.remove(g)

    if stop_after >= 1:
        p1_loads(0)
        if nt > 1:
            p1_loads(1)
        interleave([p1_front(0)])
        for t in range(nt):
            if t + 2 < nt:
                p1_loads(t + 2)
            gl = [p1_back(t)]
            if t + 1 < nt:
                gl.append(p1_front(t + 1))
            interleave(gl)
    P.barrier()
    A.off = base_mark
    if stop_after >= 2:
        for hg in range(2):
            gdn_pass(hg)
    if stop_after >= 3:
        pass3()
    return nc, P, A, ps, locals()


def finish(nc, P):
    P.final_wait()
    P.finalize()
    sems = {te: nc.alloc_semaphore("s_" + te) for te in P.teops}
    P.emit(sems)
    return nc


_CONSTS = None


def prep_core_inputs(inp, b):
    global _CONSTS
    if _CONSTS is None:
        _CONSTS = host_consts()
    cf, cbt, rope = _CONSTS
    f = lambda a: np.ascontiguousarray(np.asarray(a, dtype=np.float32))
    b_ada = np.asarray(inp["b_ada"])[0]
    conv = np.asarray(inp["gdn_conv_w"])[0]
    return {
        "x": f(np.asarray(inp["x"])[b]),
        "ccol": f(np.asarray(inp["c"])[b].reshape(16, 128).T),
        "wada": f(np.asarray(inp["w_ada"])[0]),
        "badac": f(b_ada[:4096].reshape(32, 128).T),
        "badag": f(b_ada[4096:].reshape(1, D)),
        "win": f(np.asarray(inp["w_in"])[0]),
        "convw": f(conv.reshape(4, 24, 128).transpose(2, 1, 0).reshape(128, 96)),
        "alog": f(np.asarray(inp["gdn_a_log"])[0].reshape(1, 8)),
        "dtb": f(np.asarray(inp["gdn_dt_bias"])[0].reshape(1, 8)),
        "gnw": f(np.asarray(inp["ret_gn_w"])[0].reshape(1, RW)),
        "gnb": f(np.asarray(inp["ret_gn_b"])[0].reshape(1, RW)),
        "gdnw": f(np.tile(np.asarray(inp["gdn_norm_w"])[0], 8).reshape(1, GW)),
        "wout": f(np.asarray(inp["w_out"])[0]),
        "lnw": f(np.asarray(inp["ln_w"])[0].reshape(1, D)),
        "lnb": f(np.asarray(inp["ln_b"])[0].reshape(1, D)),
        "cf": cf, "cb": cbt, "rope": rope,
    }


def kernel(**inputs):
    nc, P, A, ps, _ = build(nt=NTILE, stop_after=3, debug=False)
    finish(nc, P)
    in_maps = [prep_core_inputs(inputs, b) for b in range(8)]
    res = run_bass_kernel_spmd(nc, in_maps, core_ids=list(range(8)))
    out = np.stack([np.asarray(r["out"], dtype=np.float32) for r in res.results], axis=0)
    return out.astype(np.asarray(inputs["x"]).dtype)
```

```python
import os
import numpy as np
import ml_dtypes
import concourse.bass as bass
import concourse.mybir as mybir
from concourse.bass_utils import run_bass_kernel_spmd

F32 = mybir.dt.float32
BF16 = mybir.dt.bfloat16
AF = mybir.ActivationFunctionType
ALU = mybir.AluOpType
AX = mybir.AxisListType


class Buf:
    def __init__(self, name, t, psum=False):
        self.name = name
        self.t = t
        self.psum = psum
        self.reads = {}
        self.writes = {}

    def v(self, lo, hi, p0=0, p1=None):
        if p1 is None:
            p1 = self.t.shape[0]
        if self.psum:
            return V(self.t[p0:p1, lo:hi], self, 0, 512)
        return V(self.t[p0:p1, lo:hi], self, lo, hi)

    def whole(self):
        return self.v(0, self.t.shape[1])


class V:
    def __init__(self, ap, buf, lo, hi):
        self.ap = ap
        self.buf = buf
        self.lo = lo
        self.hi = hi

    def r(self, pattern, **kw):
        return V(self.ap.rearrange(pattern, **kw), self.buf, self.lo, self.hi)

    def __getitem__(self, idx):
        return V(self.ap[idx], self.buf, self.lo, self.hi)

    def with_ap(self, ap):
        return V(ap, self.buf, self.lo, self.hi)


class Op:
    __slots__ = ("q", "te", "seq", "deps", "fn", "waits", "signal", "ndma")


class Prog:
    QUEUES = ("pe", "act", "dve", "pool", "sp")

    def __init__(self, nc, n_lanes=12):
        self.nc = nc
        self.qops = {q: [] for q in self.QUEUES}
        self.teops = {q: [] for q in self.QUEUES}
        self.lanes = ["L%d" % i for i in range(n_lanes)]
        for l in self.lanes:
            self.teops[l] = []
        self.next_lane = 0

    def _track(self, op, outs, ins):
        deps = {}

        def need(te, seq):
            if te == op.te:
                return
            if deps.get(te, -1) < seq:
                deps[te] = seq

        raw_same = -1
        for v in ins:
            for (te, lo, hi), seq in v.buf.writes.items():
                if lo < v.hi and v.lo < hi:
                    if te == op.te:
                        raw_same = max(raw_same, seq)
                    else:
                        need(te, seq)
            if v.buf.psum:
                for (te, lo, hi), seq in v.buf.reads.items():
                    need(te, seq)
        for v in outs:
            for d in (v.buf.writes, v.buf.reads):
                for (te, lo, hi), seq in d.items():
                    if lo < v.hi and v.lo < hi:
                        if te == op.te:
                            raw_same = max(raw_same, seq)
                        else:
                            need(te, seq)
        if raw_same >= 0 and op.te in ("act", "dve", "pool"):
            deps[op.te] = raw_same
        op.deps = deps
        for v in outs:
            for d in (v.buf.writes, v.buf.reads):
                for k in [k for k in d if v.lo <= k[1] and k[2] <= v.hi]:
                    del d[k]
            v.buf.writes[(op.te, v.lo, v.hi)] = op.seq
        for v in ins:
            v.buf.reads[(op.te, v.lo, v.hi)] = op.seq

    def op(self, q, fn, outs=(), ins=()):
        o = Op()
        o.q = q
        o.te = q
        o.seq = len(self.teops[q])
        o.fn = fn
        o.ndma = 0
        self._track(o, outs, ins)
        self.teops[q].append(o)
        self.qops[q].append(o)
        return o

    def dma(self, fns, outs=(), ins=(), q="sp"):
        lane = self.lanes[self.next_lane]
        self.next_lane = (self.next_lane + 1) % len(self.lanes)
        o = Op()
        o.q = q
        o.te = lane
        o.seq = len(self.teops[lane])
        o.fn = fns
        o.ndma = len(fns)
        self._track(o, outs, ins)
        if o.seq > 0:
            o.deps[lane] = o.seq - 1
        self.teops[lane].append(o)
        self.qops[q].append(o)
        return o

    def final_wait(self):
        o = Op()
        o.q = "sp"
        o.te = "sp"
        o.seq = len(self.teops["sp"])
        o.fn = None
        o.ndma = 0
        o.deps = {te: len(ops) - 1 for te, ops in self.teops.items() if te.startswith("L") and ops}
        self.teops["sp"].append(o)
        self.qops["sp"].append(o)

    def finalize(self):
        for q in self.QUEUES:
            seen = {}
            for o in self.qops[q]:
                w = []
                for te, seq in o.deps.items():
                    if seen.get(te, -1) < seq:
                        seen[te] = seq
                        w.append((te, seq))
                o.waits = w
        for te, ops in self.teops.items():
            for o in ops:
                o.signal = te.startswith("L")
        for q in self.QUEUES:
            for o in self.qops[q]:
                for te, seq in o.waits:
                    self.teops[te][seq].signal = True
        self.val = {}
        for te, ops in self.teops.items():
            c = 0
            vals = []
            for o in ops:
                if te.startswith("L"):
                    c += 16 * o.ndma
                elif o.signal:
                    c += 1
                vals.append(c)
            self.val[te] = vals

    def emit(self, sems):
        nc = self.nc
        with nc.Block() as block:

            def mk(q):
                def body(eng):
                    for o in self.qops[q]:
                        for te, seq in o.waits:
                            eng.wait_ge(sems[te], self.val[te][seq])
                        if o.ndma:
                            for f in o.fn:
                                f(eng).then_inc(sems[o.te], 16)
                        elif o.fn is not None:
                            ins = o.fn(eng)
                            if o.signal:
                                ins.then_inc(sems[o.te], 1)

                return body

            block.tensor(mk("pe"))
            block.scalar(mk("act"))
            block.vector(mk("dve"))
            block.gpsimd(mk("pool"))
            block.sync(mk("sp"))

    def barrier(self):
        last = {}
        for te, ops in self.teops.items():
            i = len(ops) - 1
            while i >= 0 and ops[i].fn is None:
                i -= 1
            if i >= 0:
                last[te] = i
        for q in self.QUEUES:
            o = Op()
            o.q = q
            o.te = q
            o.seq = len(self.teops[q])
            o.fn = None
            o.ndma = 0
            o.deps = {te: s for te, s in last.items() if te != q}
            self.teops[q].append(o)
            self.qops[q].append(o)


def _ap(x):
    return x.ap if isinstance(x, V) else x


def _vs(*xs):
    return [x for x in xs if isinstance(x, V)]


def bcast_mid(v, n, inner):
    a = v.ap
    return v.with_ap(bass.AP(tensor=a.tensor, offset=a.offset, ap=[list(a.ap[0]), [0, n], [1, inner]]))


def bcast_last(v, n, inner):
    a = v.ap
    return v.with_ap(bass.AP(tensor=a.tensor, offset=a.offset, ap=[list(a.ap[0]), [a.ap[1][0], n], [0, inner]]))


def mm(P, out, lhsT, rhs, start=True, stop=True):
    return P.op("pe", lambda e: e.matmul(out=out.ap, lhsT=lhsT.ap, rhs=rhs.ap, start=start, stop=stop),
                outs=[out], ins=[lhsT, rhs])


def tr(P, out, in_, ident):
    return P.op("pe", lambda e: e.transpose(out=out.ap, in_=in_.ap, identity=ident.ap), outs=[out], ins=[in_, ident])


def act(P, out, in_, func, scale=1.0, bias=0.0):
    return P.op("act", lambda e: e.activation(out=out.ap, in_=in_.ap, func=func, scale=_ap(scale), bias=_ap(bias)),
                outs=[out], ins=[in_] + _vs(scale, bias))


def tt(P, q, out, a, b, op):
    return P.op(q, lambda e: e.tensor_tensor(out=out.ap, in0=a.ap, in1=b.ap, op=op), outs=[out], ins=[a, b])


def ts(P, q, out, a, s1, op0, s2=None, op1=None):
    if op1 is None:
        return P.op(q, lambda e: e.tensor_scalar(out=out.ap, in0=a.ap, scalar1=_ap(s1), scalar2=None, op0=op0),
                    outs=[out], ins=[a] + _vs(s1))
    return P.op(q, lambda e: e.tensor_scalar(out=out.ap, in0=a.ap, scalar1=_ap(s1), scalar2=_ap(s2), op0=op0, op1=op1),
                outs=[out], ins=[a] + _vs(s1, s2))


def stt(P, out, a, s, b, op0, op1):
    return P.op("dve", lambda e: e.scalar_tensor_tensor(out=out.ap, in0=a.ap, scalar=_ap(s), in1=b.ap, op0=op0, op1=op1),
                outs=[out], ins=[a, b] + _vs(s))


def cp(P, q, out, in_):
    if q == "act":
        return P.op("act", lambda e: e.copy(out=out.ap, in_=in_.ap), outs=[out], ins=[in_])
    return P.op(q, lambda e: e.tensor_copy(out=out.ap, in_=in_.ap), outs=[out], ins=[in_])


def dma(P, out, in_, q="sp", outs=None, ins=None, **kw):
    o = out.ap if isinstance(out, V) else out
    i = in_.ap if isinstance(in_, V) else in_
    return P.dma([lambda e: e.dma_start(out=o, in_=i, **kw)], outs=_vs(out) if outs is None else outs,
                 ins=_vs(in_) if ins is None else ins, q=q)


T = 4096
D = 2048
NTILE = T // 128
RQW = 512
RW = 1024
GW = 1024
IN_COLS = 7184
RET_COLS = 3072
GDN_OFF = 3072
GDN_COLS = 4112
GN_EPS = 1e-5
RMS_EPS = 1e-6
LN_EPS = 1e-5
ALPHA = 2.0 ** 0.25
GAMMAS = [1.0 - 2.0 ** (-5.0 - h) for h in range(4)]

C_ID, C_TRIU, C_ONES, C_GQ, C_GK, C_RMASK, C_NSTRICT, C_D8, C_M8, C_M16, C_M32, C_M64, C_END = 0, 128, 256, 384, 896, 1408, 1920, 2048, 2176, 2304, 2432, 2560, 2688
B_ID, B_ONES, B_TRIU, B_NEG, B_NTRIU, B_END = 0, 128, 256, 384, 512, 640


def host_consts():
    p = np.arange(128)
    cf = np.zeros((128, C_END), np.float32)
    cf[:, C_ID:C_ID + 128] = np.eye(128)
    cf[:, C_TRIU:C_TRIU + 128] = (p[:, None] <= p[None, :])
    cf[:, C_ONES:C_ONES + 128] = 1.0
    for h, g in enumerate(GAMMAS):
        lg = np.log(np.float64(g))
        cf[:, C_GQ + h * 128:C_GQ + (h + 1) * 128] = np.exp(lg * (p + 1.0))[:, None]
        cf[:, C_GK + h * 128:C_GK + (h + 1) * 128] = (np.exp(lg * (127.0 - p)) * 128.0 ** -0.5)[:, None]
        cf[:, C_RMASK + h * 128:C_RMASK + (h + 1) * 128] = np.where(p[None, :] >= p[:, None], np.exp(-lg * 128.0), 0.0)
    cf[:, C_NSTRICT:C_NSTRICT + 128] = np.where(p[None, :] > p[:, None], -1.0, 0.0)
    cf[:, C_D8:C_D8 + 128] = (p[:, None] // 8 == p[None, :] // 8)
    for col, b in ((C_M8, 8), (C_M16, 16), (C_M32, 32), (C_M64, 64)):
        cf[:, col:col + 128] = (p[:, None] // (2 * b) == p[None, :] // (2 * b)) & (p[:, None] // b != p[None, :] // b)
    cb = np.zeros((128, B_END), np.float32)
    cb[:, B_ID:B_ID + 128] = np.eye(128)
    cb[:, B_ONES:B_ONES + 128] = 1.0
    cb[:, B_TRIU:B_TRIU + 128] = (p[:, None] <= p[None, :])
    cb[:, B_NEG:B_NEG + 128] = np.where(p[None, :] < p[:, None], -30000.0, 0.0)
    cb[:, B_NTRIU:B_NTRIU + 128] = -1.0 * (p[:, None] <= p[None, :])
    cb = cb.astype(ml_dtypes.bfloat16)
    inv_freq = (1.0 / (np.float32(10000.0) ** (np.arange(0, 128, 2, dtype=np.float32) / np.float32(128)))).astype(np.float32)
    ang = (np.arange(T, dtype=np.float32)[:, None] * inv_freq[None, :]).astype(np.float32)
    cos = np.cos(ang).astype(np.float32)
    sin = np.sin(ang).astype(np.float32)
    rope = np.concatenate([cos, cos, -sin, sin], axis=1).astype(np.float32)
    return cf, cb, np.ascontiguousarray(rope)


class Arena:
    def __init__(self, nc, words, ap=None):
        self.t = nc.alloc_sbuf_tensor("arena", [128, words], F32) if ap is None else ap
        self.words = words
        self.off = 0

    def alloc(self, name, n, dt):
        w = n if dt == F32 else (n + 1) // 2
        w = (w + 7) // 8 * 8
        assert self.off + w <= self.words, ("SBUF arena overflow", name, self.off, w)
        ap = self.t[:, self.off:self.off + w]
        self.off += w
        if dt != F32:
            ap = ap.bitcast(dt)
        return Buf(name, ap[:, 0:n])


def vb(buf, lo, hi):
    ap = buf.t[:, :].bitcast(BF16)[:, lo:hi]
    return V(ap, buf, 0, 512)


def r3(v, h):
    return v.r("p (h e) -> p h e", h=h)


def build(nt=NTILE, stop_after=99, debug=False):
    TW_STAGGER = int(os.environ.get("TW_STAGGER", "0"))
    A_DELAY = int(os.environ.get("A_DELAY", "10"))
    nc = bass.Bass("TRN2", target_bir_lowering=False)

    def din(name, shape, dtype=F32):
        return nc.dram_tensor(name, shape, dtype, kind="ExternalInput").ap()

    x_d = din("x", [T, D])
    ccol_d = din("ccol", [128, 16])
    wada_d = din("wada", [D, 3 * D])
    badac_d = din("badac", [128, 32])
    badag_d = din("badag", [1, D])
    win_d = din("win", [D, IN_COLS])
    convw_d = din("convw", [128, 96])
    alog_d = din("alog", [1, 8])
    dtb_d = din("dtb", [1, 8])
    gnw_d = din("gnw", [1, RW])
    gnb_d = din("gnb", [1, RW])
    gdnw_d = din("gdnw", [1, GW])
    wout_d = din("wout", [D, D])
    lnw_d = din("lnw", [1, D])
    lnb_d = din("lnb", [1, D])
    cf_d = din("cf", [128, C_END])
    cb_d = din("cb", [128, B_END], BF16)
    rope_d = din("rope", [T, 256])
    out_d = nc.dram_tensor("out", [T, D], F32, kind="ExternalOutput").ap()
    skind = "ExternalOutput" if debug else "Internal"
    mT_d = nc.dram_tensor("mT", [NTILE, 128, D], BF16, kind=skind).ap()
    hT_d = nc.dram_tensor("hTs", [NTILE, 128, D], BF16, kind=skind).ap()

    P = Prog(nc)
    A = Arena(nc, 51968)
    ps = [Buf("ps%d" % i, nc.alloc_psum_tensor("ps%d" % i, [128, 512], F32), psum=True) for i in range(8)]

    cf = A.alloc("cf", C_END, F32)
    cb = A.alloc("cb", B_END, BF16)
    dma(P, cf.whole(), cf_d)
    dma(P, cb.whole(), cb_d)
    idf = cf.v(C_ID, C_ID + 128)
    idb = cb.v(B_ID, B_ID + 128)
    mhalf = A.alloc("mhalf", 8, F32)
    P.op("pool", lambda e: e.memset(mhalf.t[:, :], -0.5), outs=[mhalf.whole()])
    epsb = A.alloc("epsb", 1, F32)
    lqb = A.alloc("lqb", 1, F32)
    zb0 = A.alloc("zb0", 1, F32)
    oneb = A.alloc("oneb", 1, F32)
    P.op("pool", lambda e: e.memset(epsb.t[:, :], RMS_EPS), outs=[epsb.whole()])
    P.op("pool", lambda e: e.memset(lqb.t[:, :], float(np.log(128.0 ** -0.5))), outs=[lqb.whole()])
    P.op("pool", lambda e: e.memset(zb0.t[:, :], 0.0), outs=[zb0.whole()])
    P.op("pool", lambda e: e.memset(oneb.t[:, :], 1.0), outs=[oneb.whole()])
    modc = A.alloc("modc", 32, F32)
    sc1 = A.alloc("sc1", 16, F32)
    grow = A.alloc("grow", D, F32)
    R1 = A.alloc("R1", 16 * 3072, BF16)
    R2 = A.alloc("R2", 16 * 2056, BF16)
    base_mark = A.off
    R1f = R1.t.bitcast(F32)
    R2f = R2.t.bitcast(F32)

    wstg_off = A.off
    wstg = [A.alloc("wstg%d" % i, 1024, F32) for i in range(3)]
    base_mark = A.off
    wl_cnt = [0]
    wstg6 = [Buf("wstg6_%d" % j, A.t[:, wstg_off + j * 512:wstg_off + (j + 1) * 512]) for j in range(6)]

    def load_w(dst, dst_stride, src_cols, dst_col0=0):
        pieces = []
        for (c0, n) in src_cols:
            o = 0
            while o < n:
                m = min(512, n - o)
                pieces.append((c0 + o, m))
                o += m
        dc = dst_col0
        for (c0, m) in pieces:
            for k in range(16):
                i = wl_cnt[0]
                wl_cnt[0] += 1
                stg = wstg6[i % 6]
                dma(P, stg.v(0, m), win_d[k * 128:(k + 1) * 128, c0:c0 + m])
                cp(P, ("dve", "act")[i % 2], dst.v(k * dst_stride + dc, k * dst_stride + dc + m), stg.v(0, m))
            dc += m

    load_w(R1, 3072, [(0, RET_COLS)])

    ccol = A.alloc("ccol", 16, F32)
    scol = A.alloc("scol", 16, F32)
    badac = A.alloc("badac", 32, F32)
    badag = A.alloc("badag", D, F32)
    dma(P, ccol.whole(), ccol_d)
    dma(P, badac.whole(), badac_d)
    dma(P, badag.v(0, D, 0, 1), badag_d)
    act(P, scol.whole(), ccol.whole(), AF.Silu)
    AR2 = Arena(None, R2f.shape[1], ap=R2f)
    wab = [AR2.alloc("wab%d" % i, 16 * 512, F32) for i in range(2)]
    for blk in range(12):
        wb = wab[blk % 2]
        src = wada_d[:, blk * 512:(blk + 1) * 512].rearrange("(k p) n -> p k n", p=128)
        fns = []
        for k0 in range(0, 16, 4):
            o_ap = wb.t[:, k0 * 512:(k0 + 4) * 512].rearrange("p (k n) -> p k n", k=4)
            i_ap = src[:, k0:k0 + 4, :]
            fns.append(lambda e, o_ap=o_ap, i_ap=i_ap: e.dma_start(out=o_ap, in_=i_ap))
        P.dma(fns, outs=[wb.whole()])
        if blk < 8:
            for jj in range(4):
                j = blk * 4 + jj
                for k in range(16):
                    mm(P, ps[0].v(j, j + 1), wb.v(k * 512 + jj * 128, k * 512 + (jj + 1) * 128), scol.v(k, k + 1),
                       start=(k == 0), stop=(k == 15))
        else:
            g = blk - 8
            bank = ps[1 + g % 2]
            for k in range(16):
                mm(P, bank.v(0, 512, 0, 1), scol.v(k, k + 1), wb.v(k * 512, (k + 1) * 512), start=(k == 0), stop=(k == 15))
            tt(P, "dve", grow.v(g * 512, (g + 1) * 512, 0, 1), bank.v(0, 512, 0, 1), badag.v(g * 512, (g + 1) * 512, 0, 1), ALU.add)
    tt(P, "dve", modc.whole(), ps[0].v(0, 32), badac.whole(), ALU.add)
    ts(P, "dve", sc1.whole(), modc.v(16, 32), 1.0, ALU.add)
    P.barrier()
    A.off = base_mark

    def gdn_pass(hg):
        A.off = base_mark
        AR1 = Arena(None, R1f.shape[1], ap=R1f)
        al = AR1.alloc
        W = R2
        WS = 2056
        o = GDN_OFF
        load_w(R2, WS, [(o + hg * 512, 512), (o + 1024 + hg * 512, 512), (o + 2048 + hg * 512, 512),
                        (o + 3072 + hg * 512, 512), (o + 4096 + hg * 4, 4), (o + 4104 + hg * 4, 4)])
        hT2 = [al("hT2_%d" % i, 16 * 256, BF16) for i in range(2)]
        P.barrier()
        wsall = A.t[:, wstg_off:wstg_off + 3072]
        qkvTs = [al("qkvT", 12 * 256, BF16), Buf("qkvT1", wsall[:, 0:1536].bitcast(BF16))]
        sggs = [al("sgg", 1024, F32), Buf("sgg1", wsall[:, 1536:2560])]
        gabs = [al("gab", 16, F32), al("gab1", 16, F32)]
        raw = [al("raw%d" % i, 264, F32) for i in range(2)]
        a1 = [al("a1_%d" % i, 256, F32) for i in range(2)]
        halo = al("halo", 48, F32)
        convw = al("convw", 96, F32)
        sq = al("sq", 512, BF16)
        lnr = al("lnr", 512, F32)
        nega = al("nega", 4, F32)
        dtb = al("dtb", 4, F32)
        gdnw = al("gdnw", 512, F32)
        Sg32 = al("Sg32", 512, F32)
        Sgbf = al("Sgbf", 512, BF16)
        cbufs = []
        for i in range(2):
            cbufs.append(dict(nkcdT=al("nkcdT%d" % i, 512, BF16), vnew=al("vnew%d" % i, 512, BF16),
                              o1s=al("o1s%d" % i, 512, F32), osq=al("osq%d" % i, 512, F32),
                              gyb=al("gyb%d" % i, 512, BF16), mst2=al("mst2%d" % i, 512, BF16), ss8=al("ss8%d" % i, 4, F32)))

        class TB:
            pass

        tbs = []
        for i in range(2):
            b = TB()
            for nm in ("z8", "sp8", "g8", "bt", "egc", "egl", "dgl", "etl", "bege"):
                setattr(b, nm, al("%s_%d" % (nm, i), 4, F32))
            b.e8 = al("e8_%d" % i, 8, F32)
            b.gc = al("gc_%d" % i, 8, F32)
            for nm in ("Ghi", "Glo", "dm", "ndms", "kb", "kbe", "ktl", "vbt", "kbT", "attnT", "Bd", "Ad",
                       "Q0", "Q1", "QT0", "QT1", "X0", "X1", "Ao0", "Ao1"):
                setattr(b, nm, al("%s_%d" % (nm, i), 512, BF16))
            b.TUb, b.W1s = b.Q1, b.QT1
            b.pA, b.pB, b.pC = ps[3 * i], ps[3 * i + 1], ps[3 * i + 2]
            b.pD = b.pA
            tbs.append(b)
        tbs[0].other, tbs[1].other = tbs[1], tbs[0]

        ones_bf = cb.v(B_ONES, B_ONES + 128)
        triu_bf = cb.v(B_TRIU, B_TRIU + 128)
        ntriu_bf = cb.v(B_NTRIU, B_NTRIU + 128)
        neg_bf = cb.v(B_NEG, B_NEG + 128)
        dma(P, convw.whole(), convw_d)
        dma(P, nega.whole(), alog_d[:, hg * 4:(hg + 1) * 4].partition_broadcast(128))
        dma(P, dtb.whole(), dtb_d[:, hg * 4:(hg + 1) * 4].partition_broadcast(128))
        dma(P, gdnw.whole(), gdnw_d[:, hg * 512:(hg + 1) * 512].partition_broadcast(128))
        act(P, nega.whole(), nega.whole(), AF.Exp)
        ts(P, "dve", nega.whole(), nega.whole(), -1.0, ALU.mult)
        P.op("pool", lambda e: e.memset(halo.t[:, :], 0.0), outs=[halo.whole()])
        P.op("pool", lambda e: e.memset(Sg32.t[:, :], 0.0), outs=[Sg32.whole()])
        P.op("pool", lambda e: e.memset(Sgbf.t[:, :], 0.0), outs=[Sgbf.whole()])

        def hd(buf, h):
            return buf.v(h * 128, (h + 1) * 128)

        def pq(bank, h):
            return bank.v(h * 128, (h + 1) * 128)

        mk = lambda col: bcast_mid(cf.v(col, col + 128), 4, 128)

        def twork(c, b, qkvT, gab):
            qT = lambda h: qkvT.v(h * 256 + c * 128, h * 256 + (c + 1) * 128)
            kT = lambda h: qkvT.v((4 + h) * 256 + c * 128, (4 + h) * 256 + (c + 1) * 128)
            vT = lambda h: qkvT.v((8 + h) * 256 + c * 128, (8 + h) * 256 + (c + 1) * 128)
            ga = gab.v(c * 8, c * 8 + 4)
            gb = gab.v(c * 8 + 4, c * 8 + 8)
            tt(P, "dve", b.z8.whole(), ga, dtb.whole(), ALU.add)
            act(P, b.e8.v(0, 4), b.z8.whole(), AF.Exp)
            act(P, b.e8.v(4, 8), gb, AF.Exp, scale=-1.0)
            yield
            act(P, b.sp8.whole(), b.e8.v(0, 4), AF.Ln, bias=oneb.whole())
            ts(P, "dve", b.bt.whole(), b.e8.v(4, 8), 1.0, ALU.add)
            yield
            tt(P, "dve", b.g8.whole(), b.sp8.whole(), nega.whole(), ALU.mult)
            P.op("dve", lambda e: e.reciprocal(out=b.bt.t[:, :], in_=b.bt.t[:, :]), outs=[b.bt.whole()], ins=[b.bt.whole()])
            yield
            mm(P, b.pB.v(0, 4), cf.v(C_TRIU, C_TRIU + 128), b.g8.whole())
            mm(P, b.pB.v(4, 8), cf.v(C_ONES, C_ONES + 128), b.g8.whole())
            gbc_ = bcast_last(b.g8.whole(), 4, 128)
            cp(P, "dve", r3(b.Ghi.whole(), 4), gbc_)
            for h in range(4):
                tr(P, vb(b.pA, h * 128, (h + 1) * 128), kT(h), idb)
            yield
            cp(P, "dve", b.gc.whole(), b.pB.v(0, 8))
            tt(P, "dve", r3(b.Glo.whole(), 4), gbc_, r3(b.Ghi.whole(), 4), ALU.subtract)
            yield
            act(P, b.egc.whole(), b.gc.v(0, 4), AF.Exp)
            act(P, b.egl.whole(), b.gc.v(4, 8), AF.Exp)
            tt(P, "dve", b.dgl.whole(), b.gc.v(4, 8), b.gc.v(0, 4), ALU.subtract)
            for h in range(4):
                Db = pq(b.pB, h)
                mm(P, Db, hd(b.Ghi, h), triu_bf, start=True, stop=False)
                mm(P, Db, hd(b.Glo, h), triu_bf, start=False, stop=False)
                mm(P, Db, ntriu_bf, hd(b.Ghi, h), start=False, stop=False)
                mm(P, Db, ntriu_bf, hd(b.Glo, h), start=False, stop=False)
                mm(P, Db, idb, neg_bf, start=False, stop=True)
            yield
            act(P, b.etl.whole(), b.dgl.whole(), AF.Exp)
            tt(P, "dve", b.bege.whole(), b.bt.whole(), b.egc.whole(), ALU.mult)
            act(P, b.dm.whole(), b.pB.whole(), AF.Exp)
            k3 = r3(vb(b.pA, 0, 512), 4)
            tt(P, "dve", r3(b.kb.whole(), 4), k3, bcast_last(b.bt.whole(), 4, 128), ALU.mult)
            yield
            tt(P, "dve", r3(b.kbe.whole(), 4), k3, bcast_last(b.bege.whole(), 4, 128), ALU.mult)
            tt(P, "dve", r3(b.ktl.whole(), 4), k3, bcast_last(b.etl.whole(), 4, 128), ALU.mult)
            for h in range(4):
                tr(P, vb(b.pC, h * 128, (h + 1) * 128), hd(b.kb, h), idb)
            yield
            cp(P, "act", b.kbT.whole(), vb(b.pC, 0, 512))
            tt(P, "dve", r3(b.ndms.whole(), 4), r3(b.dm.whole(), 4), mk(C_NSTRICT), ALU.mult)
            for h in range(4):
                tr(P, vb(b.pA, h * 128, (h + 1) * 128), vT(h), idb)
            yield
            tt(P, "dve", r3(b.vbt.whole(), 4), r3(vb(b.pA, 0, 512), 4), bcast_last(b.bt.whole(), 4, 128), ALU.mult)
            for h in range(4):
                mm(P, pq(b.pC, h), kT(h), hd(b.kbT, h))
            for h in range(4):
                mm(P, pq(b.pD, h), kT(h), qT(h))
            yield
            tt(P, "dve", b.Q0.whole(), b.pC.whole(), b.ndms.whole(), ALU.mult)
            tt(P, "dve", b.attnT.whole(), b.pD.whole(), b.dm.whole(), ALU.mult)
            yield
            for h in range(4):
                tr(P, vb(b.pA, h * 128, (h + 1) * 128), hd(b.Q0, h), idb)
            tt(P, "dve", r3(b.Bd.whole(), 4), r3(b.Q0.whole(), 4), mk(C_D8), ALU.mult)
            yield
            cp(P, "act", b.QT0.whole(), vb(b.pA, 0, 512))
            tt(P, "dve", r3(b.X0.whole(), 4), r3(b.Bd.whole(), 4), mk(C_ID), ALU.add)
            yield
            tt(P, "dve", r3(b.Ad.whole(), 4), r3(b.QT0.whole(), 4), mk(C_D8), ALU.mult)
            yield
            for h in range(4):
                mm(P, pq(b.pB, h), hd(b.Ad, h), hd(b.Bd, h))
            for h in range(4):
                mm(P, pq(b.pC, h), hd(b.Bd, h), hd(b.Ad, h))
            aos = (b.Ao0, b.Ao1, b.Ao0, b.Ao1)
            mcols = (C_M8, C_M16, C_M32, C_M64)
            for li in range(2):
                tt(P, "dve", r3(aos[li].whole(), 4), r3(b.QT0.whole(), 4), mk(mcols[li]), ALU.mult)
            yield
            cp(P, "act", b.Q1.whole(), b.pB.whole())
            cp(P, "dve", b.QT1.whole(), b.pC.whole())
            yield
            for h in range(4):
                mm(P, pq(b.pD, h), hd(b.QT1, h), hd(b.X0, h))
            for h in range(4):
                mm(P, pq(b.pC, h), hd(b.Q1, h), hd(b.QT1, h))
            yield
            tt(P, "dve", b.X1.whole(), b.pD.whole(), b.X0.whole(), ALU.add)
            cp(P, "act", b.Ad.whole(), b.pC.whole())
            yield
            for h in range(4):
                mm(P, pq(b.pD, h), hd(b.Ad, h), hd(b.X1, h))
            yield
            tt(P, "dve", b.X0.whole(), b.pD.whole(), b.X1.whole(), ALU.add)
            yield
            xs_ = [b.X0, b.X1]
            cur = 0
            for li in range(4):
                U = xs_[cur]
                for h in range(4):
                    tr(P, vb(b.pA, h * 128, (h + 1) * 128), hd(U, h), idb)
                for h in range(4):
                    mm(P, pq(b.pB, h), hd(aos[li], h), hd(U, h))
                if li + 2 < 4:
                    tt(P, "dve", r3(aos[li].whole(), 4), r3(b.QT0.whole(), 4), mk(mcols[li + 2]), ALU.mult)
                yield
                cp(P, "act", b.TUb.whole(), vb(b.pA, 0, 512))
                cp(P, "act", b.W1s.whole(), b.pB.whole())
                yield
                for h in range(4):
                    mm(P, pq(b.pD, h), hd(b.TUb, h), hd(b.W1s, h))
                yield
                tt(P, "dve", xs_[1 - cur].whole(), b.pD.whole(), U.whole(), ALU.add)
                yield
                cur = 1 - cur
            b.TT = xs_[cur]
            for h in range(4):
                mm(P, pq(b.pB, h), hd(b.kbe, h), hd(b.TT, h))
            yield

        def chain(c, b, t, qkvT, sgg):
            qT = lambda h: qkvT.v(h * 256 + c * 128, h * 256 + (c + 1) * 128)
            TT = b.TT
            oA, oC = b.other.pA, b.other.pC
            cbf = cbufs[c]
            nkcdT, vnew, o1s, osq, gyb, mst2, ss8 = (cbf["nkcdT"], cbf["vnew"], cbf["o1s"], cbf["osq"], cbf["gyb"],
                                                     cbf["mst2"], cbf["ss8"])
            act(P, nkcdT.whole(), b.pB.whole(), AF.Identity, scale=-1.0)
            yield
            for h in range(4):
                pv = pq(b.pC, h)
                mm(P, pv, hd(TT, h), hd(b.vbt, h), start=True, stop=False)
                mm(P, pv, hd(nkcdT, h), hd(Sgbf, h), start=False, stop=True)
            for h in range(4):
                mm(P, pq(b.pA, h), qT(h), hd(Sgbf, h))
            yield
            cp(P, "act", vnew.whole(), b.pC.whole())
            tt(P, "dve", r3(o1s.whole(), 4), r3(b.pA.whole(), 4), bcast_last(b.egc.whole(), 4, 128), ALU.mult)
            yield
            for h in range(4):
                mm(P, pq(oA, h), hd(b.ktl, h), hd(vnew, h))
            for h in range(4):
                mm(P, pq(b.pB, h), hd(b.attnT, h), hd(vnew, h))
            yield
            for h in range(4):
                stt(P, hd(Sg32, h), hd(Sg32, h), b.egl.v(h, h + 1), pq(oA, h), ALU.mult, ALU.add)
            yield
            cp(P, "act", Sgbf.whole(), Sg32.whole())
            tt(P, "dve", o1s.whole(), b.pB.whole(), o1s.whole(), ALU.add)
            yield
            act(P, osq.whole(), o1s.whole(), AF.Square)
            yield
            P.op("dve", lambda e: e.tensor_reduce(out=ss8.t[:, :], in_=osq.t[:, :].rearrange("p (h e) -> p h e", h=4),
                                                  axis=AX.X, op=ALU.add), outs=[ss8.whole()], ins=[osq.whole()])
            yield
            ts(P, "pool", ss8.whole(), ss8.whole(), 1.0 / 128.0, ALU.mult, RMS_EPS, ALU.add)
            tt(P, "dve", osq.whole(), sgg.v(c * 512, (c + 1) * 512), gdnw.whole(), ALU.mult)
            yield
            tt(P, "pool", ss8.whole(), ss8.whole(), mhalf.v(0, 4), ALU.pow)
            yield
            tt(P, "dve", r3(o1s.whole(), 4), r3(o1s.whole(), 4), bcast_last(ss8.whole(), 4, 128), ALU.mult)
            yield
            tt(P, "dve", gyb.whole(), o1s.whole(), osq.whole(), ALU.mult)
            yield
            for h in range(4):
                tr(P, vb(oC, h * 128, (h + 1) * 128), hd(gyb, h), idb)
            yield
            cp(P, "act", mst2.whole(), vb(oC, 0, 512))
            dma(P, mT_d[t, :, 1024 + hg * 512:1024 + (hg + 1) * 512], mst2.whole())
            yield

        def p2_loads(s):
            H = hT2[s % 2]
            for c in range(2):
                t = 2 * s + c
                dst = V(H.t[:, :].rearrange("p (k w) -> p k w", k=16)[:, :, c * 128:(c + 1) * 128], H, 0, 16 * 256)
                dma(P, dst, hT_d[t].rearrange("p (k w) -> p k w", k=16))

        def stage_a(s):
            H = hT2[s % 2]
            qkvT, sgg, gab = qkvTs[s % 2], sggs[s % 2], gabs[s % 2]
            for c in range(2):
                bank = ps[6 + c]
                for k in range(16):
                    mm(P, bank.whole(), H.v(k * 256 + c * 128, k * 256 + (c + 1) * 128), W.v(k * WS + 1536, k * WS + 2048),
                       start=(k == 0), stop=(k == 15))
                act(P, sgg.v(c * 512, (c + 1) * 512), bank.whole(), AF.Silu)
                yield
            for c in range(2):
                for k in range(16):
                    mm(P, ps[6].v(c * 8, (c + 1) * 8), H.v(k * 256 + c * 128, k * 256 + (c + 1) * 128),
                       W.v(k * WS + 2048, k * WS + 2056), start=(k == 0), stop=(k == 15))
            cp(P, "dve", gab.whole(), ps[6].v(0, 16))
            yield
            for ch in range(12):
                bank = ps[6 + (ch + 1) % 2]
                for k in range(16):
                    mm(P, bank.v(0, 256), W.v(k * WS + ch * 128, k * WS + (ch + 1) * 128), H.v(k * 256, (k + 1) * 256),
                       start=(k == 0), stop=(k == 15))
                gch = (ch // 4) * 8 + hg * 4 + ch % 4
                R = raw[ch % 2]
                b1 = a1[ch % 2]
                cw = lambda tap, gch=gch: convw.v(gch * 4 + tap, gch * 4 + tap + 1)
                cp(P, "pool", R.v(0, 3), halo.v(ch * 4, ch * 4 + 3))
                cp(P, "act", R.v(3, 259), bank.v(0, 256))
                yield
                cp(P, "pool", halo.v(ch * 4, ch * 4 + 3), R.v(256, 259))
                act(P, b1.whole(), R.v(0, 256), AF.Identity, scale=cw(0), bias=zb0.whole())
                stt(P, b1.whole(), R.v(1, 257), cw(1), b1.whole(), ALU.mult, ALU.add)
                yield
                stt(P, b1.whole(), R.v(2, 258), cw(2), b1.whole(), ALU.mult, ALU.add)
                stt(P, b1.whole(), R.v(3, 259), cw(3), b1.whole(), ALU.mult, ALU.add)
                act(P, qkvT.v(ch * 256, (ch + 1) * 256), b1.whole(), AF.Silu)
                yield
            for pr in range(4):
                reg = qkvT.v(pr * 512, (pr + 1) * 512)
                act(P, sq.whole(), reg, AF.Square)
                bank = ps[6 + pr % 2]
                for u in range(2):
                    mm(P, bank.v(u * 256, (u + 1) * 256), ones_bf, sq.v(u * 256, (u + 1) * 256))
                yield
                act(P, lnr.whole(), bank.whole(), AF.Ln, bias=epsb.whole())
                act(P, lnr.whole(), lnr.whole(), AF.Exp, scale=-0.5, bias=(lqb.whole() if pr < 2 else zb0.whole()))
                yield
                tt(P, "dve", reg, reg, lnr.whole(), ALU.mult)
                yield

        def stage_b(s):
            qkvT, sgg, gab = qkvTs[s % 2], sggs[s % 2], gabs[s % 2]
            gens = [twork(0, tbs[0], qkvT, gab), twork(1, tbs[1], qkvT, gab)]
            for _ in range(TW_STAGGER):
                next(gens[0])
                yield
            while gens:
                for g_ in list(gens):
                    try:
                        next(g_)
                    except StopIteration:
                        gens.remove(g_)
                yield
            g0 = chain(0, tbs[0], 2 * s, qkvT, sgg)
            g1 = chain(1, tbs[1], 2 * s + 1, qkvT, sgg)
            for _ in range(6):
                next(g0)
                yield
            cg = [g0, g1]
            while cg:
                for g_ in list(cg):
                    try:
                        next(g_)
                    except StopIteration:
                        cg.remove(g_)
                yield

        ns = nt // 2
        p2_loads(0)
        if ns > 1:
            p2_loads(1)
        interleave([stage_a(0)])
        for s in range(ns):
            if s + 2 < ns:
                p2_loads(s + 2)
            gb_ = stage_b(s)
            ga_ = stage_a(s + 1) if s + 1 < ns else None
            rnd = 0
            while gb_ is not None or ga_ is not None:
                if gb_ is not None:
                    try:
                        next(gb_)
                    except StopIteration:
                        gb_ = None
                if ga_ is not None and (rnd >= A_DELAY or gb_ is None):
                    try:
                        next(ga_)
                    except StopIteration:
                        ga_ = None
                rnd += 1
        P.barrier()

    def pass3():
        A.off = base_mark
        AR1 = Arena(None, R1f.shape[1], ap=R1f)
        AR2 = Arena(None, R2f.shape[1], ap=R2f)
        Wo = AR1.alloc("Wo", 16 * D, BF16)
        gbc = AR2.alloc("gbc", D, F32)
        wst = [AR2.alloc("wst%d" % i, D, F32) for i in range(2)]
        xs3 = [AR2.alloc("x3_%d" % i, D, F32) for i in range(2)]
        mts = [AR2.alloc("mts%d" % i, D, BF16) for i in range(2)]
        zb = [AR2.alloc("zb%d" % i, D, F32) for i in range(2)]
        lnw = AR1.alloc("lnw", D, F32)
        lnb = AR1.alloc("lnb", D, F32)
        st3 = A.alloc("st3", 24, F32)
        mv3 = A.alloc("mv3", 2, F32)
        rs3 = A.alloc("rs3", 1, F32)
        nm3 = A.alloc("nm3", 1, F32)
        dma(P, lnw.whole(), lnw_d.partition_broadcast(128))
        dma(P, lnb.whole(), lnb_d.partition_broadcast(128))
        for g in range(4):
            bank = ps[g % 2]
            mm(P, bank.whole(), cf.v(C_ONES, C_ONES + 128, 0, 1), grow.v(g * 512, (g + 1) * 512, 0, 1))
            cp(P, "act", gbc.v(g * 512, (g + 1) * 512), bank.whole())
        for k in range(16):
            for hf in range(2):
                j = (2 * k + hf) % 4
                stg_ = wst[j // 2].v((j % 2) * 1024, (j % 2 + 1) * 1024)
                dma(P, stg_, wout_d[k * 128:(k + 1) * 128, hf * 1024:(hf + 1) * 1024])
                tt(P, "dve", Wo.v(k * D + hf * 1024, k * D + (hf + 1) * 1024), stg_, gbc.v(hf * 1024, (hf + 1) * 1024), ALU.mult)
        def p3_loads(t):
            dma(P, xs3[t % 2].whole(), x_d[t * 128:(t + 1) * 128, :])
            dma(P, mts[t % 2].whole(), mT_d[t])

        p3_loads(0)
        for t in range(nt):
            sl = t % 2
            if t + 1 < nt:
                p3_loads(t + 1)
            z = zb[sl]
            for n in range(4):
                bank = ps[(t % 2) * 4 + n]
                for k in range(16):
                    mm(P, bank.whole(), mts[sl].v(k * 128, (k + 1) * 128), Wo.v(k * D + n * 512, k * D + (n + 1) * 512),
                       start=(k == 0), stop=(k == 15))
                stt(P, z.v(n * 512, (n + 1) * 512), xs3[sl].v(n * 512, (n + 1) * 512), float(ALPHA), bank.whole(), ALU.mult, ALU.add)
                P.op("dve", lambda e, n=n, z=z: e.bn_stats(out=st3.t[:, n * 6:(n + 1) * 6], in_=z.t[:, n * 512:(n + 1) * 512]),
                     outs=[st3.v(n * 6, (n + 1) * 6)], ins=[z.v(n * 512, (n + 1) * 512)])
            P.op("dve", lambda e: e.bn_aggr(out=mv3.t[:, :], in_=st3.t[:, :]), outs=[mv3.whole()], ins=[st3.whole()])
            ts(P, "pool", rs3.whole(), mv3.v(1, 2), LN_EPS, ALU.add)
            tt(P, "pool", rs3.whole(), rs3.whole(), mhalf.v(0, 1), ALU.pow)
            tt(P, "pool", nm3.whole(), mv3.v(0, 1), rs3.whole(), ALU.mult)
            ts(P, "pool", nm3.whole(), nm3.whole(), -1.0, ALU.mult)
            act(P, z.whole(), z.whole(), AF.Identity, scale=rs3.whole(), bias=nm3.whole())
            tt(P, "dve", z.whole(), z.whole(), lnw.whole(), ALU.mult)
            tt(P, "dve", z.whole(), z.whole(), lnb.whole(), ALU.add)
            dma(P, out_d[t * 128:(t + 1) * 128, :], z.whole())

    AR2 = Arena(None, R2f.shape[1], ap=R2f)
    xs = [AR2.alloc("xs%d" % i, D, F32) for i in range(2)]
    hT1 = [AR2.alloc("hT1_%d" % i, D, BF16) for i in range(2)]
    rp = [AR2.alloc("rp%d" % i, 256, F32) for i in range(2)]
    qs = AR2.alloc("qs", 512, F32)
    ks = AR2.alloc("ks", 512, F32)
    tA = AR2.alloc("tA", 512, F32)
    tB = AR2.alloc("tB", 512, F32)
    qhat = AR2.alloc("qhat", 1024, BF16)
    qkT = AR2.alloc("qkT", 1024, BF16)
    vbf = AR2.alloc("vbf", 1024, BF16)
    sg = AR2.alloc("sg", 1024, F32)
    sT = AR2.alloc("sT", 512, BF16)
    S32 = AR2.alloc("S32", 1024, F32)
    Sbf = AR2.alloc("Sbf", 1024, BF16)
    yn = AR2.alloc("yn", 1024, F32)
    retb = AR2.alloc("retb", 1024, BF16)
    mst = AR2.alloc("mst", 1024, BF16)
    gnw = A.alloc("gnw", RW, F32)
    gnb = A.alloc("gnb", RW, F32)
    st1 = A.alloc("st1", 24, F32)
    mv1 = A.alloc("mv1", 8, F32)
    rs1 = A.alloc("rs1", 4, F32)
    nm1 = A.alloc("nm1", 4, F32)
    KS = int(os.environ.get("KSUB", "0"))
    if not KS & 1:
        dma(P, gnw.whole(), gnw_d.partition_broadcast(128))
        dma(P, gnb.whole(), gnb_d.partition_broadcast(128))
    if not KS & 2:
        P.op("pool", lambda e: e.memset(S32.t[:, :], 0.0), outs=[S32.whole()])
        P.op("pool", lambda e: e.memset(Sbf.t[:, :], 0.0), outs=[Sbf.whole()])

    def make_hT(xbuf, hT):
        for g4 in range(4):
            bank = ps[g4 % 2]
            for kk in range(4):
                k = g4 * 4 + kk
                tr(P, bank.v(kk * 128, (kk + 1) * 128), xbuf.v(k * 128, (k + 1) * 128), idf)
            for kk in range(4):
                k = g4 * 4 + kk
                dst = hT.v(k * 128, (k + 1) * 128)
                src = bank.v(kk * 128, (kk + 1) * 128)
                KHT = int(os.environ.get("KHT", "0"))
                if (g4 % 2 == 0 and KHT == 0) or KHT == 1:
                    act(P, dst, src, AF.Identity, scale=sc1.v(k, k + 1), bias=modc.v(k, k + 1))
                else:
                    ts(P, "dve", dst, src, sc1.v(k, k + 1), ALU.mult, modc.v(k, k + 1), ALU.add)

    def rotary(src, dst, rpb):
        a = src.whole().ap
        rot = src.whole().with_ap(bass.AP(tensor=a.tensor, offset=a.offset + 64,
                                          ap=[list(a.ap[0]), [128, 4], [-64, 2], [1, 64]]))
        c2 = bcast_mid(rpb.v(0, 128), 4, 128)
        s = rpb.v(128, 256).ap
        s2 = rpb.v(128, 256).with_ap(bass.AP(tensor=s.tensor, offset=s.offset, ap=[list(s.ap[0]), [0, 4], [64, 2], [1, 64]]))
        tt(P, "dve", r3(tA.whole(), 4), r3(src.whole(), 4), c2, ALU.mult)
        tt(P, "pool", tB.whole().r("p (h t d) -> p h t d", h=4, t=2), rot, s2, ALU.mult)
        tt(P, "dve", dst, tA.whole(), tB.whole(), ALU.add)

    def p1_loads(t):
        dma(P, xs[t % 2].whole(), x_d[t * 128:(t + 1) * 128, :])
        dma(P, rp[t % 2].whole(), rope_d[t * 128:(t + 1) * 128, :])

    qhat2 = Buf("qhat2", wstg[0].t[:, 0:512].bitcast(BF16))
    vbf2 = Buf("vbf2", wstg[0].t[:, 512:1024].bitcast(BF16))
    sg2 = Buf("sg2", wstg[1].t[:, 0:1024])
    qhats, vbfs, sgs = [qhat, qhat2], [vbf, vbf2], [sg, sg2]

    def p1_front(t):
        sl = t % 2
        H = hT1[sl]
        qh, vv, sgb = qhats[sl], vbfs[sl], sgs[sl]
        for g4 in range(4):
            bank = ps[g4 % 2]
            for kk in range(4):
                k = g4 * 4 + kk
                tr(P, bank.v(kk * 128, (kk + 1) * 128), xs[sl].v(k * 128, (k + 1) * 128), idf)
            for kk in range(4):
                k = g4 * 4 + kk
                dst = H.v(k * 128, (k + 1) * 128)
                src = bank.v(kk * 128, (kk + 1) * 128)
                if g4 % 2 == 0:
                    act(P, dst, src, AF.Identity, scale=sc1.v(k, k + 1), bias=modc.v(k, k + 1))
                else:
                    ts(P, "dve", dst, src, sc1.v(k, k + 1), ALU.mult, modc.v(k, k + 1), ALU.add)
            yield
        dma(P, hT_d[t], H.whole())
        for n in range(6):
            bank = ps[2 + n % 2]
            for k in range(16):
                mm(P, bank.whole(), H.v(k * 128, (k + 1) * 128), R1.v(k * 3072 + n * 512, k * 3072 + (n + 1) * 512),
                   start=(k == 0), stop=(k == 15))
            if n == 0:
                tt(P, "dve", qs.whole(), bank.whole(), cf.v(C_GQ, C_GQ + 512), ALU.mult)
                yield
                rotary(qs, qh.v(0, 512), rp[sl])
            elif n == 1:
                tt(P, "dve", ks.whole(), bank.whole(), cf.v(C_GK, C_GK + 512), ALU.mult)
                yield
                rotary(ks, qh.v(512, 1024), rp[sl])
            elif n < 4:
                cp(P, "act", vv.v((n - 2) * 512, (n - 1) * 512), bank.whole())
            else:
                act(P, sgb.v((n - 4) * 512, (n - 3) * 512), bank.whole(), AF.Silu)
            yield

    def p1_back(t):
        sl = t % 2
        qh, vv, sgb = qhats[sl], vbfs[sl], sgs[sl]
        for m in range(8):
            tr(P, vb(ps[4], m * 128, (m + 1) * 128), qh.v(m * 128, (m + 1) * 128), idb)
        yield
        cp(P, "dve", qkT.whole(), vb(ps[4], 0, 1024))
        yield
        for h in range(4):
            mm(P, ps[5].v(h * 128, (h + 1) * 128), qkT.v(512 + h * 128, 512 + (h + 1) * 128), qkT.v(h * 128, (h + 1) * 128))
        yield
        tt(P, "dve", sT.whole(), ps[5].whole(), cf.v(C_RMASK, C_RMASK + 512), ALU.mult)
        yield
        obs = []
        for h in range(4):
            ob = ps[6 + h // 2].v((h % 2) * 256, (h % 2 + 1) * 256)
            obs.append(ob)
            mm(P, ob, sT.v(h * 128, (h + 1) * 128), vv.v(h * 256, (h + 1) * 256), start=True, stop=False)
            mm(P, ob, qkT.v(h * 128, (h + 1) * 128), Sbf.v(h * 256, (h + 1) * 256), start=False, stop=True)
        ubs = []
        for h in range(4):
            ub = ps[5 - h // 2].v((h % 2) * 256, (h % 2 + 1) * 256)
            ubs.append(ub)
            mm(P, ub, qh.v(512 + h * 128, 512 + (h + 1) * 128), vv.v(h * 256, (h + 1) * 256))
        yield
        for h in range(4):
            P.op("dve", lambda e, h=h: e.bn_stats(out=st1.t[:, h * 6:(h + 1) * 6], in_=obs[h].ap),
                 outs=[st1.v(h * 6, (h + 1) * 6)], ins=[obs[h]])
        for h in range(4):
            stt(P, S32.v(h * 256, (h + 1) * 256), S32.v(h * 256, (h + 1) * 256), float(GAMMAS[h] ** 128), ubs[h], ALU.mult, ALU.add)
        yield
        for h in range(4):
            P.op("dve", lambda e, h=h: e.bn_aggr(out=mv1.t[:, h * 2:(h + 1) * 2], in_=st1.t[:, h * 6:(h + 1) * 6]),
                 outs=[mv1.v(h * 2, (h + 1) * 2)], ins=[st1.v(h * 6, (h + 1) * 6)])
        cp(P, "act", Sbf.whole(), S32.whole())
        yield
        mvv = mv1.whole().r("p (h two) -> p h two", two=2)
        ts(P, "pool", rs1.whole(), mvv[:, :, 1], GN_EPS, ALU.add)
        yield
        tt(P, "pool", rs1.whole(), rs1.whole(), mhalf.v(0, 4), ALU.pow)
        yield
        tt(P, "pool", nm1.whole(), mvv[:, :, 0], rs1.whole(), ALU.mult)
        yield
        ts(P, "pool", nm1.whole(), nm1.whole(), -1.0, ALU.mult)
        yield
        for h in range(4):
            act(P, yn.v(h * 256, (h + 1) * 256), obs[h], AF.Identity, scale=rs1.v(h, h + 1), bias=nm1.v(h, h + 1))
        yield
        tt(P, "pool", yn.whole(), yn.whole(), gnw.whole(), ALU.mult)
        yield
        tt(P, "dve", yn.whole(), yn.whole(), gnb.whole(), ALU.add)
        yield
        tt(P, "pool", retb.whole(), yn.whole(), sgb.whole(), ALU.mult)
        yield
        for m in range(8):
            tr(P, vb(ps[4], m * 128, (m + 1) * 128), retb.v(m * 128, (m + 1) * 128), idb)
        yield
        cp(P, "act", mst.whole(), vb(ps[4], 0, 1024))
        dma(P, mT_d[t, :, 0:1024], mst.whole())
        yield

    def interleave(gens):
        gens = list(gens)
        while gens:
            for g in list(gens):
                try:
                    next(g)
                except StopIteration:
                    gens.remove(g)

    if stop_after >= 1:
        p1_loads(0)
        if nt > 1:
            p1_loads(1)
        interleave([p1_front(0)])
        for t in range(nt):
            if t + 2 < nt:
                p1_loads(t + 2)
            gl = [p1_back(t)]
            if t + 1 < nt:
                gl.append(p1_front(t + 1))
            interleave(gl)
    P.barrier()
    A.off = base_mark
    if stop_after >= 2:
        for hg in range(2):
            gdn_pass(hg)
    if stop_after >= 3:
        pass3()
    return nc, P, A, ps, locals()


def finish(nc, P):
    P.final_wait()
    P.finalize()
    sems = {te: nc.alloc_semaphore("s_" + te) for te in P.teops}
    P.emit(sems)
    return nc


_CONSTS = None


def prep_core_inputs(inp, b):
    global _CONSTS
    if _CONSTS is None:
        _CONSTS = host_consts()
    cf, cbt, rope = _CONSTS
    f = lambda a: np.ascontiguousarray(np.asarray(a, dtype=np.float32))
    b_ada = np.asarray(inp["b_ada"])[0]
    conv = np.asarray(inp["gdn_conv_w"])[0]
    return {
        "x": f(np.asarray(inp["x"])[b]),
        "ccol": f(np.asarray(inp["c"])[b].reshape(16, 128).T),
        "wada": f(np.asarray(inp["w_ada"])[0]),
        "badac": f(b_ada[:4096].reshape(32, 128).T),
        "badag": f(b_ada[4096:].reshape(1, D)),
        "win": f(np.asarray(inp["w_in"])[0]),
        "convw": f(conv.reshape(4, 24, 128).transpose(2, 1, 0).reshape(128, 96)),
        "alog": f(np.asarray(inp["gdn_a_log"])[0].reshape(1, 8)),
        "dtb": f(np.asarray(inp["gdn_dt_bias"])[0].reshape(1, 8)),
        "gnw": f(np.asarray(inp["ret_gn_w"])[0].reshape(1, RW)),
        "gnb": f(np.asarray(inp["ret_gn_b"])[0].reshape(1, RW)),
        "gdnw": f(np.tile(np.asarray(inp["gdn_norm_w"])[0], 8).reshape(1, GW)),
        "wout": f(np.asarray(inp["w_out"])[0]),
        "lnw": f(np.asarray(inp["ln_w"])[0].reshape(1, D)),
        "lnb": f(np.asarray(inp["ln_b"])[0].reshape(1, D)),
        "cf": cf, "cb": cbt, "rope": rope,
    }


def kernel(**inputs):
    nc, P, A, ps, _ = build(nt=NTILE, stop_after=3, debug=False)
    finish(nc, P)
    in_maps = [prep_core_inputs(inputs, b) for b in range(8)]
    res = run_bass_kernel_spmd(nc, in_maps, core_ids=list(range(8)))
    out = np.stack([np.asarray(r["out"], dtype=np.float32) for r in res.results], axis=0)
    return out.astype(np.asarray(inputs["x"]).dtype)
```

```python
import os
import numpy as np
import ml_dtypes
import concourse.bass as bass
import concourse.mybir as mybir
from concourse.bass_utils import run_bass_kernel_spmd

F32 = mybir.dt.float32
BF16 = mybir.dt.bfloat16
AF = mybir.ActivationFunctionType
ALU = mybir.AluOpType
AX = mybir.AxisListType


class Buf:
    def __init__(self, name, t, psum=False):
        self.name = name
        self.t = t
        self.psum = psum
        self.reads = {}
        self.writes = {}

    def v(self, lo, hi, p0=0, p1=None):
        if p1 is None:
            p1 = self.t.shape[0]
        if self.psum:
            return V(self.t[p0:p1, lo:hi], self, 0, 512)
        return V(self.t[p0:p1, lo:hi], self, lo, hi)

    def whole(self):
        return self.v(0, self.t.shape[1])


class V:
    def __init__(self, ap, buf, lo, hi):
        self.ap = ap
        self.buf = buf
        self.lo = lo
        self.hi = hi

    def r(self, pattern, **kw):
        return V(self.ap.rearrange(pattern, **kw), self.buf, self.lo, self.hi)

    def __getitem__(self, idx):
        return V(self.ap[idx], self.buf, self.lo, self.hi)

    def with_ap(self, ap):
        return V(ap, self.buf, self.lo, self.hi)


class Op:
    __slots__ = ("q", "te", "seq", "deps", "fn", "waits", "signal", "ndma")


class Prog:
    QUEUES = ("pe", "act", "dve", "pool", "sp")

    def __init__(self, nc, n_lanes=12):
        self.nc = nc
        self.qops = {q: [] for q in self.QUEUES}
        self.teops = {q: [] for q in self.QUEUES}
        self.lanes = ["L%d" % i for i in range(n_lanes)]
        for l in self.lanes:
            self.teops[l] = []
        self.next_lane = 0

    def _track(self, op, outs, ins):
        deps = {}

        def need(te, seq):
            if te == op.te:
                return
            if deps.get(te, -1) < seq:
                deps[te] = seq

        raw_same = -1
        for v in ins:
            for (te, lo, hi), seq in v.buf.writes.items():
                if lo < v.hi and v.lo < hi:
                    if te == op.te:
                        raw_same = max(raw_same, seq)
                    else:
                        need(te, seq)
            if v.buf.psum:
                for (te, lo, hi), seq in v.buf.reads.items():
                    need(te, seq)
        for v in outs:
            for d in (v.buf.writes, v.buf.reads):
                for (te, lo, hi), seq in d.items():
                    if lo < v.hi and v.lo < hi:
                        if te == op.te:
                            raw_same = max(raw_same, seq)
                        else:
                            need(te, seq)
        if raw_same >= 0 and op.te in ("act", "dve", "pool"):
            deps[op.te] = raw_same
        op.deps = deps
        for v in outs:
            for d in (v.buf.writes, v.buf.reads):
                for k in [k for k in d if v.lo <= k[1] and k[2] <= v.hi]:
                    del d[k]
            v.buf.writes[(op.te, v.lo, v.hi)] = op.seq
        for v in ins:
            v.buf.reads[(op.te, v.lo, v.hi)] = op.seq

    def op(self, q, fn, outs=(), ins=()):
        o = Op()
        o.q = q
        o.te = q
        o.seq = len(self.teops[q])
        o.fn = fn
        o.ndma = 0
        self._track(o, outs, ins)
        self.teops[q].append(o)
        self.qops[q].append(o)
        return o

    def dma(self, fns, outs=(), ins=(), q="sp"):
        lane = self.lanes[self.next_lane]
        self.next_lane = (self.next_lane + 1) % len(self.lanes)
        o = Op()
        o.q = q
        o.te = lane
        o.seq = len(self.teops[lane])
        o.fn = fns
        o.ndma = len(fns)
        self._track(o, outs, ins)
        if o.seq > 0:
            o.deps[lane] = o.seq - 1
        self.teops[lane].append(o)
        self.qops[q].append(o)
        return o

    def final_wait(self):
        o = Op()
        o.q = "sp"
        o.te = "sp"
        o.seq = len(self.teops["sp"])
        o.fn = None
        o.ndma = 0
        o.deps = {te: len(ops) - 1 for te, ops in self.teops.items() if te.startswith("L") and ops}
        self.teops["sp"].append(o)
        self.qops["sp"].append(o)

    def finalize(self):
        for q in self.QUEUES:
            seen = {}
            for o in self.qops[q]:
                w = []
                for te, seq in o.deps.items():
                    if seen.get(te, -1) < seq:
                        seen[te] = seq
                        w.append((te, seq))
                o.waits = w
        for te, ops in self.teops.items():
            for o in ops:
                o.signal = te.startswith("L")
        for q in self.QUEUES:
            for o in self.qops[q]:
                for te, seq in o.waits:
                    self.teops[te][seq].signal = True
        self.val = {}
        for te, ops in self.teops.items():
            c = 0
            vals = []
            for o in ops:
                if te.startswith("L"):
                    c += 16 * o.ndma
                elif o.signal:
                    c += 1
                vals.append(c)
            self.val[te] = vals

    def emit(self, sems):
        nc = self.nc
        with nc.Block() as block:

            def mk(q):
                def body(eng):
                    for o in self.qops[q]:
                        for te, seq in o.waits:
                            eng.wait_ge(sems[te], self.val[te][seq])
                        if o.ndma:
                            for f in o.fn:
                                f(eng).then_inc(sems[o.te], 16)
                        elif o.fn is not None:
                            ins = o.fn(eng)
                            if o.signal:
                                ins.then_inc(sems[o.te], 1)

                return body

            block.tensor(mk("pe"))
            block.scalar(mk("act"))
            block.vector(mk("dve"))
            block.gpsimd(mk("pool"))
            block.sync(mk("sp"))

    def barrier(self):
        last = {}
        for te, ops in self.teops.items():
            i = len(ops) - 1
            while i >= 0 and ops[i].fn is None:
                i -= 1
            if i >= 0:
                last[te] = i
        for q in self.QUEUES:
            o = Op()
            o.q = q
            o.te = q
            o.seq = len(self.teops[q])
            o.fn = None
            o.ndma = 0
            o.deps = {te: s for te, s in last.items() if te != q}
            self.teops[q].append(o)
            self.qops[q].append(o)


def _ap(x):
    return x.ap if isinstance(x, V) else x


def _vs(*xs):
    return [x for x in xs if isinstance(x, V)]


def bcast_mid(v, n, inner):
    a = v.ap
    return v.with_ap(bass.AP(tensor=a.tensor, offset=a.offset, ap=[list(a.ap[0]), [0, n], [1, inner]]))


def bcast_last(v, n, inner):
    a = v.ap
    return v.with_ap(bass.AP(tensor=a.tensor, offset=a.offset, ap=[list(a.ap[0]), [a.ap[1][0], n], [0, inner]]))


def mm(P, out, lhsT, rhs, start=True, stop=True):
    return P.op("pe", lambda e: e.matmul(out=out.ap, lhsT=lhsT.ap, rhs=rhs.ap, start=start, stop=stop),
                outs=[out], ins=[lhsT, rhs])


def tr(P, out, in_, ident):
    return P.op("pe", lambda e: e.transpose(out=out.ap, in_=in_.ap, identity=ident.ap), outs=[out], ins=[in_, ident])


def act(P, out, in_, func, scale=1.0, bias=0.0):
    return P.op("act", lambda e: e.activation(out=out.ap, in_=in_.ap, func=func, scale=_ap(scale), bias=_ap(bias)),
                outs=[out], ins=[in_] + _vs(scale, bias))


def tt(P, q, out, a, b, op):
    return P.op(q, lambda e: e.tensor_tensor(out=out.ap, in0=a.ap, in1=b.ap, op=op), outs=[out], ins=[a, b])


def ts(P, q, out, a, s1, op0, s2=None, op1=None):
    if op1 is None:
        return P.op(q, lambda e: e.tensor_scalar(out=out.ap, in0=a.ap, scalar1=_ap(s1), scalar2=None, op0=op0),
                    outs=[out], ins=[a] + _vs(s1))
    return P.op(q, lambda e: e.tensor_scalar(out=out.ap, in0=a.ap, scalar1=_ap(s1), scalar2=_ap(s2), op0=op0, op1=op1),
                outs=[out], ins=[a] + _vs(s1, s2))


def stt(P, out, a, s, b, op0, op1):
    return P.op("dve", lambda e: e.scalar_tensor_tensor(out=out.ap, in0=a.ap, scalar=_ap(s), in1=b.ap, op0=op0, op1=op1),
                outs=[out], ins=[a, b] + _vs(s))


def cp(P, q, out, in_):
    if q == "act":
        return P.op("act", lambda e: e.copy(out=out.ap, in_=in_.ap), outs=[out], ins=[in_])
    return P.op(q, lambda e: e.tensor_copy(out=out.ap, in_=in_.ap), outs=[out], ins=[in_])


def dma(P, out, in_, q="sp", outs=None, ins=None, **kw):
    o = out.ap if isinstance(out, V) else out
    i = in_.ap if isinstance(in_, V) else in_
    return P.dma([lambda e: e.dma_start(out=o, in_=i, **kw)], outs=_vs(out) if outs is None else outs,
                 ins=_vs(in_) if ins is None else ins, q=q)


T = 4096
D = 2048
NTILE = T // 128
RQW = 512
RW = 1024
GW = 1024
IN_COLS = 7184
RET_COLS = 3072
GDN_OFF = 3072
GDN_COLS = 4112
GN_EPS = 1e-5
RMS_EPS = 1e-6
LN_EPS = 1e-5
ALPHA = 2.0 ** 0.25
GAMMAS = [1.0 - 2.0 ** (-5.0 - h) for h in range(4)]

C_ID, C_TRIU, C_ONES, C_GQ, C_GK, C_RMASK, C_NSTRICT, C_D8, C_M8, C_M16, C_M32, C_M64, C_END = 0, 128, 256, 384, 896, 1408, 1920, 2048, 2176, 2304, 2432, 2560, 2688
B_ID, B_ONES, B_TRIU, B_NEG, B_NTRIU, B_END = 0, 128, 256, 384, 512, 640


def host_consts():
    p = np.arange(128)
    cf = np.zeros((128, C_END), np.float32)
    cf[:, C_ID:C_ID + 128] = np.eye(128)
    cf[:, C_TRIU:C_TRIU + 128] = (p[:, None] <= p[None, :])
    cf[:, C_ONES:C_ONES + 128] = 1.0
    for h, g in enumerate(GAMMAS):
        lg = np.log(np.float64(g))
        cf[:, C_GQ + h * 128:C_GQ + (h + 1) * 128] = np.exp(lg * (p + 1.0))[:, None]
        cf[:, C_GK + h * 128:C_GK + (h + 1) * 128] = (np.exp(lg * (127.0 - p)) * 128.0 ** -0.5)[:, None]
        cf[:, C_RMASK + h * 128:C_RMASK + (h + 1) * 128] = np.where(p[None, :] >= p[:, None], np.exp(-lg * 128.0), 0.0)
    cf[:, C_NSTRICT:C_NSTRICT + 128] = np.where(p[None, :] > p[:, None], -1.0, 0.0)
    cf[:, C_D8:C_D8 + 128] = (p[:, None] // 8 == p[None, :] // 8)
    for col, b in ((C_M8, 8), (C_M16, 16), (C_M32, 32), (C_M64, 64)):
        cf[:, col:col + 128] = (p[:, None] // (2 * b) == p[None, :] // (2 * b)) & (p[:, None] // b != p[None, :] // b)
    cb = np.zeros((128, B_END), np.float32)
    cb[:, B_ID:B_ID + 128] = np.eye(128)
    cb[:, B_ONES:B_ONES + 128] = 1.0
    cb[:, B_TRIU:B_TRIU + 128] = (p[:, None] <= p[None, :])
    cb[:, B_NEG:B_NEG + 128] = np.where(p[None, :] < p[:, None], -30000.0, 0.0)
    cb[:, B_NTRIU:B_NTRIU + 128] = -1.0 * (p[:, None] <= p[None, :])
    cb = cb.astype(ml_dtypes.bfloat16)
    inv_freq = (1.0 / (np.float32(10000.0) ** (np.arange(0, 128, 2, dtype=np.float32) / np.float32(128)))).astype(np.float32)
    ang = (np.arange(T, dtype=np.float32)[:, None] * inv_freq[None, :]).astype(np.float32)
    cos = np.cos(ang).astype(np.float32)
    sin = np.sin(ang).astype(np.float32)
    rope = np.concatenate([cos, cos, -sin, sin], axis=1).astype(np.float32)
    return cf, cb, np.ascontiguousarray(rope)


class Arena:
    def __init__(self, nc, words, ap=None):
        self.t = nc.alloc_sbuf_tensor("arena", [128, words], F32) if ap is None else ap
        self.words = words
        self.off = 0

    def alloc(self, name, n, dt):
        w = n if dt == F32 else (n + 1) // 2
        w = (w + 7) // 8 * 8
        assert self.off + w <= self.words, ("SBUF arena overflow", name, self.off, w)
        ap = self.t[:, self.off:self.off + w]
        self.off += w
        if dt != F32:
            ap = ap.bitcast(dt)
        return Buf(name, ap[:, 0:n])


def vb(buf, lo, hi):
    ap = buf.t[:, :].bitcast(BF16)[:, lo:hi]
    return V(ap, buf, 0, 512)


def r3(v, h):
    return v.r("p (h e) -> p h e", h=h)


def build(nt=NTILE, stop_after=99, debug=False):
    TW_STAGGER = int(os.environ.get("TW_STAGGER", "0"))
    A_DELAY = int(os.environ.get("A_DELAY", "10"))
    nc = bass.Bass("TRN2", target_bir_lowering=False)

    def din(name, shape, dtype=F32):
        return nc.dram_tensor(name, shape, dtype, kind="ExternalInput").ap()

    x_d = din("x", [T, D])
    ccol_d = din("ccol", [128, 16])
    wada_d = din("wada", [D, 3 * D])
    badac_d = din("badac", [128, 32])
    badag_d = din("badag", [1, D])
    win_d = din("win", [D, IN_COLS])
    convw_d = din("convw", [128, 96])
    alog_d = din("alog", [1, 8])
    dtb_d = din("dtb", [1, 8])
    gnw_d = din("gnw", [1, RW])
    gnb_d = din("gnb", [1, RW])
    gdnw_d = din("gdnw", [1, GW])
    wout_d = din("wout", [D, D])
    lnw_d = din("lnw", [1, D])
    lnb_d = din("lnb", [1, D])
    cf_d = din("cf", [128, C_END])
    cb_d = din("cb", [128, B_END], BF16)
    rope_d = din("rope", [T, 256])
    out_d = nc.dram_tensor("out", [T, D], F32, kind="ExternalOutput").ap()
    skind = "ExternalOutput" if debug else "Internal"
    mT_d = nc.dram_tensor("mT", [NTILE, 128, D], BF16, kind=skind).ap()
    hT_d = nc.dram_tensor("hTs", [NTILE, 128, D], BF16, kind=skind).ap()

    P = Prog(nc)
    A = Arena(nc, 51968)
    ps = [Buf("ps%d" % i, nc.alloc_psum_tensor("ps%d" % i, [128, 512], F32), psum=True) for i in range(8)]

    cf = A.alloc("cf", C_END, F32)
    cb = A.alloc("cb", B_END, BF16)
    dma(P, cf.whole(), cf_d)
    dma(P, cb.whole(), cb_d)
    idf = cf.v(C_ID, C_ID + 128)
    idb = cb.v(B_ID, B_ID + 128)
    mhalf = A.alloc("mhalf", 8, F32)
    P.op("pool", lambda e: e.memset(mhalf.t[:, :], -0.5), outs=[mhalf.whole()])
    epsb = A.alloc("epsb", 1, F32)
    lqb = A.alloc("lqb", 1, F32)
    zb0 = A.alloc("zb0", 1, F32)
    oneb = A.alloc("oneb", 1, F32)
    P.op("pool", lambda e: e.memset(epsb.t[:, :], RMS_EPS), outs=[epsb.whole()])
    P.op("pool", lambda e: e.memset(lqb.t[:, :], float(np.log(128.0 ** -0.5))), outs=[lqb.whole()])
    P.op("pool", lambda e: e.memset(zb0.t[:, :], 0.0), outs=[zb0.whole()])
    P.op("pool", lambda e: e.memset(oneb.t[:, :], 1.0), outs=[oneb.whole()])
    modc = A.alloc("modc", 32, F32)
    sc1 = A.alloc("sc1", 16, F32)
    grow = A.alloc("grow", D, F32)
    R1 = A.alloc("R1", 16 * 3072, BF16)
    R2 = A.alloc("R2", 16 * 2056, BF16)
    base_mark = A.off
    R1f = R1.t.bitcast(F32)
    R2f = R2.t.bitcast(F32)

    wstg_off = A.off
    wstg = [A.alloc("wstg%d" % i, 1024, F32) for i in range(3)]
    base_mark = A.off
    wl_cnt = [0]
    wstg6 = [Buf("wstg6_%d" % j, A.t[:, wstg_off + j * 512:wstg_off + (j + 1) * 512]) for j in range(6)]

    def load_w(dst, dst_stride, src_cols, dst_col0=0):
        pieces = []
        for (c0, n) in src_cols:
            o = 0
            while o < n:
                m = min(512, n - o)
                pieces.append((c0 + o, m))
                o += m
        dc = dst_col0
        for (c0, m) in pieces:
            for k in range(16):
                i = wl_cnt[0]
                wl_cnt[0] += 1
                stg = wstg6[i % 6]
                dma(P, stg.v(0, m), win_d[k * 128:(k + 1) * 128, c0:c0 + m])
                cp(P, ("dve", "act")[i % 2], dst.v(k * dst_stride + dc, k * dst_stride + dc + m), stg.v(0, m))
            dc += m

    load_w(R1, 3072, [(0, RET_COLS)])

    ccol = A.alloc("ccol", 16, F32)
    scol = A.alloc("scol", 16, F32)
    badac = A.alloc("badac", 32, F32)
    dma(P, ccol.whole(), ccol_d)
    dma(P, badac.whole(), badac_d)
    dma(P, grow.v(0, D, 0, 1), badag_d)
    act(P, scol.whole(), ccol.whole(), AF.Silu)
    rowb = [A.alloc("rowb%d" % i, 512, F32) for i in range(2)]
    AR2 = Arena(None, R2f.shape[1], ap=R2f)
    wab = [AR2.alloc("wab%d" % i, 16 * 512, F32) for i in range(2)]
    for blk in range(12):
        wb = wab[blk % 2]
        src = wada_d[:, blk * 512:(blk + 1) * 512].rearrange("(k p) n -> p k n", p=128)
        fns = []
        for k0 in range(0, 16, 4):
            o_ap = wb.t[:, k0 * 512:(k0 + 4) * 512].rearrange("p (k n) -> p k n", k=4)
            i_ap = src[:, k0:k0 + 4, :]
            fns.append(lambda e, o_ap=o_ap, i_ap=i_ap: e.dma_start(out=o_ap, in_=i_ap))
        P.dma(fns, outs=[wb.whole()])
        if blk < 8:
            bank = ps[1 + blk % 2]
            for k in range(16):
                mm(P, bank.v(0, 512, 0, 1), scol.v(k, k + 1), wb.v(k * 512, (k + 1) * 512), start=(k == 0), stop=(k == 15))
            rb = rowb[blk % 2]
            cp(P, "dve", rb.v(0, 512, 0, 1), bank.v(0, 512, 0, 1))
            for jj in range(4):
                j = blk * 4 + jj
                mm(P, ps[0].v(j, j + 1), rb.v(jj * 128, (jj + 1) * 128, 0, 1), cf.v(C_ONES, C_ONES + 1, 0, 1))
        else:
            g = blk - 8
            bank = ps[1 + g % 2]
            for k in range(16):
                mm(P, bank.v(0, 512, 0, 1), scol.v(k, k + 1), wb.v(k * 512, (k + 1) * 512), start=(k == 0), stop=(k == 15))
            tt(P, "dve", grow.v(g * 512, (g + 1) * 512, 0, 1), bank.v(0, 512, 0, 1), grow.v(g * 512, (g + 1) * 512, 0, 1), ALU.add)
    tt(P, "dve", modc.whole(), ps[0].v(0, 32), badac.whole(), ALU.add)
    ts(P, "dve", sc1.whole(), modc.v(16, 32), 1.0, ALU.add)
    P.barrier()
    A.off = base_mark

    def gdn_pass(hg):
        A.off = base_mark
        AR1 = Arena(None, R1f.shape[1], ap=R1f)
        al = AR1.alloc
        W = R2
        WS = 2056
        o = GDN_OFF
        load_w(R2, WS, [(o + hg * 512, 512), (o + 1024 + hg * 512, 512), (o + 2048 + hg * 512, 512),
                        (o + 3072 + hg * 512, 512), (o + 4096 + hg * 4, 4), (o + 4104 + hg * 4, 4)])
        hT2 = [al("hT2_%d" % i, 16 * 256, BF16) for i in range(2)]
        P.barrier()
        wsall = A.t[:, wstg_off:wstg_off + 3072]
        qkvTs = [al("qkvT", 12 * 256, BF16), Buf("qkvT1", wsall[:, 0:1536].bitcast(BF16))]
        sggs = [al("sgg", 1024, F32), Buf("sgg1", wsall[:, 1536:2560])]
        gabs = [al("gab", 16, F32), al("gab1", 16, F32)]
        raw = [al("raw%d" % i, 264, F32) for i in range(2)]
        a1 = [al("a1_%d" % i, 256, F32) for i in range(2)]
        halo = al("halo", 48, F32)
        convw = al("convw", 96, F32)
        sq = al("sq", 512, BF16)
        lnr = al("lnr", 512, F32)
        nega = al("nega", 4, F32)
        dtb = al("dtb", 4, F32)
        gdnw = al("gdnw", 512, F32)
        Sg32 = al("Sg32", 512, F32)
        Sgbf = al("Sgbf", 512, BF16)
        cbufs = []
        for i in range(2):
            cbufs.append(dict(nkcdT=al("nkcdT%d" % i, 512, BF16), vnew=al("vnew%d" % i, 512, BF16),
                              o1s=al("o1s%d" % i, 512, F32), osq=al("osq%d" % i, 512, F32),
                              gyb=al("gyb%d" % i, 512, BF16), mst2=al("mst2%d" % i, 512, BF16), ss8=al("ss8%d" % i, 4, F32)))

        class TB:
            pass

        tbs = []
        for i in range(2):
            b = TB()
            for nm in ("z8", "sp8", "g8", "bt", "egc", "egl", "dgl", "etl", "bege"):
                setattr(b, nm, al("%s_%d" % (nm, i), 4, F32))
            b.e8 = al("e8_%d" % i, 8, F32)
            b.gc = al("gc_%d" % i, 8, F32)
            for nm in ("Ghi", "Glo", "dm", "ndms", "kb", "kbe", "ktl", "vbt", "kbT", "attnT", "Bd", "Ad",
                       "Q0", "Q1", "QT0", "QT1", "X0", "X1", "Ao0", "Ao1"):
                setattr(b, nm, al("%s_%d" % (nm, i), 512, BF16))
            b.TUb, b.W1s = b.Q1, b.QT1
            b.pA, b.pB, b.pC = ps[3 * i], ps[3 * i + 1], ps[3 * i + 2]
            b.pD = b.pA
            tbs.append(b)
        tbs[0].other, tbs[1].other = tbs[1], tbs[0]

        ones_bf = cb.v(B_ONES, B_ONES + 128)
        triu_bf = cb.v(B_TRIU, B_TRIU + 128)
        ntriu_bf = cb.v(B_NTRIU, B_NTRIU + 128)
        neg_bf = cb.v(B_NEG, B_NEG + 128)
        dma(P, convw.whole(), convw_d)
        dma(P, nega.whole(), alog_d[:, hg * 4:(hg + 1) * 4].partition_broadcast(128))
        dma(P, dtb.whole(), dtb_d[:, hg * 4:(hg + 1) * 4].partition_broadcast(128))
        dma(P, gdnw.whole(), gdnw_d[:, hg * 512:(hg + 1) * 512].partition_broadcast(128))
        act(P, nega.whole(), nega.whole(), AF.Exp)
        ts(P, "dve", nega.whole(), nega.whole(), -1.0, ALU.mult)
        P.op("pool", lambda e: e.memset(halo.t[:, :], 0.0), outs=[halo.whole()])
        P.op("pool", lambda e: e.memset(Sg32.t[:, :], 0.0), outs=[Sg32.whole()])
        P.op("pool", lambda e: e.memset(Sgbf.t[:, :], 0.0), outs=[Sgbf.whole()])

        def hd(buf, h):
            return buf.v(h * 128, (h + 1) * 128)

        def pq(bank, h):
            return bank.v(h * 128, (h + 1) * 128)

        mk = lambda col: bcast_mid(cf.v(col, col + 128), 4, 128)

        def twork(c, b, qkvT, gab):
            qT = lambda h: qkvT.v(h * 256 + c * 128, h * 256 + (c + 1) * 128)
            kT = lambda h: qkvT.v((4 + h) * 256 + c * 128, (4 + h) * 256 + (c + 1) * 128)
            vT = lambda h: qkvT.v((8 + h) * 256 + c * 128, (8 + h) * 256 + (c + 1) * 128)
            ga = gab.v(c * 8, c * 8 + 4)
            gb = gab.v(c * 8 + 4, c * 8 + 8)
            tt(P, "dve", b.z8.whole(), ga, dtb.whole(), ALU.add)
            act(P, b.e8.v(0, 4), b.z8.whole(), AF.Exp)
            act(P, b.e8.v(4, 8), gb, AF.Exp, scale=-1.0)
            yield
            act(P, b.sp8.whole(), b.e8.v(0, 4), AF.Ln, bias=oneb.whole())
            ts(P, "dve", b.bt.whole(), b.e8.v(4, 8), 1.0, ALU.add)
            yield
            tt(P, "dve", b.g8.whole(), b.sp8.whole(), nega.whole(), ALU.mult)
            P.op("dve", lambda e: e.reciprocal(out=b.bt.t[:, :], in_=b.bt.t[:, :]), outs=[b.bt.whole()], ins=[b.bt.whole()])
            yield
            mm(P, b.pB.v(0, 4), cf.v(C_TRIU, C_TRIU + 128), b.g8.whole())
            mm(P, b.pB.v(4, 8), cf.v(C_ONES, C_ONES + 128), b.g8.whole())
            gbc_ = bcast_last(b.g8.whole(), 4, 128)
            cp(P, "dve", r3(b.Ghi.whole(), 4), gbc_)
            for h in range(4):
                tr(P, vb(b.pA, h * 128, (h + 1) * 128), kT(h), idb)
            yield
            cp(P, "dve", b.gc.whole(), b.pB.v(0, 8))
            tt(P, "dve", r3(b.Glo.whole(), 4), gbc_, r3(b.Ghi.whole(), 4), ALU.subtract)
            yield
            act(P, b.egc.whole(), b.gc.v(0, 4), AF.Exp)
            act(P, b.egl.whole(), b.gc.v(4, 8), AF.Exp)
            tt(P, "dve", b.dgl.whole(), b.gc.v(4, 8), b.gc.v(0, 4), ALU.subtract)
            for h in range(4):
                Db = pq(b.pB, h)
                mm(P, Db, hd(b.Ghi, h), triu_bf, start=True, stop=False)
                mm(P, Db, hd(b.Glo, h), triu_bf, start=False, stop=False)
                mm(P, Db, ntriu_bf, hd(b.Ghi, h), start=False, stop=False)
                mm(P, Db, ntriu_bf, hd(b.Glo, h), start=False, stop=False)
                mm(P, Db, idb, neg_bf, start=False, stop=True)
            yield
            act(P, b.etl.whole(), b.dgl.whole(), AF.Exp)
            tt(P, "dve", b.bege.whole(), b.bt.whole(), b.egc.whole(), ALU.mult)
            act(P, b.dm.whole(), b.pB.whole(), AF.Exp)
            k3 = r3(vb(b.pA, 0, 512), 4)
            tt(P, "dve", r3(b.kb.whole(), 4), k3, bcast_last(b.bt.whole(), 4, 128), ALU.mult)
            yield
            tt(P, "dve", r3(b.kbe.whole(), 4), k3, bcast_last(b.bege.whole(), 4, 128), ALU.mult)
            tt(P, "dve", r3(b.ktl.whole(), 4), k3, bcast_last(b.etl.whole(), 4, 128), ALU.mult)
            for h in range(4):
                tr(P, vb(b.pC, h * 128, (h + 1) * 128), hd(b.kb, h), idb)
            yield
            cp(P, "act", b.kbT.whole(), vb(b.pC, 0, 512))
            tt(P, "dve", r3(b.ndms.whole(), 4), r3(b.dm.whole(), 4), mk(C_NSTRICT), ALU.mult)
            for h in range(4):
                tr(P, vb(b.pA, h * 128, (h + 1) * 128), vT(h), idb)
            yield
            tt(P, "dve", r3(b.vbt.whole(), 4), r3(vb(b.pA, 0, 512), 4), bcast_last(b.bt.whole(), 4, 128), ALU.mult)
            for h in range(4):
                mm(P, pq(b.pC, h), kT(h), hd(b.kbT, h))
            for h in range(4):
                mm(P, pq(b.pD, h), kT(h), qT(h))
            yield
            tt(P, "dve", b.Q0.whole(), b.pC.whole(), b.ndms.whole(), ALU.mult)
            tt(P, "dve", b.attnT.whole(), b.pD.whole(), b.dm.whole(), ALU.mult)
            yield
            for h in range(4):
                tr(P, vb(b.pA, h * 128, (h + 1) * 128), hd(b.Q0, h), idb)
            tt(P, "dve", r3(b.Bd.whole(), 4), r3(b.Q0.whole(), 4), mk(C_D8), ALU.mult)
            yield
            cp(P, "act", b.QT0.whole(), vb(b.pA, 0, 512))
            tt(P, "dve", r3(b.X0.whole(), 4), r3(b.Bd.whole(), 4), mk(C_ID), ALU.add)
            yield
            tt(P, "dve", r3(b.Ad.whole(), 4), r3(b.QT0.whole(), 4), mk(C_D8), ALU.mult)
            yield
            for h in range(4):
                mm(P, pq(b.pB, h), hd(b.Ad, h), hd(b.Bd, h))
            for h in range(4):
                mm(P, pq(b.pC, h), hd(b.Bd, h), hd(b.Ad, h))
            aos = (b.Ao0, b.Ao1, b.Ao0, b.Ao1)
            mcols = (C_M8, C_M16, C_M32, C_M64)
            for li in range(2):
                tt(P, "dve", r3(aos[li].whole(), 4), r3(b.QT0.whole(), 4), mk(mcols[li]), ALU.mult)
            yield
            cp(P, "act", b.Q1.whole(), b.pB.whole())
            cp(P, "dve", b.QT1.whole(), b.pC.whole())
            yield
            for h in range(4):
                mm(P, pq(b.pD, h), hd(b.QT1, h), hd(b.X0, h))
            for h in range(4):
                mm(P, pq(b.pC, h), hd(b.Q1, h), hd(b.QT1, h))
            yield
            tt(P, "dve", b.X1.whole(), b.pD.whole(), b.X0.whole(), ALU.add)
            cp(P, "act", b.Ad.whole(), b.pC.whole())
            yield
            for h in range(4):
                mm(P, pq(b.pD, h), hd(b.Ad, h), hd(b.X1, h))
            yield
            tt(P, "dve", b.X0.whole(), b.pD.whole(), b.X1.whole(), ALU.add)
            yield
            xs_ = [b.X0, b.X1]
            cur = 0
            for li in range(4):
                U = xs_[cur]
                for h in range(4):
                    tr(P, vb(b.pA, h * 128, (h + 1) * 128), hd(U, h), idb)
                for h in range(4):
                    mm(P, pq(b.pB, h), hd(aos[li], h), hd(U, h))
                if li + 2 < 4:
                    tt(P, "dve", r3(aos[li].whole(), 4), r3(b.QT0.whole(), 4), mk(mcols[li + 2]), ALU.mult)
                yield
                cp(P, "act", b.TUb.whole(), vb(b.pA, 0, 512))
                cp(P, "act", b.W1s.whole(), b.pB.whole())
                yield
                for h in range(4):
                    mm(P, pq(b.pD, h), hd(b.TUb, h), hd(b.W1s, h))
                yield
                tt(P, "dve", xs_[1 - cur].whole(), b.pD.whole(), U.whole(), ALU.add)
                yield
                cur = 1 - cur
            b.TT = xs_[cur]
            for h in range(4):
                mm(P, pq(b.pB, h), hd(b.kbe, h), hd(b.TT, h))
            yield

        def chain(c, b, t, qkvT, sgg):
            qT = lambda h: qkvT.v(h * 256 + c * 128, h * 256 + (c + 1) * 128)
            TT = b.TT
            oA, oC = b.other.pA, b.other.pC
            cbf = cbufs[c]
            nkcdT, vnew, o1s, osq, gyb, mst2, ss8 = (cbf["nkcdT"], cbf["vnew"], cbf["o1s"], cbf["osq"], cbf["gyb"],
                                                     cbf["mst2"], cbf["ss8"])
            act(P, nkcdT.whole(), b.pB.whole(), AF.Identity, scale=-1.0)
            yield
            for h in range(4):
                pv = pq(b.pC, h)
                mm(P, pv, hd(TT, h), hd(b.vbt, h), start=True, stop=False)
                mm(P, pv, hd(nkcdT, h), hd(Sgbf, h), start=False, stop=True)
            for h in range(4):
                mm(P, pq(b.pA, h), qT(h), hd(Sgbf, h))
            yield
            cp(P, "act", vnew.whole(), b.pC.whole())
            tt(P, "dve", r3(o1s.whole(), 4), r3(b.pA.whole(), 4), bcast_last(b.egc.whole(), 4, 128), ALU.mult)
            yield
            for h in range(4):
                mm(P, pq(oA, h), hd(b.ktl, h), hd(vnew, h))
            for h in range(4):
                mm(P, pq(b.pB, h), hd(b.attnT, h), hd(vnew, h))
            yield
            for h in range(4):
                stt(P, hd(Sg32, h), hd(Sg32, h), b.egl.v(h, h + 1), pq(oA, h), ALU.mult, ALU.add)
            yield
            cp(P, "act", Sgbf.whole(), Sg32.whole())
            tt(P, "dve", o1s.whole(), b.pB.whole(), o1s.whole(), ALU.add)
            yield
            act(P, osq.whole(), o1s.whole(), AF.Square)
            yield
            P.op("dve", lambda e: e.tensor_reduce(out=ss8.t[:, :], in_=osq.t[:, :].rearrange("p (h e) -> p h e", h=4),
                                                  axis=AX.X, op=ALU.add), outs=[ss8.whole()], ins=[osq.whole()])
            yield
            ts(P, "pool", ss8.whole(), ss8.whole(), 1.0 / 128.0, ALU.mult, RMS_EPS, ALU.add)
            tt(P, "dve", osq.whole(), sgg.v(c * 512, (c + 1) * 512), gdnw.whole(), ALU.mult)
            yield
            tt(P, "pool", ss8.whole(), ss8.whole(), mhalf.v(0, 4), ALU.pow)
            yield
            tt(P, "dve", r3(o1s.whole(), 4), r3(o1s.whole(), 4), bcast_last(ss8.whole(), 4, 128), ALU.mult)
            yield
            tt(P, "dve", gyb.whole(), o1s.whole(), osq.whole(), ALU.mult)
            yield
            for h in range(4):
                tr(P, vb(oC, h * 128, (h + 1) * 128), hd(gyb, h), idb)
            yield
            cp(P, "act", mst2.whole(), vb(oC, 0, 512))
            dma(P, mT_d[t, :, 1024 + hg * 512:1024 + (hg + 1) * 512], mst2.whole())
            yield

        def p2_loads(s):
            H = hT2[s % 2]
            for c in range(2):
                t = 2 * s + c
                dst = V(H.t[:, :].rearrange("p (k w) -> p k w", k=16)[:, :, c * 128:(c + 1) * 128], H, 0, 16 * 256)
                dma(P, dst, hT_d[t].rearrange("p (k w) -> p k w", k=16))

        def stage_a(s):
            H = hT2[s % 2]
            qkvT, sgg, gab = qkvTs[s % 2], sggs[s % 2], gabs[s % 2]
            for c in range(2):
                bank = ps[6 + c]
                for k in range(16):
                    mm(P, bank.whole(), H.v(k * 256 + c * 128, k * 256 + (c + 1) * 128), W.v(k * WS + 1536, k * WS + 2048),
                       start=(k == 0), stop=(k == 15))
                act(P, sgg.v(c * 512, (c + 1) * 512), bank.whole(), AF.Silu)
                yield
            for c in range(2):
                for k in range(16):
                    mm(P, ps[6].v(c * 8, (c + 1) * 8), H.v(k * 256 + c * 128, k * 256 + (c + 1) * 128),
                       W.v(k * WS + 2048, k * WS + 2056), start=(k == 0), stop=(k == 15))
            cp(P, "dve", gab.whole(), ps[6].v(0, 16))
            yield
            for ch in range(12):
                bank = ps[6 + (ch + 1) % 2]
                for k in range(16):
                    mm(P, bank.v(0, 256), W.v(k * WS + ch * 128, k * WS + (ch + 1) * 128), H.v(k * 256, (k + 1) * 256),
                       start=(k == 0), stop=(k == 15))
                gch = (ch // 4) * 8 + hg * 4 + ch % 4
                R = raw[ch % 2]
                b1 = a1[ch % 2]
                cw = lambda tap, gch=gch: convw.v(gch * 4 + tap, gch * 4 + tap + 1)
                cp(P, "pool", R.v(0, 3), halo.v(ch * 4, ch * 4 + 3))
                cp(P, "act", R.v(3, 259), bank.v(0, 256))
                yield
                cp(P, "pool", halo.v(ch * 4, ch * 4 + 3), R.v(256, 259))
                act(P, b1.whole(), R.v(0, 256), AF.Identity, scale=cw(0), bias=zb0.whole())
                stt(P, b1.whole(), R.v(1, 257), cw(1), b1.whole(), ALU.mult, ALU.add)
                yield
                stt(P, b1.whole(), R.v(2, 258), cw(2), b1.whole(), ALU.mult, ALU.add)
                stt(P, b1.whole(), R.v(3, 259), cw(3), b1.whole(), ALU.mult, ALU.add)
                act(P, qkvT.v(ch * 256, (ch + 1) * 256), b1.whole(), AF.Silu)
                yield
            for pr in range(4):
                reg = qkvT.v(pr * 512, (pr + 1) * 512)
                act(P, sq.whole(), reg, AF.Square)
                bank = ps[6 + pr % 2]
                for u in range(2):
                    mm(P, bank.v(u * 256, (u + 1) * 256), ones_bf, sq.v(u * 256, (u + 1) * 256))
                yield
                act(P, lnr.whole(), bank.whole(), AF.Ln, bias=epsb.whole())
                act(P, lnr.whole(), lnr.whole(), AF.Exp, scale=-0.5, bias=(lqb.whole() if pr < 2 else zb0.whole()))
                yield
                tt(P, "dve", reg, reg, lnr.whole(), ALU.mult)
                yield

        def stage_b(s):
            qkvT, sgg, gab = qkvTs[s % 2], sggs[s % 2], gabs[s % 2]
            gens = [twork(0, tbs[0], qkvT, gab), twork(1, tbs[1], qkvT, gab)]
            for _ in range(TW_STAGGER):
                next(gens[0])
                yield
            while gens:
                for g_ in list(gens):
                    try:
                        next(g_)
                    except StopIteration:
                        gens.remove(g_)
                yield
            g0 = chain(0, tbs[0], 2 * s, qkvT, sgg)
            g1 = chain(1, tbs[1], 2 * s + 1, qkvT, sgg)
            for _ in range(6):
                next(g0)
                yield
            cg = [g0, g1]
            while cg:
                for g_ in list(cg):
                    try:
                        next(g_)
                    except StopIteration:
                        cg.remove(g_)
                yield

        ns = nt // 2
        p2_loads(0)
        if ns > 1:
            p2_loads(1)
        interleave([stage_a(0)])
        for s in range(ns):
            if s + 2 < ns:
                p2_loads(s + 2)
            gb_ = stage_b(s)
            ga_ = stage_a(s + 1) if s + 1 < ns else None
            rnd = 0
            while gb_ is not None or ga_ is not None:
                if gb_ is not None:
                    try:
                        next(gb_)
                    except StopIteration:
                        gb_ = None
                if ga_ is not None and (rnd >= A_DELAY or gb_ is None):
                    try:
                        next(ga_)
                    except StopIteration:
                        ga_ = None
                rnd += 1
        P.barrier()

    def pass3():
        A.off = base_mark
        AR1 = Arena(None, R1f.shape[1], ap=R1f)
        AR2 = Arena(None, R2f.shape[1], ap=R2f)
        Wo = AR1.alloc("Wo", 16 * D, BF16)
        gbc = AR2.alloc("gbc", D, F32)
        wst = [AR2.alloc("wst%d" % i, D, F32) for i in range(2)]
        xs3 = [AR2.alloc("x3_%d" % i, D, F32) for i in range(2)]
        mts = [AR2.alloc("mts%d" % i, D, BF16) for i in range(2)]
        zb = [AR2.alloc("zb%d" % i, D, F32) for i in range(2)]
        lnw = AR1.alloc("lnw", D, F32)
        lnb = AR1.alloc("lnb", D, F32)
        st3 = A.alloc("st3", 24, F32)
        mv3 = A.alloc("mv3", 2, F32)
        rs3 = A.alloc("rs3", 1, F32)
        nm3 = A.alloc("nm3", 1, F32)
        dma(P, lnw.whole(), lnw_d.partition_broadcast(128))
        dma(P, lnb.whole(), lnb_d.partition_broadcast(128))
        for g in range(4):
            bank = ps[g % 2]
            mm(P, bank.whole(), cf.v(C_ONES, C_ONES + 128, 0, 1), grow.v(g * 512, (g + 1) * 512, 0, 1))
            cp(P, "act", gbc.v(g * 512, (g + 1) * 512), bank.whole())
        for k in range(16):
            for hf in range(2):
                j = (2 * k + hf) % 4
                stg_ = wst[j // 2].v((j % 2) * 1024, (j % 2 + 1) * 1024)
                dma(P, stg_, wout_d[k * 128:(k + 1) * 128, hf * 1024:(hf + 1) * 1024])
                tt(P, "dve", Wo.v(k * D + hf * 1024, k * D + (hf + 1) * 1024), stg_, gbc.v(hf * 1024, (hf + 1) * 1024), ALU.mult)
        def p3_loads(t):
            dma(P, xs3[t % 2].whole(), x_d[t * 128:(t + 1) * 128, :])
            dma(P, mts[t % 2].whole(), mT_d[t])

        p3_loads(0)
        for t in range(nt):
            sl = t % 2
            if t + 1 < nt:
                p3_loads(t + 1)
            z = zb[sl]
            for n in range(4):
                bank = ps[(t % 2) * 4 + n]
                for k in range(16):
                    mm(P, bank.whole(), mts[sl].v(k * 128, (k + 1) * 128), Wo.v(k * D + n * 512, k * D + (n + 1) * 512),
                       start=(k == 0), stop=(k == 15))
                stt(P, z.v(n * 512, (n + 1) * 512), xs3[sl].v(n * 512, (n + 1) * 512), float(ALPHA), bank.whole(), ALU.mult, ALU.add)
                P.op("dve", lambda e, n=n, z=z: e.bn_stats(out=st3.t[:, n * 6:(n + 1) * 6], in_=z.t[:, n * 512:(n + 1) * 512]),
                     outs=[st3.v(n * 6, (n + 1) * 6)], ins=[z.v(n * 512, (n + 1) * 512)])
            P.op("dve", lambda e: e.bn_aggr(out=mv3.t[:, :], in_=st3.t[:, :]), outs=[mv3.whole()], ins=[st3.whole()])
            ts(P, "pool", rs3.whole(), mv3.v(1, 2), LN_EPS, ALU.add)
            tt(P, "pool", rs3.whole(), rs3.whole(), mhalf.v(0, 1), ALU.pow)
            tt(P, "pool", nm3.whole(), mv3.v(0, 1), rs3.whole(), ALU.mult)
            ts(P, "pool", nm3.whole(), nm3.whole(), -1.0, ALU.mult)
            act(P, z.whole(), z.whole(), AF.Identity, scale=rs3.whole(), bias=nm3.whole())
            tt(P, "dve", z.whole(), z.whole(), lnw.whole(), ALU.mult)
            tt(P, "dve", z.whole(), z.whole(), lnb.whole(), ALU.add)
            dma(P, out_d[t * 128:(t + 1) * 128, :], z.whole())

    AR2 = Arena(None, R2f.shape[1], ap=R2f)
    xs = [AR2.alloc("xs%d" % i, D, F32) for i in range(2)]
    hT1 = [AR2.alloc("hT1_%d" % i, D, BF16) for i in range(2)]
    rp = [AR2.alloc("rp%d" % i, 256, F32) for i in range(2)]
    qs = AR2.alloc("qs", 512, F32)
    ks = AR2.alloc("ks", 512, F32)
    tA = AR2.alloc("tA", 512, F32)
    tB = AR2.alloc("tB", 512, F32)
    qhat = AR2.alloc("qhat", 1024, BF16)
    qkT = AR2.alloc("qkT", 1024, BF16)
    vbf = AR2.alloc("vbf", 1024, BF16)
    sg = AR2.alloc("sg", 1024, F32)
    sT = AR2.alloc("sT", 512, BF16)
    S32 = AR2.alloc("S32", 1024, F32)
    Sbf = AR2.alloc("Sbf", 1024, BF16)
    yn = AR2.alloc("yn", 1024, F32)
    retb = AR2.alloc("retb", 1024, BF16)
    mst = AR2.alloc("mst", 1024, BF16)
    gnw = A.alloc("gnw", RW, F32)
    gnb = A.alloc("gnb", RW, F32)
    st1 = A.alloc("st1", 24, F32)
    mv1 = A.alloc("mv1", 8, F32)
    rs1 = A.alloc("rs1", 4, F32)
    nm1 = A.alloc("nm1", 4, F32)
    KS = int(os.environ.get("KSUB", "0"))
    if not KS & 1:
        dma(P, gnw.whole(), gnw_d.partition_broadcast(128))
        dma(P, gnb.whole(), gnb_d.partition_broadcast(128))
    if not KS & 2:
        P.op("pool", lambda e: e.memset(S32.t[:, :], 0.0), outs=[S32.whole()])
        P.op("pool", lambda e: e.memset(Sbf.t[:, :], 0.0), outs=[Sbf.whole()])

    def make_hT(xbuf, hT):
        for g4 in range(4):
            bank = ps[g4 % 2]
            for kk in range(4):
                k = g4 * 4 + kk
                tr(P, bank.v(kk * 128, (kk + 1) * 128), xbuf.v(k * 128, (k + 1) * 128), idf)
            for kk in range(4):
                k = g4 * 4 + kk
                dst = hT.v(k * 128, (k + 1) * 128)
                src = bank.v(kk * 128, (kk + 1) * 128)
                KHT = int(os.environ.get("KHT", "0"))
                if (g4 % 2 == 0 and KHT == 0) or KHT == 1:
                    act(P, dst, src, AF.Identity, scale=sc1.v(k, k + 1), bias=modc.v(k, k + 1))
                else:
                    ts(P, "dve", dst, src, sc1.v(k, k + 1), ALU.mult, modc.v(k, k + 1), ALU.add)

    def rotary(src, dst, rpb):
        a = src.whole().ap
        rot = src.whole().with_ap(bass.AP(tensor=a.tensor, offset=a.offset + 64,
                                          ap=[list(a.ap[0]), [128, 4], [-64, 2], [1, 64]]))
        c2 = bcast_mid(rpb.v(0, 128), 4, 128)
        s = rpb.v(128, 256).ap
        s2 = rpb.v(128, 256).with_ap(bass.AP(tensor=s.tensor, offset=s.offset, ap=[list(s.ap[0]), [0, 4], [64, 2], [1, 64]]))
        tt(P, "dve", r3(tA.whole(), 4), r3(src.whole(), 4), c2, ALU.mult)
        tt(P, "pool", tB.whole().r("p (h t d) -> p h t d", h=4, t=2), rot, s2, ALU.mult)
        tt(P, "dve", dst, tA.whole(), tB.whole(), ALU.add)

    def p1_loads(t):
        dma(P, xs[t % 2].whole(), x_d[t * 128:(t + 1) * 128, :])
        dma(P, rp[t % 2].whole(), rope_d[t * 128:(t + 1) * 128, :])

    qhat2 = Buf("qhat2", wstg[0].t[:, 0:512].bitcast(BF16))
    vbf2 = Buf("vbf2", wstg[0].t[:, 512:1024].bitcast(BF16))
    sg2 = Buf("sg2", wstg[1].t[:, 0:1024])
    qhats, vbfs, sgs = [qhat, qhat2], [vbf, vbf2], [sg, sg2]

    def p1_front(t):
        sl = t % 2
        H = hT1[sl]
        qh, vv, sgb = qhats[sl], vbfs[sl], sgs[sl]
        for g4 in range(4):
            bank = ps[g4 % 2]
            for kk in range(4):
                k = g4 * 4 + kk
                tr(P, bank.v(kk * 128, (kk + 1) * 128), xs[sl].v(k * 128, (k + 1) * 128), idf)
            for kk in range(4):
                k = g4 * 4 + kk
                dst = H.v(k * 128, (k + 1) * 128)
                src = bank.v(kk * 128, (kk + 1) * 128)
                if g4 % 2 == 0:
                    act(P, dst, src, AF.Identity, scale=sc1.v(k, k + 1), bias=modc.v(k, k + 1))
                else:
                    ts(P, "dve", dst, src, sc1.v(k, k + 1), ALU.mult, modc.v(k, k + 1), ALU.add)
            yield
        dma(P, hT_d[t], H.whole())
        for n in range(6):
            bank = ps[2 + n % 2]
            for k in range(16):
                mm(P, bank.whole(), H.v(k * 128, (k + 1) * 128), R1.v(k * 3072 + n * 512, k * 3072 + (n + 1) * 512),
                   start=(k == 0), stop=(k == 15))
            if n == 0:
                tt(P, "dve", qs.whole(), bank.whole(), cf.v(C_GQ, C_GQ + 512), ALU.mult)
                yield
                rotary(qs, qh.v(0, 512), rp[sl])
            elif n == 1:
                tt(P, "dve", ks.whole(), bank.whole(), cf.v(C_GK, C_GK + 512), ALU.mult)
                yield
                rotary(ks, qh.v(512, 1024), rp[sl])
            elif n < 4:
                cp(P, "act", vv.v((n - 2) * 512, (n - 1) * 512), bank.whole())
            else:
                act(P, sgb.v((n - 4) * 512, (n - 3) * 512), bank.whole(), AF.Silu)
            yield

    def p1_back(t):
        sl = t % 2
        qh, vv, sgb = qhats[sl], vbfs[sl], sgs[sl]
        for m in range(8):
            tr(P, vb(ps[4], m * 128, (m + 1) * 128), qh.v(m * 128, (m + 1) * 128), idb)
        yield
        cp(P, "dve", qkT.whole(), vb(ps[4], 0, 1024))
        yield
        for h in range(4):
            mm(P, ps[5].v(h * 128, (h + 1) * 128), qkT.v(512 + h * 128, 512 + (h + 1) * 128), qkT.v(h * 128, (h + 1) * 128))
        yield
        tt(P, "dve", sT.whole(), ps[5].whole(), cf.v(C_RMASK, C_RMASK + 512), ALU.mult)
        yield
        obs = []
        for h in range(4):
            ob = ps[6 + h // 2].v((h % 2) * 256, (h % 2 + 1) * 256)
            obs.append(ob)
            mm(P, ob, sT.v(h * 128, (h + 1) * 128), vv.v(h * 256, (h + 1) * 256), start=True, stop=False)
            mm(P, ob, qkT.v(h * 128, (h + 1) * 128), Sbf.v(h * 256, (h + 1) * 256), start=False, stop=True)
        ubs = []
        for h in range(4):
            ub = ps[5 - h // 2].v((h % 2) * 256, (h % 2 + 1) * 256)
            ubs.append(ub)
            mm(P, ub, qh.v(512 + h * 128, 512 + (h + 1) * 128), vv.v(h * 256, (h + 1) * 256))
        yield
        for h in range(4):
            P.op("dve", lambda e, h=h: e.bn_stats(out=st1.t[:, h * 6:(h + 1) * 6], in_=obs[h].ap),
                 outs=[st1.v(h * 6, (h + 1) * 6)], ins=[obs[h]])
        for h in range(4):
            stt(P, S32.v(h * 256, (h + 1) * 256), S32.v(h * 256, (h + 1) * 256), float(GAMMAS[h] ** 128), ubs[h], ALU.mult, ALU.add)
        yield
        for h in range(4):
            P.op("dve", lambda e, h=h: e.bn_aggr(out=mv1.t[:, h * 2:(h + 1) * 2], in_=st1.t[:, h * 6:(h + 1) * 6]),
                 outs=[mv1.v(h * 2, (h + 1) * 2)], ins=[st1.v(h * 6, (h + 1) * 6)])
        cp(P, "act", Sbf.whole(), S32.whole())
        yield
        mvv = mv1.whole().r("p (h two) -> p h two", two=2)
        ts(P, "pool", rs1.whole(), mvv[:, :, 1], GN_EPS, ALU.add)
        yield
        tt(P, "pool", rs1.whole(), rs1.whole(), mhalf.v(0, 4), ALU.pow)
        yield
        tt(P, "pool", nm1.whole(), mvv[:, :, 0], rs1.whole(), ALU.mult)
        yield
        ts(P, "pool", nm1.whole(), nm1.whole(), -1.0, ALU.mult)
        yield
        for h in range(4):
            act(P, yn.v(h * 256, (h + 1) * 256), obs[h], AF.Identity, scale=rs1.v(h, h + 1), bias=nm1.v(h, h + 1))
        yield
        tt(P, "pool", yn.whole(), yn.whole(), gnw.whole(), ALU.mult)
        yield
        tt(P, "dve", yn.whole(), yn.whole(), gnb.whole(), ALU.add)
        yield
        tt(P, "pool", retb.whole(), yn.whole(), sgb.whole(), ALU.mult)
        yield
        for m in range(8):
            tr(P, vb(ps[4], m * 128, (m + 1) * 128), retb.v(m * 128, (m + 1) * 128), idb)
        yield
        cp(P, "act", mst.whole(), vb(ps[4], 0, 1024))
        dma(P, mT_d[t, :, 0:1024], mst.whole())
        yield

    def interleave(gens):
        gens = list(gens)
        while gens:
            for g in list(gens):
                try:
                    next(g)
                except StopIteration:
                    gens.remove(g)

    if stop_after >= 1:
        p1_loads(0)
        if nt > 1:
            p1_loads(1)
        interleave([p1_front(0)])
        for t in range(nt):
            if t + 2 < nt:
                p1_loads(t + 2)
            gl = [p1_back(t)]
            if t + 1 < nt:
                gl.append(p1_front(t + 1))
            interleave(gl)
    P.barrier()
    A.off = base_mark
    if stop_after >= 2:
        for hg in range(2):
            gdn_pass(hg)
    if stop_after >= 3:
        pass3()
    return nc, P, A, ps, locals()


def finish(nc, P):
    P.final_wait()
    P.finalize()
    sems = {te: nc.alloc_semaphore("s_" + te) for te in P.teops}
    P.emit(sems)
    return nc


_CONSTS = None


def prep_core_inputs(inp, b):
    global _CONSTS
    if _CONSTS is None:
        _CONSTS = host_consts()
    cf, cbt, rope = _CONSTS
    f = lambda a: np.ascontiguousarray(np.asarray(a, dtype=np.float32))
    b_ada = np.asarray(inp["b_ada"])[0]
    conv = np.asarray(inp["gdn_conv_w"])[0]
    return {
        "x": f(np.asarray(inp["x"])[b]),
        "ccol": f(np.asarray(inp["c"])[b].reshape(16, 128).T),
        "wada": f(np.asarray(inp["w_ada"])[0]),
        "badac": f(b_ada[:4096].reshape(32, 128).T),
        "badag": f(b_ada[4096:].reshape(1, D)),
        "win": f(np.asarray(inp["w_in"])[0]),
        "convw": f(conv.reshape(4, 24, 128).transpose(2, 1, 0).reshape(128, 96)),
        "alog": f(np.asarray(inp["gdn_a_log"])[0].reshape(1, 8)),
        "dtb": f(np.asarray(inp["gdn_dt_bias"])[0].reshape(1, 8)),
        "gnw": f(np.asarray(inp["ret_gn_w"])[0].reshape(1, RW)),
        "gnb": f(np.asarray(inp["ret_gn_b"])[0].reshape(1, RW)),
        "gdnw": f(np.tile(np.asarray(inp["gdn_norm_w"])[0], 8).reshape(1, GW)),
        "wout": f(np.asarray(inp["w_out"])[0]),
        "lnw": f(np.asarray(inp["ln_w"])[0].reshape(1, D)),
        "lnb": f(np.asarray(inp["ln_b"])[0].reshape(1, D)),
        "cf": cf, "cb": cbt, "rope": rope,
    }


def kernel(**inputs):
    nc, P, A, ps, _ = build(nt=NTILE, stop_after=3, debug=False)
    finish(nc, P)
    in_maps = [prep_core_inputs(inputs, b) for b in range(8)]
    res = run_bass_kernel_spmd(nc, in_maps, core_ids=list(range(8)))
    out = np.stack([np.asarray(r["out"], dtype=np.float32) for r in res.results], axis=0)
    return out.astype(np.asarray(inputs["x"]).dtype)
```
